# Optimizing a Trainium2 kernel written in Bass

```python
import math
import jax, jax.numpy as jnp
from jax import lax
import numpy as np

D_MODEL = 1024
BATCH = 32
SEQ = 2048
DEPTH = 1

MIX_W = D_MODEL
POOL_W = (3 * D_MODEL) // 8
SSM_W = (3 * D_MODEL) // 8
ATT_W = D_MODEL - POOL_W - SSM_W
IN_W = 2 * MIX_W
POOL_WINDOWS = (2, 4, 8, 16)
POOL_GROUPS = len(POOL_WINDOWS)
POOL_GW = POOL_W // POOL_GROUPS
SSM_GROUP = 16
SSM_NG = SSM_W // SSM_GROUP
SSM_N = 64
DT_MIN = 1e-3
DT_MAX = 1e-1
N_MEM = 256
MEM_HEADS = 4
MEM_HD = ATT_W // MEM_HEADS
EPS = 1e-6

kernel_name = "hybrid_pool_s5_memattn_layer"


def rmsnorm(x, g):
    xf = x.astype(jnp.float32)
    xf = xf * lax.rsqrt(jnp.mean(xf * xf, axis=-1, keepdims=True) + EPS)
    return (xf * g.astype(jnp.float32)).astype(x.dtype)


def pool_mixer(u, w_pool, pool_scale):
    b, l, _ = u.shape
    uf = u.astype(jnp.float32)
    cs0 = jnp.concatenate([jnp.zeros((b, 1, POOL_W), jnp.float32), jnp.cumsum(uf, axis=1)], axis=1)
    pos = jnp.arange(1, l + 1, dtype=jnp.float32)[None, :, None]
    outs = []
    for gi, w in enumerate(POOL_WINDOWS):
        sl = slice(gi * POOL_GW, (gi + 1) * POOL_GW)
        c = cs0[..., sl]
        lower = jnp.concatenate([jnp.zeros((b, w - 1, POOL_GW), jnp.float32), c[:, :l - w + 1]], axis=1)
        mean = (c[:, 1:] - lower) / jnp.minimum(pos, float(w))
        outs.append(jnp.einsum('blc,cd->bld', mean - uf[..., sl], w_pool[gi].astype(jnp.float32)))
    y = jnp.concatenate(outs, axis=-1) * pool_scale.astype(jnp.float32)
    return y.astype(u.dtype)


def _ssm_combine(e1, e2):
    a1, b1 = e1
    a2, b2 = e2
    return a1 * a2, a2 * b1 + b2


def s5_mixer(u, a_re, a_im, log_dt, b_re, b_im, c_re, c_im, d_skip, w_glu):
    bsz, l, _ = u.shape
    f32 = jnp.float32
    uf = u.astype(f32).reshape(bsz, l, SSM_NG, SSM_GROUP)
    lam = lax.complex(a_re.astype(f32), a_im.astype(f32))
    dt = jnp.exp(log_dt.astype(f32))[:, None]
    lam_bar = jnp.exp(lam * dt)
    b_mat = lax.complex(b_re.astype(f32), b_im.astype(f32))
    c_mat = lax.complex(c_re.astype(f32), c_im.astype(f32))
    b_bar = ((lam_bar - 1.0) / lam)[..., None] * b_mat
    bu = jnp.einsum('blgc,gnc->blgn', uf.astype(jnp.complex64), b_bar)
    lam_all = jnp.broadcast_to(lam_bar, bu.shape)
    _, hs = lax.associative_scan(_ssm_combine, (lam_all, bu), axis=1)
    y = jnp.einsum('blgn,gcn->blgc', hs, c_mat).real + d_skip.astype(f32).reshape(SSM_NG, SSM_GROUP) * uf
    y = jax.nn.gelu(y.reshape(bsz, l, SSM_W))
    z = y @ w_glu.astype(f32)
    out = z[..., :SSM_W] * jax.nn.sigmoid(z[..., SSM_W:])
    return out.astype(u.dtype)


def memory_attention(q, mem, g_mem, w_kv):
    bsz, l, _ = q.shape
    m = rmsnorm(mem, g_mem)
    kv = m @ w_kv
    k = kv[..., :ATT_W].reshape(bsz, -1, MEM_HEADS, MEM_HD).astype(jnp.float32)
    v = kv[..., ATT_W:].reshape(bsz, -1, MEM_HEADS, MEM_HD).astype(jnp.float32)
    qh = q.reshape(bsz, l, MEM_HEADS, MEM_HD).astype(jnp.float32)
    s = jnp.einsum('blhd,bmhd->bhlm', qh, k) * (MEM_HD ** -0.5)
    p = jax.nn.softmax(s, axis=-1)
    o = jnp.einsum('bhlm,bmhd->blhd', p, v).reshape(bsz, l, ATT_W)
    return o.astype(q.dtype)


def setup_inputs(seed: int = 0) -> dict:
    key = jax.random.key(seed)
    ks = jax.random.split(key, 20)
    f32 = jnp.float32
    nrm = lambda k, shape, s: jax.random.normal(k, shape, f32) * s
    n_idx = jnp.arange(SSM_N, dtype=f32)
    a_re = -0.5 + nrm(ks[5], (DEPTH, SSM_NG, SSM_N), 1e-2)
    a_im = math.pi * n_idx[None, None, :] + nrm(ks[6], (DEPTH, SSM_NG, SSM_N), 1e-2)
    log_dt = jax.random.uniform(ks[7], (DEPTH, SSM_NG), f32, math.log(DT_MIN), math.log(DT_MAX))
    return {
        "x": nrm(ks[0], (BATCH, SEQ, D_MODEL), 1.0),
        "mem": nrm(ks[1], (BATCH, N_MEM, D_MODEL), 1.0),
        "g_pre": 1.0 + nrm(ks[2], (DEPTH, D_MODEL), 0.02),
        "w_in": nrm(ks[3], (DEPTH, D_MODEL, IN_W), D_MODEL ** -0.5),
        "w_pool": nrm(ks[4], (DEPTH, POOL_GROUPS, POOL_GW, POOL_GW), POOL_GW ** -0.5),
        "pool_scale": 1.0 + nrm(ks[8], (DEPTH, POOL_W), 0.02),
        "a_re": a_re,
        "a_im": a_im,
        "log_dt": log_dt,
        "b_re": nrm(ks[9], (DEPTH, SSM_NG, SSM_N, SSM_GROUP), (2 * SSM_GROUP) ** -0.5),
        "b_im": nrm(ks[10], (DEPTH, SSM_NG, SSM_N, SSM_GROUP), (2 * SSM_GROUP) ** -0.5),
        "c_re": nrm(ks[11], (DEPTH, SSM_NG, SSM_GROUP, SSM_N), (2 * SSM_N) ** -0.5),
        "c_im": nrm(ks[12], (DEPTH, SSM_NG, SSM_GROUP, SSM_N), (2 * SSM_N) ** -0.5),
        "d_skip": nrm(ks[13], (DEPTH, SSM_W), 1.0),
        "w_glu": nrm(ks[14], (DEPTH, SSM_W, 2 * SSM_W), SSM_W ** -0.5),
        "g_mem": 1.0 + nrm(ks[15], (DEPTH, D_MODEL), 0.02),
        "w_kv": nrm(ks[16], (DEPTH, D_MODEL, 2 * ATT_W), D_MODEL ** -0.5),
        "w_out": nrm(ks[17], (DEPTH, MIX_W, D_MODEL), MIX_W ** -0.5),
        "g_post": 1.0 + nrm(ks[18], (DEPTH, D_MODEL), 0.02),
    }


def reference(x, mem, g_pre, w_in, w_pool, pool_scale, a_re, a_im, log_dt, b_re, b_im,
              c_re, c_im, d_skip, w_glu, g_mem, w_kv, w_out, g_post):
    for i in range(DEPTH):
        h = rmsnorm(x, g_pre[i])
        proj = h @ w_in[i]
        val, gate = proj[..., :MIX_W], proj[..., MIX_W:]
        u_pool = val[..., :POOL_W]
        u_ssm = val[..., POOL_W:POOL_W + SSM_W]
        q = val[..., POOL_W + SSM_W:]
        y_pool = pool_mixer(u_pool, w_pool[i], pool_scale[i])
        y_ssm = s5_mixer(u_ssm, a_re[i], a_im[i], log_dt[i], b_re[i], b_im[i],
                         c_re[i], c_im[i], d_skip[i], w_glu[i])
        y_att = memory_attention(q, mem, g_mem[i], w_kv[i])
        y = jnp.concatenate([y_pool, y_ssm, y_att], axis=-1) * jax.nn.silu(gate)
        out = y @ w_out[i]
        x = x + rmsnorm(out, g_post[i])
    return x
```

```python
import contextlib
import math

import numpy as np

import concourse.bass as bass
import concourse.mybir as mybir
from concourse.bass_utils import run_bass_kernel_spmd

F32 = mybir.dt.float32
BF16 = mybir.dt.bfloat16
AF = mybir.ActivationFunctionType
ALU = mybir.AluOpType

N_CORES = 8
D = 1024
POOL_W = 384
SSM_W = 384
ATT_W = 256
POOL_GW = 96
WINDOWS = (2, 4, 8, 16)
N_MEM = 256
HD = 64
EPS = 1e-6
Q = 8
T = 512
NCH = T // Q


class Buf:
    def __init__(self, name, base):
        self.name = name
        self.base = base
        self.tensor = base.tensor
        self.off0 = base.offset
        self.pstep = base.ap[0][0]
        self.n = base.ap[-1][1]
        self.last_w = None
        self.readers = []

    def v(self, dims=None, off=0, p0=0, np_=128):
        if dims is None:
            dims = [[1, self.n]]
        return bass.AP(self.tensor, self.off0 + p0 * self.pstep + off,
                       [[self.pstep, np_]] + [list(d) for d in dims])


class Sched:
    SEM_LIMIT = 30000

    def __init__(self, nc, es):
        self.nc = nc
        self.es = es
        self.ops = []
        self.sem_cache = {}

    def buf(self, name, n, dtype, psum=False):
        if psum:
            t = self.es.enter_context(self.nc.psum_tensor(name, [128, n], dtype))
        else:
            t = self.es.enter_context(self.nc.sbuf_tensor(name, [128, n], dtype))
        return Buf(name, t[:])

    def arena(self, n_f32):
        self.arena_t = self.es.enter_context(self.nc.sbuf_tensor("arena", [128, n_f32], F32))
        self.arena_n = n_f32
        self.arena_pos = 0
        self.arena_max = 0

    def arena_reset(self):
        self.phase_max = getattr(self, "phase_max", []) + [self.arena_pos]
        self.arena_pos = 0

    def carve(self, name, n, dtype):
        w = n if dtype == F32 else (n + 1) // 2
        w = (w + 1) // 2 * 2
        c0 = self.arena_pos
        assert c0 + w <= self.arena_n, f"arena overflow at {name}: {c0 + w} > {self.arena_n}"
        self.arena_pos += w
        self.arena_max = max(self.arena_max, self.arena_pos)
        base = self.arena_t[:, c0:c0 + w]
        if dtype != F32:
            base = base.bitcast(dtype)
        return Buf(name, base)

    def barrier(self, exclude=()):
        last = {}
        for i, o in enumerate(self.ops):
            key = ("dma", o["dma"]) if o["dma"] is not None else ("eng", o["eng"])
            if o["dma"] is not None and o["dma"] in exclude:
                continue
            last[key] = i
        deps = set(last.values())
        for eng in ("tensor", "vector", "scalar", "gpsimd", "sync"):
            idx = len(self.ops)
            self.ops.append(dict(eng=eng, fn=(lambda e: e.nop()), deps=set(deps), dma=None))

    def _deps(self, eng, reads, writes):
        deps = set()
        for b in reads:
            if b.last_w is not None:
                deps.add(b.last_w)
        for b in writes:
            if b.last_w is not None:
                deps.add(b.last_w)
            for r in b.readers:
                deps.add(r)
        return deps

    def op(self, eng, fn, reads=(), writes=(), dma=None):
        cap = getattr(self, "_cap", None)
        if cap is not None:
            cap.append((eng, fn, tuple(reads), tuple(writes), dma))
            return -1
        idx = len(self.ops)
        deps = self._deps(eng, reads, writes)
        self.ops.append(dict(eng=eng, fn=fn, deps=deps, dma=dma))
        for b in reads:
            b.readers.append(idx)
        for b in writes:
            b.last_w = idx
            b.readers = []
        return idx

    def capture(self, f):
        self._cap = []
        try:
            f()
            return self._cap
        finally:
            self._cap = None

    def replay_interleaved(self, *lists):
        its = [iter(l) for l in lists]
        while its:
            for it in list(its):
                try:
                    self.op(*next(it))
                except StopIteration:
                    its.remove(it)

    def mark(self, name):
        self.marks = getattr(self, "marks", {})
        if name not in self.marks:
            self.marks[name] = len(self.ops)

    def _sem(self, key):
        if key not in self.sem_cache:
            name = "s_" + "_".join(str(k) for k in key)
            self.sem_cache[key] = self.es.enter_context(self.nc.semaphore(name))
        return self.sem_cache[key]

    def lower(self):
        ops = self.ops
        needed = set()
        for o in ops:
            needed |= o["deps"]
        prog = {e: [] for e in ("tensor", "vector", "scalar", "gpsimd", "sync")}
        cnt = {}
        waited = {}
        for i, o in enumerate(ops):
            eng = o["eng"]
            if o["dma"] is not None:
                stream = ("dma", o["dma"])
                inc = 16
                sig = True
            else:
                stream = ("eng", eng)
                inc = 1
                sig = i in needed
            wl = {}
            for d in o["deps"]:
                od = ops[d]
                ds = od["stream"]
                if ds == stream and eng == "tensor" and od["dma"] is None:
                    continue
                v = od["sig"]
                if ds not in wl or wl[ds] < v:
                    wl[ds] = v
            for ds, v in wl.items():
                w = waited.get((eng, ds))
                if w is not None and w >= v:
                    continue
                waited[(eng, ds)] = v
                sem = self._sem(ds + (v[0],))
                prog[eng].append(("wait", sem, v[1]))
            if sig:
                ep, c = cnt.get(stream, (0, 0))
                if c + inc > self.SEM_LIMIT:
                    ep, c = ep + 1, 0
                c += inc
                cnt[stream] = (ep, c)
                o["sig"] = (ep, c)
                sem = self._sem(stream + (ep,))
                prog[eng].append(("op", o["fn"], sem, inc))
            else:
                o["sig"] = None
                prog[eng].append(("op", o["fn"], None, 0))
            o["stream"] = stream
        self.prog = prog
        return prog

    def emit(self):
        import os
        stop = os.environ.get("KSTOP")
        if stop and stop in getattr(self, "marks", {}):
            self.ops = self.ops[:self.marks[stop]]
        last = {}
        for i, o in enumerate(self.ops):
            key = ("dma", o["dma"]) if o["dma"] is not None else ("eng", o["eng"])
            last[key] = i
        self.ops.append(dict(eng="sync", fn=(lambda e: e.nop()), deps=set(last.values()), dma=None))
        prog = self.lower()
        nc = self.nc

        def run(e, name):
            for it in prog[name]:
                if it[0] == "wait":
                    e.wait_ge(it[1], it[2])
                else:
                    ins = it[1](e)
                    if it[2] is not None:
                        ins.then_inc(it[2], it[3])

        with nc.Block() as block:
            @block.sync
            def _(e):
                run(e, "sync")

            @block.scalar
            def _(e):
                run(e, "scalar")

            @block.vector
            def _(e):
                run(e, "vector")

            @block.gpsimd
            def _(e):
                run(e, "gpsimd")

            @block.tensor
            def _(e):
                run(e, "tensor")


def build_nc(NB, L, dbg=False):
    nc = bass.Bass("TRN2", target_bir_lowering=False)
    NT = L // T

    def din(name, shape):
        return nc.dram_tensor(name, list(shape), F32, kind="ExternalInput").ap()

    x_d = din("x", [NB, L, D])
    mem_d = din("mem", [NB, N_MEM, D])
    w_in_d = din("w_in", [D, 2 * D])
    w_out_d = din("w_out", [D, D])
    w_glu_d = din("w_glu", [SSM_W, 2 * SSM_W])
    w_kv_d = din("w_kv", [D, 2 * ATT_W])
    w_pool_d = din("w_pool_t", [96, 4 * 96])
    gpre_d = din("gpre_t", [128, 8])
    gmem_d = din("gmem_t", [128, 8])
    gpost_d = din("g_post", [1, D])
    pscale_d = din("pscale_t", [96, 4])
    dskip_d = din("dskip_t", [128, 3])
    ctab_d = din("pool_ctab", [96, 64])
    ident_d = din("ident", [128, 128])
    sm_d = {k: din("sm_" + k, [128, 12]) for k in ("are", "aim", "ldt")}
    for k in ("bre", "bim", "cre", "cim"):
        sm_d[k] = din("sm_" + k, [128, 192])
    cm_d = {k: din("cm_" + k, [128, 384]) for k in ("are", "aim", "ldt", "bre", "bim")}
    out_d = nc.dram_tensor("out", [NB, L, D], F32, kind="ExternalOutput").ap()
    dbg_d = {}
    if dbg:
        for k, n in (("proj", 2048), ("ypool", 384), ("yssm", 384), ("yatt", 256), ("ylin", 384)):
            dbg_d[k] = nc.dram_tensor("dbg_" + k, [n, T], F32, kind="ExternalOutput").ap()

    with contextlib.ExitStack() as es:
        S = Sched(nc, es)
        V, A, P, G_, SY = "vector", "scalar", "tensor", "gpsimd", "sync"

        wi = S.buf("wi", 8 * 2048, BF16)
        wo = S.buf("wo", 9 * 1024, BF16)
        wg = S.buf("wg", 3 * 768, BF16)
        wkv = S.buf("wkv", 8 * 512, BF16)
        wp1 = S.buf("wp1", 4 * 96, BF16)
        wp2 = S.buf("wp2", 4 * 96, BF16)
        W1 = S.buf("W1", 3 * 8 * 2 * 128, BF16)
        W3r = S.buf("W3r", 12 * 9 * 32, BF16)
        W3i = S.buf("W3i", 12 * 9 * 32, BF16)
        Kc = S.buf("Kc", 3 * 8 * 128, BF16)
        cosT = S.buf("cosT", 12 * 64, F32)
        sinT = S.buf("sinT", 12 * 64, F32)
        decT = S.buf("decT", 12 * 64, F32)
        rQ = S.buf("rQ", 12, F32)
        gpost = S.buf("gpost", 1024, F32)
        identf = S.buf("identf", 128, F32)
        identb = S.buf("identb", 128, BF16)
        ones = S.buf("ones", 64, BF16)
        small = S.buf("small", 64, F32)
        ctab = S.buf("ctab", 64, F32)
        _ps = es.enter_context(nc.psum_tensor("ps_all", [128, 4096], F32))
        ps_po = [Buf("ps_po0", _ps[:, 0:512]), Buf("ps_po1", _ps[:, 512:1024])]
        ps_T = Buf("ps_T", _ps[:, 1024:1536].bitcast(BF16))
        ps_w = [Buf(f"ps_w{i}", _ps[:, 1536 + 512 * i:2048 + 512 * i]) for i in range(2)]
        ps_m = [Buf(f"ps_m{i}", _ps[:, 2560 + 512 * i:3072 + 512 * i]) for i in range(3)]
        ps_Tf = Buf("ps_Tf", _ps[:, 1024:1536])
        rr = {"w": 0, "m": 0}

        def bank(kind, force=None):
            lst = ps_w if kind == "w" else ps_m
            if force is not None:
                rr[kind] = force
            b = lst[rr[kind] % len(lst)]
            rr[kind] += 1
            assert b.last_w is None or len(b.readers) > 0, f"PSUM bank {b.name} reused before consumed"
            return b

        S.arena(int(nc.sbuf_bytes_remaining) // 4 - 256)

        dq = [SY, A]

        def load(buf, dram_ap, np_=128, dims=None, q=SY, off=0):
            S.op(q, lambda e: e.dma_start(out=buf.v(dims, off=off, np_=np_), in_=dram_ap),
                 writes=[buf], dma=buf.name)

        load(small, gpre_d, dims=[[1, 8]], off=0)
        load(small, gmem_d, dims=[[1, 8]], off=8)
        load(small, pscale_d, np_=96, dims=[[1, 4]], off=16)
        load(small, dskip_d, dims=[[1, 3]], off=20)
        S.op(V, lambda e: e.memset(small.v([[1, 1]], off=23), -0.5), writes=[small])
        load(ctab, ctab_d, np_=96)
        load(identf, ident_d)
        load(gpost, gpost_d.partition_broadcast(128))
        S.op(V, lambda e: e.tensor_copy(out=identb.v(), in_=identf.v()), reads=[identf], writes=[identb])
        S.op(V, lambda e: e.memset(ones.v(), 1.0), writes=[ones])


        def tt(out, a, b, op, reads, writes, eng=V):
            S.op(eng, lambda e: e.tensor_tensor(out=out, in0=a, in1=b, op=op), reads=reads, writes=writes)

        def lam_tables(pref, src, F, emax, EN):
            are = S.carve(pref + "are", F, F32)
            aim = S.carve(pref + "aim", F, F32)
            ldt = S.carve(pref + "ldt", F, F32)
            load(are, src["are"])
            load(aim, src["aim"])
            load(ldt, src["ldt"])
            def tt(out, a_, b_, op, reads, writes):
                S.op(EN, lambda e: e.tensor_tensor(out=out, in0=a_, in1=b_, op=op), reads=reads, writes=writes)

            dt = S.carve(pref + "dt", F, F32)
            ar = S.carve(pref + "ar", F, F32)
            ai = S.carve(pref + "ai", F, F32)
            c = S.carve(pref + "c", F, F32)
            s_ = S.carve(pref + "s", F, F32)
            t1 = S.carve(pref + "t1", F, F32)
            t2 = S.carve(pref + "t2", F, F32)
            mag = S.carve(pref + "mag", F, F32)
            hp = S.carve(pref + "hp", 2, F32)
            S.op(EN, lambda e: e.memset(hp.v([[1, 1]]), math.pi / 2), writes=[hp])
            S.op(A, lambda e: e.activation(out=dt.v(), in_=ldt.v(), func=AF.Exp), reads=[ldt], writes=[dt])
            tt(ar.v(), are.v(), dt.v(), ALU.mult, [are, dt], [ar])
            tt(ai.v(), aim.v(), dt.v(), ALU.mult, [aim, dt], [ai])
            S.op(A, lambda e: e.activation(out=mag.v(), in_=ar.v(), func=AF.Exp), reads=[ar], writes=[mag])
            S.op(A, lambda e: e.activation(out=s_.v(), in_=ai.v(), func=AF.Sin, scale=1.0 / 8),
                 reads=[ai], writes=[s_])
            S.op(A, lambda e: e.activation(out=t1.v(), in_=ai.v(), func=AF.Sin, scale=1.0 / 16),
                 reads=[ai], writes=[t1])
            tt(t2.v(), t1.v(), t1.v(), ALU.mult, [t1], [t2])
            S.op(EN, lambda e: e.tensor_scalar(out=c.v(), in0=t2.v(), scalar1=-2.0, scalar2=1.0,
                                               op0=ALU.mult, op1=ALU.add), reads=[t2], writes=[c])

            def square(cr, ci_):
                tt(t1.v(), cr.v(), cr.v(), ALU.mult, [cr], [t1])
                tt(t2.v(), ci_.v(), ci_.v(), ALU.mult, [ci_], [t2])
                tt(ci_.v(), cr.v(), ci_.v(), ALU.mult, [cr, ci_], [ci_])
                S.op(EN, lambda e: e.tensor_scalar(out=ci_.v(), in0=ci_.v(), scalar1=2.0, scalar2=None,
                                                   op0=ALU.mult), reads=[ci_], writes=[ci_])
                tt(cr.v(), t1.v(), t2.v(), ALU.subtract, [t1, t2], [cr])

            for _ in range(3):
                square(c, s_)
            Lr = S.carve(pref + "Lr", (emax + 1) * F, F32)
            Li = S.carve(pref + "Li", (emax + 1) * F, F32)
            S.op(EN, lambda e: e.memset(Lr.v([[1, F]]), 1.0), writes=[Lr])
            S.op(EN, lambda e: e.memset(Li.v([[1, F]]), 0.0), writes=[Li])
            tt(Lr.v([[1, F]], off=F), mag.v(), c.v(), ALU.mult, [mag, c], [Lr])
            tt(Li.v([[1, F]], off=F), mag.v(), s_.v(), ALU.mult, [mag, s_], [Li])

            def cmul(outr, outi, ar_, ai_, br_, bi_, rd, wr):
                tt(t1.v(), ar_, br_, ALU.mult, rd, [t1])
                tt(t2.v(), ai_, bi_, ALU.mult, rd, [t2])
                tt(outr, t1.v(), t2.v(), ALU.subtract, [t1, t2], wr)
                tt(t1.v(), ar_, bi_, ALU.mult, rd, [t1])
                tt(t2.v(), ai_, br_, ALU.mult, rd, [t2])
                tt(outi, t1.v(), t2.v(), ALU.add, [t1, t2], wr)

            for e_ in range(2, emax + 1):
                cmul(Lr.v([[1, F]], off=e_ * F), Li.v([[1, F]], off=e_ * F),
                     Lr.v([[1, F]], off=(e_ - 1) * F), Li.v([[1, F]], off=(e_ - 1) * F),
                     Lr.v([[1, F]], off=F), Li.v([[1, F]], off=F), [Lr, Li], [Lr, Li])
            cr = S.carve(pref + "cr", F, F32)
            ci_ = S.carve(pref + "ci", F, F32)
            den = dt
            lm1 = ai
            S.op(EN, lambda e: e.tensor_scalar(out=lm1.v(), in0=Lr.v([[1, F]], off=F), scalar1=-1.0,
                                               scalar2=None, op0=ALU.add), reads=[Lr], writes=[lm1])
            tt(t1.v(), are.v(), are.v(), ALU.mult, [are], [t1])
            tt(t2.v(), aim.v(), aim.v(), ALU.mult, [aim], [t2])
            tt(den.v(), t1.v(), t2.v(), ALU.add, [t1, t2], [den])
            S.op(V, lambda e: e.reciprocal(out=den.v(), in_=den.v()), reads=[den], writes=[den])
            L1i = Li.v([[1, F]], off=F)
            tt(t1.v(), lm1.v(), are.v(), ALU.mult, [lm1, are], [t1])
            tt(t2.v(), L1i, aim.v(), ALU.mult, [Li, aim], [t2])
            tt(cr.v(), t1.v(), t2.v(), ALU.add, [t1, t2], [cr])
            tt(cr.v(), cr.v(), den.v(), ALU.mult, [cr, den], [cr])
            tt(t1.v(), L1i, are.v(), ALU.mult, [Li, are], [t1])
            tt(t2.v(), lm1.v(), aim.v(), ALU.mult, [lm1, aim], [t2])
            tt(ci_.v(), t1.v(), t2.v(), ALU.subtract, [t1, t2], [ci_])
            tt(ci_.v(), ci_.v(), den.v(), ALU.mult, [ci_, den], [ci_])
            return dict(Lr=Lr, Li=Li, ar=ar, cr=cr, ci=ci_, ur=c, ui=s_, t1=t1, t2=t2, cmul=cmul,
                        square=square)

        def part_c():
            F = 384
            cm = lam_tables("cm_", cm_d, F, 1, V)
            bre = S.carve("cm_bre", F, F32)
            bim = S.carve("cm_bim", F, F32)
            load(bre, cm_d["bre"])
            load(bim, cm_d["bim"])
            cur = [S.carve(f"cm_cur{i}", 2 * F, F32) for i in range(2)]
            cm["cmul"](cur[0].v([[1, F]]), cur[0].v([[1, F]], off=F), cm["cr"].v(), cm["ci"].v(), bre.v(), bim.v(),
                       [cm["cr"], cm["ci"], bre, bim], [cur[0]])
            for j in range(7, -1, -1):
                k_ = (7 - j) % 2
                cb = cur[k_]
                for part in range(2):
                    S.op(V, lambda e, j=j, part=part, cb=cb: e.tensor_copy(
                        out=W1.v([[8 * 2 * 128, 3], [1, 128]], off=(j * 2 + part) * 128),
                        in_=cb.v([[128, 3], [1, 128]], off=part * 384)), reads=[cb], writes=[W1])
                if j > 0:
                    nb_ = cur[1 - k_]
                    cm["cmul"](nb_.v([[1, F]]), nb_.v([[1, F]], off=F), cb.v([[1, F]]), cb.v([[1, F]], off=F),
                               cm["Lr"].v([[1, F]], off=F), cm["Li"].v([[1, F]], off=F),
                               [cb, cm["Lr"], cm["Li"]], [nb_])


        def part_s():
            sm = lam_tables("sm_", sm_d, 12, 8, V)
            F = 12
            sb = {}
            for k in ("bre", "bim", "cre", "cim"):
                sb[k] = S.carve("sm_" + k, 192, F32)
                load(sb[k], sm_d[k])
            t1w = S.carve("sm_t1w", 192, F32)
            t2w = S.carve("sm_t2w", 192, F32)

            def bc16(buf, off=0):
                return buf.v([[1, 12], [0, 16]], off=off)

            def w16(buf):
                return buf.v([[16, 12], [1, 16]])

            sbbr = S.carve("sm_bbr", 192, F32)
            sbbi = S.carve("sm_bbi", 192, F32)
            tt(w16(t1w), bc16(sm["cr"]), w16(sb["bre"]), ALU.mult, [sm["cr"], sb["bre"]], [t1w])
            tt(w16(t2w), bc16(sm["ci"]), w16(sb["bim"]), ALU.mult, [sm["ci"], sb["bim"]], [t2w])
            tt(sbbr.v(), t1w.v(), t2w.v(), ALU.subtract, [t1w, t2w], [sbbr])
            tt(w16(t1w), bc16(sm["cr"]), w16(sb["bim"]), ALU.mult, [sm["cr"], sb["bim"]], [t1w])
            tt(w16(t2w), bc16(sm["ci"]), w16(sb["bre"]), ALU.mult, [sm["ci"], sb["bre"]], [t2w])
            tt(sbbi.v(), t1w.v(), t2w.v(), ALU.add, [t1w, t2w], [sbbi])
            Bpr = S.carve("Bpr", 12 * 128, F32)
            Bpi = S.carve("Bpi", 12 * 128, F32)
            for bp, srcb in ((Bpr, sbbr), (Bpi, sbbi)):
                S.op(A, lambda e, bp=bp: e.memzero(bp.v()), writes=[bp])
                for pp in range(4):
                    for h in range(2):
                        S.op(A, lambda e, bp=bp, srcb=srcb, pp=pp, h=h: e.activation(
                            out=bp.v([[4 * 128, 3], [1, 16]], off=pp * 128 + 32 * pp + 16 * h, p0=64 * h, np_=64),
                            in_=srcb.v([[4 * 16, 3], [1, 16]], off=pp * 16, p0=64 * h, np_=64), func=AF.Copy),
                            reads=[srcb], writes=[bp])
            CLr = S.carve("CLr", 12 * 9 * 32, F32)
            CLi = S.carve("CLi", 12 * 9 * 32, F32)
            S.op(A, lambda e: e.memzero(CLr.v()), writes=[CLr])
            S.op(A, lambda e: e.memzero(CLi.v()), writes=[CLi])
            ta = S.carve("sm_ta", 192, F32)
            tb = S.carve("sm_tb", 192, F32)
            for e_ in range(9):
                lr = bc16(sm["Lr"], off=e_ * 12)
                li = bc16(sm["Li"], off=e_ * 12)
                tt(w16(t1w), lr, w16(sb["cre"]), ALU.mult, [sm["Lr"], sb["cre"]], [t1w])
                tt(w16(t2w), li, w16(sb["cim"]), ALU.mult, [sm["Li"], sb["cim"]], [t2w])
                tt(w16(ta), li, w16(sb["cre"]), ALU.mult, [sm["Li"], sb["cre"]], [ta])
                tt(w16(tb), lr, w16(sb["cim"]), ALU.mult, [sm["Lr"], sb["cim"]], [tb])
                for h in range(2):
                    dst = dict(dims=[[9 * 32, 12], [1, 16]], off=e_ * 32 + 16 * h, p0=64 * h, np_=64)
                    srcv = dict(dims=[[16, 12], [1, 16]], p0=64 * h, np_=64)
                    S.op(V, lambda e, dst=dst, srcv=srcv: e.tensor_tensor(
                        out=CLr.v(**dst), in0=t1w.v(**srcv), in1=t2w.v(**srcv), op=ALU.subtract),
                        reads=[t1w, t2w], writes=[CLr])
                    S.op(V, lambda e, dst=dst, srcv=srcv: e.scalar_tensor_tensor(
                        out=CLi.v(**dst), in0=ta.v(**srcv), scalar=-1.0, in1=tb.v(**srcv),
                        op0=ALU.mult, op1=ALU.subtract), reads=[ta, tb], writes=[CLi])
            S.op(A, lambda e: e.activation(out=W3r.v(), in_=CLr.v(), func=AF.Copy), reads=[CLr], writes=[W3r])
            S.op(A, lambda e: e.activation(out=W3i.v(), in_=CLi.v(), func=AF.Copy), reads=[CLi], writes=[W3i])
            for p in range(12):
                ct, pp = divmod(p, 4)
                pb = bank("m")
                S.op(P, lambda e, p=p, pb=pb: e.matmul(
                    pb.v([[1, 256]]), lhsT=Bpr.v([[1, 128]], off=p * 128),
                    rhs=CLr.v([[1, 256]], off=p * 9 * 32), start=True, stop=False),
                    reads=[Bpr, CLr], writes=[pb])
                S.op(P, lambda e, p=p, pb=pb: e.matmul(
                    pb.v([[1, 256]]), lhsT=Bpi.v([[1, 128]], off=p * 128),
                    rhs=CLi.v([[1, 256]], off=p * 9 * 32), start=False, stop=True),
                    reads=[Bpi, CLi], writes=[pb])
                S.op(A, lambda e, ct=ct, pp=pp, pb=pb: e.activation(
                    out=Kc.v([[128, 8], [1, 32]], off=ct * 1024 + 32 * pp), in_=pb.v([[32, 8], [1, 32]]),
                    func=AF.Copy),
                    reads=[pb], writes=[Kc])
            for ct in range(3):
                S.op(V, lambda e, ct=ct: e.scalar_tensor_tensor(
                    out=Kc.v([[1, 128]], off=ct * 1024), in0=identf.v(), scalar=small.v([[1, 1]], off=20 + ct),
                    in1=Kc.v([[1, 128]], off=ct * 1024), op0=ALU.mult, op1=ALU.add),
                    reads=[identf, small, Kc], writes=[Kc])
            S.op(A, lambda e: e.activation(out=rQ.v(), in_=sm["ar"].v(), func=AF.Exp, scale=float(Q)),
                 reads=[sm["ar"]], writes=[rQ])
            for _ in range(3):
                sm["square"](sm["ur"], sm["ui"])
            pwr = S.carve("pwr", 12, F32)
            pwi = S.carve("pwi", 12, F32)
            S.op(V, lambda e: e.tensor_copy(out=pwr.v(), in_=sm["ur"].v()), reads=[sm["ur"]], writes=[pwr])
            S.op(V, lambda e: e.tensor_copy(out=pwi.v(), in_=sm["ui"].v()), reads=[sm["ui"]], writes=[pwi])
            S.op(V, lambda e: e.tensor_copy(out=cosT.v([[64, 12], [1, 1]]), in_=pwr.v([[1, 12], [1, 1]])),
                 reads=[pwr], writes=[cosT])
            S.op(V, lambda e: e.tensor_copy(out=sinT.v([[64, 12], [1, 1]]), in_=pwi.v([[1, 12], [1, 1]])),
                 reads=[pwi], writes=[sinT])
            tmr = S.carve("tmr", 12 * 32, F32)
            tmi = S.carve("tmi", 12 * 32, F32)
            n = 1
            while n < 64:
                def tv(buf, off, cnt, n=n):
                    return buf.v([[64, 12], [1, cnt]], off=off)

                def pv(buf, cnt):
                    return buf.v([[1, 12], [0, cnt]])

                def tm(buf, cnt):
                    return buf.v([[32, 12], [1, cnt]])
                tt(tm(tmr, n), tv(cosT, 0, n), pv(pwr, n), ALU.mult, [cosT, pwr], [tmr])
                tt(tm(tmi, n), tv(sinT, 0, n), pv(pwi, n), ALU.mult, [sinT, pwi], [tmi])
                tt(tv(cosT, n, n), tm(tmr, n), tm(tmi, n), ALU.subtract, [tmr, tmi], [cosT])
                tt(tm(tmr, n), tv(cosT, 0, n), pv(pwi, n), ALU.mult, [cosT, pwi], [tmr])
                tt(tm(tmi, n), tv(sinT, 0, n), pv(pwr, n), ALU.mult, [sinT, pwr], [tmi])
                tt(tv(sinT, n, n), tm(tmr, n), tm(tmi, n), ALU.add, [tmr, tmi], [sinT])
                sm["square"](pwr, pwi)
                n *= 2
            S.op(V, lambda e: e.tensor_copy(out=decT.v([[64, 12], [1, 64]]),
                                            in_=rQ.v([[1, 12], [0, 64]])), reads=[rQ], writes=[decT])
            S.op(V, lambda e: e.memset(decT.v([[64, 12], [1, 1]]), 0.0), writes=[decT])


        ops_c = S.capture(part_c)
        ops_s = S.capture(part_s)

        def split_head(ops_):
            n_act = 0
            for k_, o_ in enumerate(ops_):
                if o_[0] == A:
                    n_act += 1
                    if n_act == 4:
                        return ops_[:k_ + 1], ops_[k_ + 1:]
            return ops_, []

        hc_, tc_ = split_head(ops_c)
        hs_, ts_ = split_head(ops_s)
        hc_ = hc_ + [o_ for o_ in tc_ if o_[4] is not None and o_[0] == SY]
        tc_ = [o_ for o_ in tc_ if not (o_[4] is not None and o_[0] == SY)]
        hs_ = hs_ + [o_ for o_ in ts_ if o_[4] is not None and o_[0] == SY]
        ts_ = [o_ for o_ in ts_ if not (o_[4] is not None and o_[0] == SY)]
        hs_ = hs_ + [o_ for o_ in ts_ if o_[0] == A and o_[4] is None and len(o_[2]) == 0]
        ts_ = [o_ for o_ in ts_ if not (o_[0] == A and o_[4] is None and len(o_[2]) == 0)]
        S.replay_interleaved(hc_, hs_)
        wps = S.carve("wps", 384, F32)
        load(wps, w_pool_d, np_=96)
        for gi, w in enumerate(WINDOWS):
            S.op(V, lambda e, gi=gi, w=w: e.tensor_scalar(
                out=wp1.v([[1, 96]], off=gi * 96, np_=96), in0=wps.v([[1, 96]], off=gi * 96, np_=96),
                scalar1=1.0 / w, scalar2=None, op0=ALU.mult), reads=[wps], writes=[wp1])
            S.op(V, lambda e, gi=gi: e.tensor_scalar(
                out=wp2.v([[1, 96]], off=gi * 96, np_=96), in0=wps.v([[1, 96]], off=gi * 96, np_=96),
                scalar1=-1.0, scalar2=None, op0=ALU.mult), reads=[wps], writes=[wp2])

        stg = [S.carve(f"stg{i}", 1024, F32) for i in range(2)]
        for dc in range(8):
            for hf in range(2):
                k_ = (dc * 2 + hf) % 2
                st = stg[k_]
                load(st, w_in_d[dc * 128:(dc + 1) * 128, hf * 1024:(hf + 1) * 1024], q=dq[k_])
                S.op(A, lambda e, st=st, dc=dc, hf=hf: e.activation(
                    out=wi.v([[1, 1024]], off=dc * 2048 + hf * 1024), in_=st.v(), func=AF.Copy,
                    scale=small.v([[1, 1]], off=dc)), reads=[st, small], writes=[wi])
        rows = [96] * 4 + [128] * 5
        r0 = 0
        for ci, nr in enumerate(rows):
            S.op(G_, lambda e, ci=ci, nr=nr, r0=r0: e.dma_start(
                out=wo.v([[1, 1024]], off=ci * 1024, np_=nr), in_=w_out_d[r0:r0 + nr, :]),
                writes=[wo], dma="wo")
            r0 += nr
        for ct in range(3):
            S.op(G_, lambda e, ct=ct: e.dma_start(
                out=wg.v([[1, 768]], off=ct * 768), in_=w_glu_d[ct * 128:(ct + 1) * 128, :]),
                writes=[wg], dma="wg")
        for dc in range(8):
            S.op(G_, lambda e, dc=dc: e.dma_start(
                out=wkv.v([[1, 512]], off=dc * 512), in_=w_kv_d[dc * 128:(dc + 1) * 128, :]),
                writes=[wkv], dma="wkv")
        S.replay_interleaved(tc_, ts_)
        S.mark('t2')
        S.barrier(exclude=('wo', 'wg', 'wkv'))
        S.arena_reset()
        env = dict(locals())
        build_main(nc, S, env)
        S.emit()
    return nc


def build_main(nc, S, g):
    V, A, P, G_, SY = "vector", "scalar", "tensor", "gpsimd", "sync"
    NB, L, NT = g["NB"], g["L"], g["NT"]
    x_d, mem_d, out_d, w_kv_d, dbg_d = g["x_d"], g["mem_d"], g["out_d"], g["w_kv_d"], g["dbg_d"]
    wi, wo, wg, wp1, wp2 = g["wi"], g["wo"], g["wg"], g["wp1"], g["wp2"]
    W1, W3r, W3i, Kc = g["W1"], g["W3r"], g["W3i"], g["Kc"]
    cosT, sinT, decT, rQ = g["cosT"], g["sinT"], g["decT"], g["rQ"]
    gpost, identf, identb, ones, small, ctab = (g["gpost"], g["identf"], g["identb"], g["ones"],
                                                g["small"], g["ctab"])
    ps_po, ps_T, bank = g["ps_po"], g["ps_T"], g["bank"]

    def tt(out, a, b, op, reads, writes, eng=V):
        S.op(eng, lambda e: e.tensor_tensor(out=out, in0=a, in1=b, op=op), reads=reads, writes=writes)

    class BankRef:
        def __init__(self, main, bufs=None):
            self.main = main
            self.bufs = bufs if bufs is not None else [main]

        def v(self, *a, **k):
            return self.main.v(*a, **k)

        def consumed(self):
            return all(b_.last_w is None or len(b_.readers) > 0 for b_ in self.bufs)

    class Pool_:
        def __init__(self, refs):
            self.refs = list(refs)
            self.i = 0

        def take(self):
            r = self.refs[self.i % len(self.refs)]
            self.i += 1
            assert r.consumed(), f"PSUM bank {r.main.name} reused before consumed"
            return r

        def add(self, ref):
            self.refs.insert(self.i % len(self.refs) if self.refs else 0, ref)

    ps_w, ps_m, ps_Tf = g["ps_w"], g["ps_m"], g["ps_Tf"]
    R_w = [BankRef(ps_w[0]), BankRef(ps_w[1])]
    R_m = [BankRef(ps_m[0]), BankRef(ps_m[1]), BankRef(ps_m[2])]
    R_T = BankRef(ps_Tf, [ps_Tf, g["ps_T"]])
    R_po = [BankRef(g["ps_po"][0]), BankRef(g["ps_po"][1])]

    def mm(out, lhsT, rhs, start, stop, reads, writes, tp=None):
        if tp is None:
            S.op(P, lambda e: e.matmul(out, lhsT=lhsT, rhs=rhs, start=start, stop=stop),
                 reads=reads, writes=writes)
        else:
            S.op(P, lambda e: e.matmul(out, lhsT=lhsT, rhs=rhs, start=start, stop=stop, tile_position=tp),
                 reads=reads, writes=writes)

    xb = [S.carve(f"xb{i}", 1024, F32) for i in range(2)]
    xr = xb
    hb2 = [S.carve(f"hb{i}", 1024, BF16) for i in range(2)]
    hT = S.carve("hT", 8 * 512, BF16)
    _Up = S.carve("Up", 4 * 528, BF16)
    Up2 = [_Up, _Up]
    SA = S.carve("SA", 528, F32)
    SB = S.carve("SB", 528, F32)
    Sb = S.carve("Sb", 4 * 512, BF16)
    Uz2 = [S.carve(f"Uz{k}", 3 * 1024, BF16) for k in range(2)]
    qT2 = [S.carve(f"qT{k}", 2 * 512, BF16) for k in range(2)]
    Gt = S.carve("Gt", 9 * 512, BF16)
    PT = S.carve("PT", 4 * 512, BF16)
    bt = S.carve("bt", 2 * 768, F32)
    tmpS = S.carve("tmpS", 2 * 768, F32)
    Hs = S.carve("Hs", 2 * 12 * 65, BF16)
    Hc = S.carve("Hc", 24, F32)
    hct = S.carve("hct", 48, F32)
    Yc = S.carve("Yc", 1024, F32)
    Ycb = [Buf("Yc_lo", Yc.base), Buf("Yc_hi", Yc.base)]
    scr = [S.carve(f"scr{i}", 512, F32) for i in range(3)]
    Yg = S.carve("Yg", 3 * 512, BF16)
    t1f = S.carve("t1f", 1024, F32)
    junk = t1f
    kT = S.carve("kT", 2 * 256, BF16)
    Vv = S.carve("Vv", 2 * 256, BF16)
    stb = [S.carve(f"st{k}", 4, F32) for k in range(4)]
    scr_i = [0]

    def scratch():
        b = scr[scr_i[0] % 3]
        scr_i[0] += 1
        assert b.last_w is None or len(b.readers) > 0, f"scratch {b.name} reused before consumed"
        return b

    for Uz in Uz2:
        S.op(G_, lambda e, Uz=Uz: e.memset(Uz.v(), 0.0), writes=[Uz])
    tix = {(b_, i_): n_ for n_, (b_, i_) in enumerate((b_, i_) for b_ in range(NB) for i_ in range(NT))}

    cmm_done = set()

    def par(b, i):
        return tix[(b, i)] % 2

    def rms_stats_a(src_ap, src_bufs, k, jb=None):
        sb_ = stb[k]
        jb = junk if jb is None else jb
        S.op(A, lambda e: e.activation(out=jb.v([[1, 1024]]), in_=src_ap, func=AF.Square,
                                       accum_out=sb_.v([[1, 1]], off=0)),
             reads=src_bufs, writes=[jb, sb_])
        return sb_.v([[1, 1]], off=2)

    def rms_stats_b(k):
        sb_ = stb[k]
        S.op(G_, lambda e: e.tensor_scalar(out=sb_.v([[1, 1]], off=1), in0=sb_.v([[1, 1]], off=0),
                                           scalar1=1.0 / D, scalar2=EPS, op0=ALU.mult, op1=ALU.add),
             reads=[sb_], writes=[sb_])
        S.op(G_, lambda e: e.tensor_tensor(out=sb_.v([[1, 1]], off=2), in0=sb_.v([[1, 1]], off=1),
                                           in1=small.v([[1, 1]], off=23), op=ALU.pow),
             reads=[sb_, small], writes=[sb_])

    def nt_front(srcs):
        rs = [rms_stats_a(src.v(), [src], k) for k, src in enumerate(srcs)]
        for k, src in enumerate(srcs):
            rms_stats_b(k)
        for k, src in enumerate(srcs):
            S.op(A, lambda e, k=k, src=src: e.activation(out=hb2[k].v(), in_=src.v(), func=AF.Copy, scale=rs[k]),
                 reads=[src, stb[k]], writes=[hb2[k]])

    def nt_back(dst_offs, ncols_dst, scale_cols=None):
        for k in range(2):
            hb = hb2[k]
            for dc in range(8):
                S.op(P, lambda e, dc=dc, hb=hb: e.transpose(ps_T.v([[1, 128]], off=dc * 128),
                                                            hb.v([[1, 128]], off=dc * 128), identb.v()),
                     reads=[hb, identb], writes=[ps_T, ps_Tf])
            dst_off = dst_offs[k]
            if scale_cols is None:
                S.op(A, lambda e, dst_off=dst_off: e.activation(
                    out=hT.v([[ncols_dst, 8], [1, 128]], off=dst_off), in_=ps_T.v([[128, 8], [1, 128]]),
                    func=AF.Copy), reads=[ps_T, ps_Tf], writes=[hT])
            else:
                S.op(V, lambda e, dst_off=dst_off: e.tensor_tensor(
                    out=hT.v([[ncols_dst, 8], [1, 128]], off=dst_off), in0=ps_T.v([[128, 8], [1, 128]]),
                    in1=small.v([[1, 8], [0, 128]], off=scale_cols), op=ALU.mult),
                    reads=[ps_T, ps_Tf, small], writes=[hT])

    def norm_transpose_pair(srcs, dst_offs, ncols_dst, scale_cols=None):
        nt_front(srcs)
        nt_back(dst_offs, ncols_dst, scale_cols)

    def dbg_store(key, row0, buf, ap, nrows):
        if not dbg_d:
            return
        S.op(SY, lambda e: e.dma_start(out=dbg_d[key][row0:row0 + nrows, :], in_=ap), reads=[buf],
             dma="dbg_" + buf.name)

    import os as _os
    dbg_tile = (0, int(_os.environ.get('KDBGT', 1 if NT > 1 else 0)))

    wkv = g["wkv"]
    CHUNKS_B1P = [("pool", gi, 96 * gi, 96) for gi in range(4)]
    CHUNKS_B1S = ([("ssm", ct, 384 + 128 * ct, 128) for ct in range(3)]
                  + [("q", a, 768 + 128 * a, 128) for a in range(2)])
    CHUNKS_B1 = CHUNKS_B1P + CHUNKS_B1S
    CHUNKS_B2 = ([("gate", gi, 1024 + 96 * gi, 96) for gi in range(4)]
                 + [("gate", 4 + k, 1024 + 384 + 128 * k, 128) for k in range(5)])

    def gen_kv(b, pool):
        for mb in range(2):
            xs = xb[mb]
            S.op(SY, lambda e, xs=xs, mb=mb, b=b: e.dma_start(out=xs.v(), in_=mem_d[b, mb * 128:(mb + 1) * 128, :]),
                 writes=[xs], dma=xs.name)
        nt_front(xb)
        yield
        yield
        nt_back([0, 128], 512, scale_cols=8)
        yield
        for a in range(2):
            pr = pool.take()
            for dc in range(8):
                mm(pr.v([[1, 256]]), wkv.v([[1, 128]], off=dc * 512 + a * 128), hT.v([[1, 256]], off=dc * 512),
                   dc == 0, dc == 7, [wkv, hT], pr.bufs)
            S.op(A, lambda e, a=a, pr=pr: e.activation(out=kT.v([[1, 256]], off=a * 256), in_=pr.v([[1, 256]]),
                                                       func=AF.Copy, scale=0.125), reads=pr.bufs, writes=[kT])
            yield
        for mc in range(2):
            pr = pool.take()
            for dc in range(8):
                mm(pr.v([[1, 256]]), hT.v([[1, 128]], off=dc * 512 + mc * 128), wkv.v([[1, 256]], off=dc * 512 + 256),
                   dc == 0, dc == 7, [wkv, hT], pr.bufs)
            S.op(A, lambda e, mc=mc, pr=pr: e.activation(out=Vv.v([[1, 256]], off=mc * 256), in_=pr.v([[1, 256]]),
                                                         func=AF.Copy), reads=pr.bufs, writes=[Vv])
            yield

    def st_kv(b):
        for _ in gen_kv(b, Pool_([R_m[2], R_w[1]])):
            pass

    def gen_boundary(b, i, nxt):
        yield from gen_E(b, i, [R_m[2], R_w[1]])
        yield from gen_kv(nxt[0], Pool_([R_m[2], R_w[1]]))
        yield from gen_A(*nxt)
        st_reset_halo(nxt[0])
        yield from gen_B(*nxt, CHUNKS_B1S + CHUNKS_B1P, Pool_([R_w[0]]))

    def st_reset_state():
        S.op(G_, lambda e: e.memset(Hc.v(), 0.0), writes=[Hc])

    def st_reset_halo(b):
        Up = Up2[par(b, 0)]
        S.op(G_, lambda e: e.memset(Up.v([[528, 4], [1, 16]], np_=96), 0.0), writes=[Up])

    def a_front(b, i, pr):
        t0 = i * T
        for k in range(2):
            xs = xb[k]
            blk = pr * 2 + k
            S.op(SY, lambda e, xs=xs, b=b, r0=t0 + blk * 128: e.dma_start(out=xs.v(), in_=x_d[b, r0:r0 + 128, :]),
                 writes=[xs], dma=xs.name)
        nt_front(xb)

    def gen_A(b, i, skip_front0=False):
        if not skip_front0:
            a_front(b, i, 0)
            yield
        nt_back([0, 128], 512)
        a_front(b, i, 1)
        yield
        if skip_front0:
            yield
            yield
        nt_back([256, 384], 512)
        yield

    def st_A(b, i):
        for _ in gen_A(b, i):
            pass

    def gen_AB(b, i, chunks, pool, skip_front0=False, stride=1):
        yield from gen_A(b, i, skip_front0)
        yield from gen_B(b, i, chunks, pool, stride)

    def gen_B(b, i, chunks, pool, stride=1):
        dbg_on = bool(dbg_d) and (b, i) == dbg_tile
        Up, Uz, qT = Up2[par(b, i)], Uz2[par(b, i)], qT2[par(b, i)]
        for kind, idx, c0, M in chunks:
            pr = pool.take()
            pb = pr
            for dc in range(8):
                mm(pb.v([[1, 512]], np_=M), wi.v([[1, M]], off=dc * 2048 + c0), hT.v([[1, 512]], off=dc * 512),
                   dc == 0, dc == 7, [wi, hT], pr.bufs)
            if dbg_on:
                sd = scratch()
                S.op(V, lambda e, pb=pb, sd=sd, M=M: e.tensor_copy(out=sd.v(np_=M), in_=pb.v(np_=M)),
                     reads=pr.bufs, writes=[sd])
                dbg_store("proj", c0, sd, sd.v(np_=M), M)
            if kind == "pool":
                assert tix[(b, i)] == 0 or (tix[(b, i)] - 1) in cmm_done, "pool chunk before previous Cmm"
                S.op(A, lambda e, pb=pb, idx=idx: e.activation(out=Up.v([[1, 512]], off=idx * 528 + 16, np_=96),
                                                               in_=pb.v(np_=96), func=AF.Copy),
                     reads=pr.bufs, writes=[Up])
            elif kind == "ssm":
                S.op(A, lambda e, pb=pb, idx=idx: e.activation(
                    out=Uz.v([[16, 64], [1, 8]], off=idx * 1024 + 8), in_=pb.v([[8, 64], [1, 8]]), func=AF.Copy),
                    reads=pr.bufs, writes=[Uz])
            elif kind == "q":
                S.op(A, lambda e, pb=pb, idx=idx: e.activation(out=qT.v([[1, 512]], off=idx * 512), in_=pb.v(),
                                                               func=AF.Copy), reads=pr.bufs, writes=[qT])
            else:
                S.op(A, lambda e, pb=pb, idx=idx, M=M: e.activation(
                    out=Gt.v([[1, 512]], off=idx * 512, np_=M), in_=pb.v(np_=M), func=AF.Silu),
                    reads=pr.bufs, writes=[Gt])
            for _ in range(stride):
                yield

    def st_B(b, i, chunks, pool=None):
        for _ in gen_B(b, i, chunks, pool if pool is not None else Pool_(R_w)):
            pass

    def st_Csum(b, i):
        Up = Up2[par(b, i)]
        for gi, w in enumerate(WINDOWS):
            u0 = gi * 528

            def U(c_lo, n, u0=u0):
                return Up.v([[1, n]], off=u0 + c_lo, np_=96)

            def sa(buf, c_lo, n):
                return buf.v([[1, n]], off=c_lo, np_=96)

            outS = Sb.v([[1, 512]], off=gi * 512, np_=96)
            steps = int(math.log2(w))
            prev_buf = None
            for s_i in range(steps):
                sh = 1 << s_i
                last = s_i == steps - 1
                lo = 16 if last else (2 * sh - 1)
                n = 528 - lo
                if s_i == 0:
                    in0, in1, rd = U(lo, n), U(lo - sh, n), [Up]
                else:
                    in0, in1, rd = sa(prev_buf, lo, n), sa(prev_buf, lo - sh, n), [prev_buf]
                if last:
                    tt(outS, in0, in1, ALU.add, rd, [Sb], eng=G_)
                    if i == 0 and w > 1:
                        dst16 = SB if prev_buf is SA else SA
                        if s_i == 0:
                            a0, a1 = U(16, 16), U(16 - sh, 16)
                        else:
                            a0, a1 = sa(prev_buf, 16, 16), sa(prev_buf, 16 - sh, 16)
                        tt(sa(dst16, 0, 16), a0, a1, ALU.add, rd, [dst16], eng=G_)
                        tt(Sb.v([[1, 16]], off=gi * 512, np_=96), sa(dst16, 0, 16),
                           ctab.v([[1, 16]], off=gi * 16, np_=96), ALU.mult, [dst16, ctab], [Sb], eng=G_)
                else:
                    dst = SA if prev_buf is not SA else SB
                    tt(sa(dst, lo, n), in0, in1, ALU.add, rd, [dst], eng=G_)
                    prev_buf = dst

    def gen_Cmm(b, i, cmm_pool):
        dbg_on = bool(dbg_d) and (b, i) == dbg_tile
        Up = Up2[par(b, i)]
        UpN = Up2[1 - par(b, i)]
        for gi, w in enumerate(WINDOWS):
            outS = Sb.v([[1, 512]], off=gi * 512, np_=96)
            pr = cmm_pool.take()
            pb = pr
            mm(pb.v([[1, 512]], np_=96), wp1.v([[1, 96]], off=gi * 96, np_=96), outS, True, False, [wp1, Sb], pr.bufs)
            mm(pb.v([[1, 512]], np_=96), wp2.v([[1, 96]], off=gi * 96, np_=96),
               Up.v([[1, 512]], off=gi * 528 + 16, np_=96), False, True, [wp2, Up], pr.bufs)
            if dbg_on:
                sd = scratch()
                S.op(V, lambda e, pb=pb, sd=sd, gi=gi: e.tensor_scalar(
                    out=sd.v(np_=96), in0=pb.v(np_=96), scalar1=small.v([[1, 1]], off=16 + gi, np_=96),
                    scalar2=None, op0=ALU.mult), reads=pr.bufs + [small], writes=[sd])
                dbg_store("ypool", 96 * gi, sd, sd.v(np_=96), 96)
            gv = Gt.v([[1, 512]], off=gi * 512, np_=96)
            S.op(V, lambda e, pb=pb, gi=gi, gv=gv: e.scalar_tensor_tensor(
                out=gv, in0=pb.v(np_=96), scalar=small.v([[1, 1]], off=16 + gi, np_=96), in1=gv,
                op0=ALU.mult, op1=ALU.mult), reads=pr.bufs + [small, Gt], writes=[Gt])
            if gi == 3:
                if i + 1 < NT:
                    S.op(G_, lambda e: e.tensor_copy(out=UpN.v([[528, 4], [1, 15]], off=1, np_=96),
                                                     in_=Up.v([[528, 4], [1, 15]], off=513, np_=96)),
                         reads=[Up], writes=[UpN] if UpN is not Up else [Up])
                cmm_done.add(tix[(b, i)])
            yield

    xb4 = []

    def st_D1a(b, i):
        Uz = Uz2[par(b, i)]
        S.op(V, lambda e: e.tensor_copy(out=Hs.v([[65, 24], [1, 1]]), in_=Hc.v([[1, 24], [1, 1]])),
             reads=[Hc], writes=[Hs])
        xbanks = [bank("w", force=0), bank("w"), bank("m", force=0), bank("m")]
        xb4[:] = xbanks
        for ct in range(3):
            for part in range(2):
                for j in range(8):
                    for pp in range(4):
                        mm(xbanks[pp].v([[1, 64]], off=(ct * 2 + part) * 64),
                           W1.v([[1, 128]], off=ct * 2048 + (j * 2 + part) * 128, p0=32 * pp, np_=32),
                           Uz.v([[16, 64]], off=ct * 1024 + 8 + j, p0=32 * pp, np_=32),
                           j == 0, j == 7, [W1, Uz], [xbanks[pp]], tp=((96, 0) if pp == 3 else None))

    def st_D1b(b, i):
        xbanks = list(xb4)
        w0 = xbanks[0]
        xre = w0.v([[512, 4], [128, 3], [1, 64]], off=0)
        xim = w0.v([[512, 4], [128, 3], [1, 64]], off=64)
        tabv = lambda buf: buf.v([[64, 4], [256, 3], [1, 64]])
        t3 = lambda buf: buf.v([[192, 4], [64, 3], [1, 64]])
        btv = lambda off: bt.v([[64, 4], [256, 3], [1, 64]], off=off)
        T0 = lambda dims=None: tmpS.v(dims if dims is not None else [[1, 768]], off=0)
        T1 = lambda dims=None: tmpS.v(dims if dims is not None else [[1, 768]], off=768)
        d3 = [[192, 4], [64, 3], [1, 64]]
        tt(T0(d3), xre, tabv(cosT), ALU.mult, xbanks + [cosT], [tmpS])
        tt(T1(d3), xim, tabv(sinT), ALU.mult, xbanks + [sinT], [tmpS])
        tt(btv(0), T0(d3), T1(d3), ALU.add, [tmpS], [bt])
        tt(T0(d3), xim, tabv(cosT), ALU.mult, xbanks + [cosT], [tmpS])
        tt(T1(d3), xre, tabv(sinT), ALU.mult, xbanks + [sinT], [tmpS])
        tt(btv(768), T0(d3), T1(d3), ALU.subtract, [tmpS], [bt])
        tt(hct.v([[12, 2], [1, 12]]), Hc.v([[12, 2], [1, 12]]), rQ.v([[0, 2], [1, 12]]), ALU.mult, [Hc, rQ], [hct])
        tt(bt.v([[64, 24], [1, 1]]), bt.v([[64, 24], [1, 1]]), hct.v([[1, 24], [1, 1]]), ALU.add, [bt, hct], [bt])
        for part in range(2):
            S.op(V, lambda e, part=part: e.tensor_tensor_scan(
                out=bt.v([[1, 768]], off=part * 768), data0=decT.v(), data1=bt.v([[1, 768]], off=part * 768),
                initial=0.0, op0=ALU.mult, op1=ALU.add), reads=[decT, bt], writes=[bt])
        Gs = bt
        gre = Gs.v([[1, 768]], off=0)
        gim = Gs.v([[1, 768]], off=768)
        hv = lambda off: Hs.v([[65, 12], [1, 64]], off=off + 1)
        d12 = [[64, 12], [1, 64]]
        tt(T0(), gre, cosT.v(), ALU.mult, [Gs, cosT], [tmpS])
        tt(T1(), gim, sinT.v(), ALU.mult, [Gs, sinT], [tmpS])
        tt(hv(0), T0(d12), T1(d12), ALU.subtract, [tmpS], [Hs])
        tt(T0(), gre, sinT.v(), ALU.mult, [Gs, sinT], [tmpS])
        tt(T1(), gim, cosT.v(), ALU.mult, [Gs, cosT], [tmpS])
        tt(hv(780), T0(d12), T1(d12), ALU.add, [tmpS], [Hs])

    def st_D1c(b, i):
        Gs = bt
        lastv = lambda buf, off: buf.v([[64, 12], [1, 1]], off=off + 63)
        h4 = [hct.v([[1, 12], [1, 1]], off=12 * k) for k in range(4)]
        tt(h4[0], lastv(Gs, 0), lastv(cosT, 0), ALU.mult, [Gs, cosT], [hct], eng=G_)
        tt(h4[1], lastv(Gs, 768), lastv(sinT, 0), ALU.mult, [Gs, sinT], [hct], eng=G_)
        tt(h4[2], lastv(Gs, 0), lastv(sinT, 0), ALU.mult, [Gs, sinT], [hct], eng=G_)
        tt(h4[3], lastv(Gs, 768), lastv(cosT, 0), ALU.mult, [Gs, cosT], [hct], eng=G_)
        tt(Hc.v([[1, 12], [1, 1]]), h4[0], h4[1], ALU.subtract, [hct], [Hc], eng=G_)
        tt(Hc.v([[1, 12], [1, 1]], off=12), h4[2], h4[3], ALU.add, [hct], [Hc], eng=G_)

    def gen_D2(b, i, d2_idle=2):
        dbg_on = bool(dbg_d) and (b, i) == dbg_tile
        Uz = Uz2[par(b, i)]
        pool = Pool_([R_m[0], R_m[1]])

        def conv(ct, pr):
            for m in range(8):
                mm(pr.v([[8, 64], [1, 8]]), Kc.v([[1, 128]], off=ct * 1024 + m * 128),
                   Uz.v([[16, 64], [1, 8]], off=ct * 1024 + 8 - m), m == 0, m == 7, [Kc, Uz], pr.bufs)

        assert R_po[0].consumed() and R_po[1].consumed()
        conv(0, R_po[0])
        yield
        conv(1, R_po[1])
        yield
        for _ in range(d2_idle):
            yield
        yield from gen_Cmm(b, i, pool)
        for ct in range(3):
            hp_ = 64 * (ct % 2)
            for half in range(2):
                pr = pool.take()
                for q2 in range(2):
                    pp = half * 2 + q2
                    p = ct * 4 + pp
                    mm(pr.v([[1, 256]], off=q2 * 256, p0=hp_, np_=64), Hs.v([[1, 64]], off=p * 65),
                       W3r.v([[1, 256]], off=p * 288 + 32), True, False, [Hs, W3r], pr.bufs)
                    mm(pr.v([[1, 256]], off=q2 * 256, p0=hp_, np_=64), Hs.v([[1, 64]], off=780 + p * 65),
                       W3i.v([[1, 256]], off=p * 288 + 32), False, True, [Hs, W3i], pr.bufs)
                S.op(V, lambda e, pr=pr, half=half, hp_=hp_: e.tensor_copy(
                    out=Yc.v([[32, 2], [128, 8], [1, 32]], off=half * 64, p0=hp_, np_=64),
                    in_=pr.v([[256, 2], [32, 8], [1, 32]], p0=hp_, np_=64)),
                    reads=pr.bufs, writes=[Ycb[ct % 2]])
            yield
            ptr = pool.take()
            for j in range(8):
                S.op(P, lambda e, j=j, ptr=ptr, hp_=hp_: e.transpose(
                    ptr.v([[1, 64]], off=j * 64), Yc.v([[1, 128]], off=j * 128, p0=hp_, np_=64),
                    identf.v([[1, 64]], off=hp_, p0=hp_, np_=64)), reads=[Ycb[ct % 2], identf], writes=ptr.bufs)
            isb = tmpS
            S.op(V, lambda e, ptr=ptr, isb=isb: e.tensor_copy(
                out=isb.v([[1, 8], [8, 64]]), in_=ptr.v([[64, 8], [1, 64]])),
                reads=ptr.bufs, writes=[isb])
            yield
            if ct < 2:
                pcv = R_po[ct]
            else:
                pcv = pool.take()
                conv(2, pcv)
            yl = scratch()
            tt(yl.v(), pcv.v(), isb.v([[1, 512]]), ALU.add, pcv.bufs + [isb], [yl])
            if ct < 2:
                pool.add(R_po[ct])
            if dbg_on:
                dbg_store("ylin", 128 * ct, yl, yl.v(), 128)
            S.op(A, lambda e, ct=ct, yl=yl: e.activation(out=Yg.v([[1, 512]], off=ct * 512), in_=yl.v(),
                                                         func=AF.Gelu_apprx_tanh), reads=[yl], writes=[Yg])
            yield
        for c3 in range(3):
            pz = []
            for oc in (c3, 3 + c3):
                pr = pool.take()
                for ct in range(3):
                    mm(pr.v(), wg.v([[1, 128]], off=ct * 768 + oc * 128), Yg.v([[1, 512]], off=ct * 512),
                       ct == 0, ct == 2, [wg, Yg], pr.bufs)
                pz.append(pr)
            sg = scratch()
            S.op(A, lambda e, sg=sg, pr=pz[1]: e.activation(out=sg.v(), in_=pr.v(), func=AF.Sigmoid),
                 reads=pz[1].bufs, writes=[sg])
            tmp = scratch()
            tt(tmp.v(), pz[0].v(), sg.v(), ALU.mult, pz[0].bufs + [sg], [tmp])
            if dbg_on:
                dbg_store("yssm", 128 * c3, tmp, tmp.v(), 128)
            gv = Gt.v([[1, 512]], off=(4 + c3) * 512)
            tt(gv, tmp.v(), gv, ALU.mult, [tmp, Gt], [Gt], eng=G_)
            yield

    def gen_E(b, i, banks):
        dbg_on = bool(dbg_d) and (b, i) == dbg_tile
        qT = qT2[par(b, i)]
        pool = Pool_(banks)
        for a in range(2):
            for hh in range(2):
                for mc in range(2):
                    pr = pool.take()
                    mm(pr.v(), kT.v([[1, 128]], off=a * 256 + mc * 128, p0=64 * hh, np_=64),
                       qT.v([[1, 512]], off=a * 512, p0=64 * hh, np_=64), True, True, [kT, qT], pr.bufs)
                    S.op(A, lambda e, pr=pr, hh=hh, mc=mc: e.activation(
                        out=PT.v([[1, 512]], off=(hh * 2 + mc) * 512), in_=pr.v(), func=AF.Exp),
                        reads=pr.bufs, writes=[PT])
                    yield
            po_ = pool.take()
            pd_ = pool.take()
            for hh in range(2):
                for mc in range(2):
                    mm(po_.v([[1, 512]], p0=64 * hh, np_=64),
                       Vv.v([[1, 64]], off=mc * 256 + (2 * a + hh) * 64),
                       PT.v([[1, 512]], off=(hh * 2 + mc) * 512), mc == 0, mc == 1, [Vv, PT], po_.bufs)
            for hh in range(2):
                for mc in range(2):
                    mm(pd_.v([[1, 512]], p0=64 * hh, np_=64), ones.v([[1, 64]]),
                       PT.v([[1, 512]], off=(hh * 2 + mc) * 512), mc == 0, mc == 1, [ones, PT], pd_.bufs)
            ld_ = scratch()
            S.op(A, lambda e, ld_=ld_, pd_=pd_: e.activation(out=ld_.v(), in_=pd_.v(), func=AF.Ln),
                 reads=pd_.bufs, writes=[ld_])
            rd_ = ld_
            S.op(A, lambda e, rd_=rd_, ld_=ld_: e.activation(out=rd_.v(), in_=ld_.v(), func=AF.Exp, scale=-1.0),
                 reads=[ld_], writes=[rd_])
            ya = scratch()
            tt(ya.v(), po_.v(), rd_.v(), ALU.mult, po_.bufs + [rd_], [ya])
            if dbg_on:
                dbg_store("yatt", 128 * a, ya, ya.v(), 128)
            gv = Gt.v([[1, 512]], off=(7 + a) * 512)
            tt(gv, ya.v(), gv, ALU.mult, [ya, Gt], [Gt], eng=G_)
            yield

    def merge(*gens):
        gens = list(gens)
        while gens:
            for g_ in list(gens):
                try:
                    next(g_)
                except StopIteration:
                    gens.remove(g_)

    def st_F(b, i):
        t0 = i * T
        rows = [96] * 4 + [128] * 5
        ps_w = g["ps_w"]
        for blk in range(4):
            xs = xr[blk % 2]
            r0 = t0 + blk * 128
            S.op(SY, lambda e, xs=xs, b=b, r0=r0: e.dma_start(out=xs.v(), in_=x_d[b, r0:r0 + 128, :]),
                 writes=[xs], dma=xs.name)
            pbufs = ps_po if blk % 2 == 1 else [ps_w[0], ps_w[1]]
            po_ap = pbufs[0].v([[1, 1024]])
            halves = [pbufs[0].v(), pbufs[1].v()]
            hb_ = pbufs
            for hf in range(2):
                for ci, nr in enumerate(rows):
                    mm(halves[hf], Gt.v([[1, 128]], off=ci * 512 + blk * 128, np_=nr),
                       wo.v([[1, 512]], off=ci * 1024 + hf * 512, np_=nr), ci == 0, ci == 8, [Gt, wo], [hb_[hf]])
            k = 2 + (blk % 2)
            r = rms_stats_a(po_ap, pbufs, k, jb=PT)
            rms_stats_b(k)
            S.op(V, lambda e, r=r, po_ap=po_ap: e.scalar_tensor_tensor(
                out=t1f.v(), in0=po_ap, scalar=r, in1=gpost.v(), op0=ALU.mult, op1=ALU.mult),
                reads=list(pbufs) + [stb[k], gpost], writes=[t1f])
            tt(xs.v(), xs.v(), t1f.v(), ALU.add, [xs, t1f], [xs], eng=G_)
            S.op(SY, lambda e, xs=xs, b=b, r0=r0: e.dma_start(out=out_d[b, r0:r0 + 128, :], in_=xs.v()),
                 reads=[xs], dma=xs.name)

    tiles = [(b, i) for b in range(NB) for i in range(NT)]
    st_kv(0)
    st_reset_state()
    st_reset_halo(0)
    st_A(*tiles[0])
    st_B(*tiles[0], CHUNKS_B1)
    for n_, (b, i) in enumerate(tiles):
        nxt = tiles[n_ + 1] if n_ + 1 < len(tiles) else None
        same = nxt is not None and nxt[0] == b
        st_Csum(b, i)
        st_B(b, i, CHUNKS_B2)
        if same:
            a_front(*nxt, 0)
        st_D1a(b, i)
        st_D1b(b, i)
        if same:
            merge(gen_D2(b, i), gen_E(b, i, [R_m[2], R_w[1]]),
                  gen_AB(*nxt, CHUNKS_B1S + CHUNKS_B1P, Pool_([R_w[0]]), skip_front0=True, stride=2))
            st_D1c(b, i)
        elif nxt is not None:
            merge(gen_D2(b, i), gen_boundary(b, i, nxt))
            st_D1c(b, i)
            st_reset_state()
        else:
            merge(gen_D2(b, i), gen_E(b, i, [R_m[2], R_w[1]]))
            st_D1c(b, i)
        st_F(b, i)


def host_layout(inp):
    f = lambda a: np.ascontiguousarray(np.asarray(a, dtype=np.float32))
    m = {}
    m["w_in"] = f(inp["w_in"][0])
    m["w_out"] = f(inp["w_out"][0])
    m["w_glu"] = f(inp["w_glu"][0])
    m["w_kv"] = f(inp["w_kv"][0])
    m["w_pool_t"] = f(np.transpose(inp["w_pool"][0], (1, 0, 2)).reshape(96, 384))
    m["gpre_t"] = f(inp["g_pre"][0].reshape(8, 128).T)
    m["gmem_t"] = f(inp["g_mem"][0].reshape(8, 128).T)
    m["g_post"] = f(inp["g_post"][0].reshape(1, D))
    m["pscale_t"] = f(inp["pool_scale"][0].reshape(4, 96).T)
    m["dskip_t"] = f(inp["d_skip"][0].reshape(3, 128).T)
    ct = np.ones((4, 16), np.float32)
    for gi, w in enumerate(WINDOWS):
        for t in range(16):
            ct[gi, t] = float(w) / float(min(t + 1, w))
    m["pool_ctab"] = f(np.broadcast_to(ct.reshape(1, 64), (96, 64)))
    m["ident"] = f(np.eye(128, dtype=np.float32))
    a_re, a_im, ldt = inp["a_re"][0], inp["a_im"][0], inp["log_dt"][0]
    b_re, b_im, c_re, c_im = inp["b_re"][0], inp["b_im"][0], inp["c_re"][0], inp["c_im"][0]
    sm = lambda a: f(a.reshape(12, 128).T)
    m["sm_are"], m["sm_aim"] = sm(a_re), sm(a_im)
    m["sm_ldt"] = f(np.repeat(ldt.reshape(12, 2), 64, axis=1).T)
    smb = lambda a: f(a.reshape(12, 2, 64, 16).transpose(1, 2, 0, 3).reshape(128, 192))
    m["sm_bre"], m["sm_bim"] = smb(b_re), smb(b_im)
    smc = lambda a: f(a.reshape(12, 2, 16, 64).transpose(1, 3, 0, 2).reshape(128, 192))
    m["sm_cre"], m["sm_cim"] = smc(c_re), smc(c_im)
    def cm_rep(a):
        v = a.reshape(3, 4, 2, 64)
        v = v.transpose(1, 0, 2, 3).reshape(4, 1, 3 * 128)
        return f(np.broadcast_to(v, (4, 32, 384)).reshape(128, 384))
    m["cm_are"], m["cm_aim"] = cm_rep(a_re), cm_rep(a_im)
    m["cm_ldt"] = cm_rep(np.repeat(ldt.reshape(24, 1), 64, axis=1))
    def cm_b(a):
        v = a.reshape(3, 4, 2, 64, 16)
        o = np.zeros((4, 2, 16, 3, 2, 64), np.float32)
        for g2 in range(2):
            o[:, g2, :, :, g2, :] = v[:, :, g2].transpose(1, 3, 0, 2)
        return f(o.reshape(128, 384))
    m["cm_bre"], m["cm_bim"] = cm_b(b_re), cm_b(b_im)
    return m


_NC_CACHE = {}


def kernel(**inputs):
    x = np.asarray(inputs["x"], dtype=np.float32)
    mem = np.asarray(inputs["mem"], dtype=np.float32)
    B, L, _ = x.shape
    NB = B // N_CORES
    key = (NB, L)
    if key not in _NC_CACHE:
        _NC_CACHE[key] = build_nc(NB, L)
    nc = _NC_CACHE[key]
    shared = host_layout(inputs)
    in_maps = []
    for c in range(N_CORES):
        mp = dict(shared)
        mp["x"] = np.ascontiguousarray(x[c * NB:(c + 1) * NB])
        mp["mem"] = np.ascontiguousarray(mem[c * NB:(c + 1) * NB])
        in_maps.append(mp)
    res = run_bass_kernel_spmd(nc, in_maps, core_ids=list(range(N_CORES)))
    return np.concatenate([np.asarray(r["out"], dtype=np.float32) for r in res.results], axis=0)
```

```python
import contextlib
import math

import numpy as np

import concourse.bass as bass
import concourse.mybir as mybir
from concourse.bass_utils import run_bass_kernel_spmd

F32 = mybir.dt.float32
BF16 = mybir.dt.bfloat16
AF = mybir.ActivationFunctionType
ALU = mybir.AluOpType

N_CORES = 8
D = 1024
POOL_W = 384
SSM_W = 384
ATT_W = 256
POOL_GW = 96
WINDOWS = (2, 4, 8, 16)
N_MEM = 256
HD = 64
EPS = 1e-6
Q = 8
T = 512
NCH = T // Q


class Buf:
    def __init__(self, name, base):
        self.name = name
        self.base = base
        self.tensor = base.tensor
        self.off0 = base.offset
        self.pstep = base.ap[0][0]
        self.n = base.ap[-1][1]
        self.last_w = None
        self.readers = []

    def v(self, dims=None, off=0, p0=0, np_=128):
        if dims is None:
            dims = [[1, self.n]]
        return bass.AP(self.tensor, self.off0 + p0 * self.pstep + off,
                       [[self.pstep, np_]] + [list(d) for d in dims])


class Sched:
    SEM_LIMIT = 30000

    def __init__(self, nc, es):
        self.nc = nc
        self.es = es
        self.ops = []
        self.sem_cache = {}

    def buf(self, name, n, dtype, psum=False):
        if psum:
            t = self.es.enter_context(self.nc.psum_tensor(name, [128, n], dtype))
        else:
            t = self.es.enter_context(self.nc.sbuf_tensor(name, [128, n], dtype))
        return Buf(name, t[:])

    def arena(self, n_f32):
        self.arena_t = self.es.enter_context(self.nc.sbuf_tensor("arena", [128, n_f32], F32))
        self.arena_n = n_f32
        self.arena_pos = 0
        self.arena_max = 0

    def arena_reset(self):
        self.phase_max = getattr(self, "phase_max", []) + [self.arena_pos]
        self.arena_pos = 0

    def carve(self, name, n, dtype):
        w = n if dtype == F32 else (n + 1) // 2
        w = (w + 1) // 2 * 2
        c0 = self.arena_pos
        assert c0 + w <= self.arena_n, f"arena overflow at {name}: {c0 + w} > {self.arena_n}"
        self.arena_pos += w
        self.arena_max = max(self.arena_max, self.arena_pos)
        base = self.arena_t[:, c0:c0 + w]
        if dtype != F32:
            base = base.bitcast(dtype)
        return Buf(name, base)

    def barrier(self, exclude=()):
        last = {}
        for i, o in enumerate(self.ops):
            key = ("dma", o["dma"]) if o["dma"] is not None else ("eng", o["eng"])
            if o["dma"] is not None and o["dma"] in exclude:
                continue
            last[key] = i
        deps = set(last.values())
        for eng in ("tensor", "vector", "scalar", "gpsimd", "sync"):
            idx = len(self.ops)
            self.ops.append(dict(eng=eng, fn=(lambda e: e.nop()), deps=set(deps), dma=None))

    def _deps(self, eng, reads, writes):
        deps = set()
        for b in reads:
            if b.last_w is not None:
                deps.add(b.last_w)
        for b in writes:
            if b.last_w is not None:
                deps.add(b.last_w)
            for r in b.readers:
                deps.add(r)
        return deps

    def op(self, eng, fn, reads=(), writes=(), dma=None):
        cap = getattr(self, "_cap", None)
        if cap is not None:
            cap.append((eng, fn, tuple(reads), tuple(writes), dma))
            return -1
        idx = len(self.ops)
        deps = self._deps(eng, reads, writes)
        self.ops.append(dict(eng=eng, fn=fn, deps=deps, dma=dma))
        for b in reads:
            b.readers.append(idx)
        for b in writes:
            b.last_w = idx
            b.readers = []
        return idx

    def capture(self, f):
        self._cap = []
        try:
            f()
            return self._cap
        finally:
            self._cap = None

    def replay_interleaved(self, *lists):
        its = [iter(l) for l in lists]
        while its:
            for it in list(its):
                try:
                    self.op(*next(it))
                except StopIteration:
                    its.remove(it)

    def mark(self, name):
        self.marks = getattr(self, "marks", {})
        if name not in self.marks:
            self.marks[name] = len(self.ops)

    def _sem(self, key):
        if key not in self.sem_cache:
            name = "s_" + "_".join(str(k) for k in key)
            self.sem_cache[key] = self.es.enter_context(self.nc.semaphore(name))
        return self.sem_cache[key]

    def lower(self):
        ops = self.ops
        needed = set()
        for o in ops:
            needed |= o["deps"]
        prog = {e: [] for e in ("tensor", "vector", "scalar", "gpsimd", "sync")}
        cnt = {}
        waited = {}
        for i, o in enumerate(ops):
            eng = o["eng"]
            if o["dma"] is not None:
                stream = ("dma", o["dma"])
                inc = 16
                sig = True
            else:
                stream = ("eng", eng)
                inc = 1
                sig = i in needed
            wl = {}
            for d in o["deps"]:
                od = ops[d]
                ds = od["stream"]
                if ds == stream and eng == "tensor" and od["dma"] is None:
                    continue
                v = od["sig"]
                if ds not in wl or wl[ds] < v:
                    wl[ds] = v
            for ds, v in wl.items():
                w = waited.get((eng, ds))
                if w is not None and w >= v:
                    continue
                waited[(eng, ds)] = v
                sem = self._sem(ds + (v[0],))
                prog[eng].append(("wait", sem, v[1]))
            if sig:
                ep, c = cnt.get(stream, (0, 0))
                if c + inc > self.SEM_LIMIT:
                    ep, c = ep + 1, 0
                c += inc
                cnt[stream] = (ep, c)
                o["sig"] = (ep, c)
                sem = self._sem(stream + (ep,))
                prog[eng].append(("op", o["fn"], sem, inc))
            else:
                o["sig"] = None
                prog[eng].append(("op", o["fn"], None, 0))
            o["stream"] = stream
        self.prog = prog
        return prog

    def emit(self):
        import os
        stop = os.environ.get("KSTOP")
        if stop and stop in getattr(self, "marks", {}):
            self.ops = self.ops[:self.marks[stop]]
        last = {}
        for i, o in enumerate(self.ops):
            key = ("dma", o["dma"]) if o["dma"] is not None else ("eng", o["eng"])
            last[key] = i
        self.ops.append(dict(eng="sync", fn=(lambda e: e.nop()), deps=set(last.values()), dma=None))
        prog = self.lower()
        nc = self.nc

        def run(e, name):
            for it in prog[name]:
                if it[0] == "wait":
                    e.wait_ge(it[1], it[2])
                else:
                    ins = it[1](e)
                    if it[2] is not None:
                        ins.then_inc(it[2], it[3])

        with nc.Block() as block:
            @block.sync
            def _(e):
                run(e, "sync")

            @block.scalar
            def _(e):
                run(e, "scalar")

            @block.vector
            def _(e):
                run(e, "vector")

            @block.gpsimd
            def _(e):
                run(e, "gpsimd")

            @block.tensor
            def _(e):
                run(e, "tensor")


def build_nc(NB, L, dbg=False):
    nc = bass.Bass("TRN2", target_bir_lowering=False)
    NT = L // T

    def din(name, shape):
        return nc.dram_tensor(name, list(shape), F32, kind="ExternalInput").ap()

    x_d = din("x", [NB, L, D])
    mem_d = din("mem", [NB, N_MEM, D])
    w_in_d = din("w_in", [D, 2 * D])
    w_out_d = din("w_out", [D, D])
    w_glu_d = din("w_glu", [SSM_W, 2 * SSM_W])
    w_kv_d = din("w_kv", [D, 2 * ATT_W])
    w_pool_d = din("w_pool_t", [96, 4 * 96])
    gpre_d = din("gpre_t", [128, 8])
    gmem_d = din("gmem_t", [128, 8])
    gpost_d = din("g_post", [1, D])
    pscale_d = din("pscale_t", [96, 4])
    dskip_d = din("dskip_t", [128, 3])
    ctab_d = din("pool_ctab", [96, 64])
    ident_d = din("ident", [128, 128])
    sm_d = {k: din("sm_" + k, [128, 12]) for k in ("are", "aim", "ldt")}
    for k in ("bre", "bim", "cre", "cim"):
        sm_d[k] = din("sm_" + k, [128, 192])
    cm_d = {k: din("cm_" + k, [128, 384]) for k in ("are", "aim", "ldt", "bre", "bim")}
    out_d = nc.dram_tensor("out", [NB, L, D], F32, kind="ExternalOutput").ap()
    dbg_d = {}
    if dbg:
        for k, n in (("proj", 2048), ("ypool", 384), ("yssm", 384), ("yatt", 256), ("ylin", 384)):
            dbg_d[k] = nc.dram_tensor("dbg_" + k, [n, T], F32, kind="ExternalOutput").ap()

    with contextlib.ExitStack() as es:
        S = Sched(nc, es)
        V, A, P, G_, SY = "vector", "scalar", "tensor", "gpsimd", "sync"

        wi = S.buf("wi", 8 * 2048, BF16)
        wo = S.buf("wo", 9 * 1024, BF16)
        wg = S.buf("wg", 3 * 768, BF16)
        wkv = S.buf("wkv", 8 * 512, BF16)
        wp1 = S.buf("wp1", 4 * 96, BF16)
        wp2 = S.buf("wp2", 4 * 96, BF16)
        W1 = S.buf("W1", 3 * 8 * 2 * 128, BF16)
        W3r = S.buf("W3r", 12 * 9 * 32, BF16)
        W3i = S.buf("W3i", 12 * 9 * 32, BF16)
        Kc = S.buf("Kc", 3 * 8 * 128, BF16)
        cosT = S.buf("cosT", 12 * 64, F32)
        sinT = S.buf("sinT", 12 * 64, F32)
        decT = S.buf("decT", 12 * 64, F32)
        rQ = S.buf("rQ", 12, F32)
        gpost = S.buf("gpost", 1024, F32)
        identf = S.buf("identf", 128, F32)
        identb = S.buf("identb", 128, BF16)
        ones = S.buf("ones", 64, BF16)
        small = S.buf("small", 64, F32)
        ctab = S.buf("ctab", 64, F32)
        _ps = es.enter_context(nc.psum_tensor("ps_all", [128, 4096], F32))
        ps_po = [Buf("ps_po0", _ps[:, 0:512]), Buf("ps_po1", _ps[:, 512:1024])]
        ps_T = Buf("ps_T", _ps[:, 1024:1536].bitcast(BF16))
        ps_w = [Buf(f"ps_w{i}", _ps[:, 1536 + 512 * i:2048 + 512 * i]) for i in range(2)]
        ps_m = [Buf(f"ps_m{i}", _ps[:, 2560 + 512 * i:3072 + 512 * i]) for i in range(3)]
        ps_Tf = Buf("ps_Tf", _ps[:, 1024:1536])
        rr = {"w": 0, "m": 0}

        def bank(kind, force=None):
            lst = ps_w if kind == "w" else ps_m
            if force is not None:
                rr[kind] = force
            b = lst[rr[kind] % len(lst)]
            rr[kind] += 1
            assert b.last_w is None or len(b.readers) > 0, f"PSUM bank {b.name} reused before consumed"
            return b

        S.arena(int(nc.sbuf_bytes_remaining) // 4 - 256)

        dq = [SY, A]

        def load(buf, dram_ap, np_=128, dims=None, q=SY, off=0):
            S.op(q, lambda e: e.dma_start(out=buf.v(dims, off=off, np_=np_), in_=dram_ap),
                 writes=[buf], dma=buf.name)

        load(small, gpre_d, dims=[[1, 8]], off=0)
        load(small, gmem_d, dims=[[1, 8]], off=8)
        load(small, pscale_d, np_=96, dims=[[1, 4]], off=16)
        load(small, dskip_d, dims=[[1, 3]], off=20)
        S.op(V, lambda e: e.memset(small.v([[1, 1]], off=23), -0.5), writes=[small])
        load(ctab, ctab_d, np_=96)
        load(identf, ident_d)
        load(gpost, gpost_d.partition_broadcast(128))
        S.op(V, lambda e: e.tensor_copy(out=identb.v(), in_=identf.v()), reads=[identf], writes=[identb])
        S.op(V, lambda e: e.memset(ones.v(), 1.0), writes=[ones])


        def tt(out, a, b, op, reads, writes, eng=V):
            S.op(eng, lambda e: e.tensor_tensor(out=out, in0=a, in1=b, op=op), reads=reads, writes=writes)

        def lam_tables(pref, src, F, emax, EN):
            are = S.carve(pref + "are", F, F32)
            aim = S.carve(pref + "aim", F, F32)
            ldt = S.carve(pref + "ldt", F, F32)
            load(are, src["are"])
            load(aim, src["aim"])
            load(ldt, src["ldt"])
            def tt(out, a_, b_, op, reads, writes):
                S.op(EN, lambda e: e.tensor_tensor(out=out, in0=a_, in1=b_, op=op), reads=reads, writes=writes)

            dt = S.carve(pref + "dt", F, F32)
            ar = S.carve(pref + "ar", F, F32)
            ai = S.carve(pref + "ai", F, F32)
            c = S.carve(pref + "c", F, F32)
            s_ = S.carve(pref + "s", F, F32)
            t1 = S.carve(pref + "t1", F, F32)
            t2 = S.carve(pref + "t2", F, F32)
            mag = S.carve(pref + "mag", F, F32)
            hp = S.carve(pref + "hp", 2, F32)
            S.op(EN, lambda e: e.memset(hp.v([[1, 1]]), math.pi / 2), writes=[hp])
            S.op(A, lambda e: e.activation(out=dt.v(), in_=ldt.v(), func=AF.Exp), reads=[ldt], writes=[dt])
            tt(ar.v(), are.v(), dt.v(), ALU.mult, [are, dt], [ar])
            tt(ai.v(), aim.v(), dt.v(), ALU.mult, [aim, dt], [ai])
            S.op(A, lambda e: e.activation(out=mag.v(), in_=ar.v(), func=AF.Exp), reads=[ar], writes=[mag])
            S.op(A, lambda e: e.activation(out=s_.v(), in_=ai.v(), func=AF.Sin, scale=1.0 / 8),
                 reads=[ai], writes=[s_])
            S.op(A, lambda e: e.activation(out=t1.v(), in_=ai.v(), func=AF.Sin, scale=1.0 / 16),
                 reads=[ai], writes=[t1])
            tt(t2.v(), t1.v(), t1.v(), ALU.mult, [t1], [t2])
            S.op(EN, lambda e: e.tensor_scalar(out=c.v(), in0=t2.v(), scalar1=-2.0, scalar2=1.0,
                                               op0=ALU.mult, op1=ALU.add), reads=[t2], writes=[c])

            def square(cr, ci_):
                tt(t1.v(), cr.v(), cr.v(), ALU.mult, [cr], [t1])
                tt(t2.v(), ci_.v(), ci_.v(), ALU.mult, [ci_], [t2])
                tt(ci_.v(), cr.v(), ci_.v(), ALU.mult, [cr, ci_], [ci_])
                S.op(EN, lambda e: e.tensor_scalar(out=ci_.v(), in0=ci_.v(), scalar1=2.0, scalar2=None,
                                                   op0=ALU.mult), reads=[ci_], writes=[ci_])
                tt(cr.v(), t1.v(), t2.v(), ALU.subtract, [t1, t2], [cr])

            for _ in range(3):
                square(c, s_)
            Lr = S.carve(pref + "Lr", (emax + 1) * F, F32)
            Li = S.carve(pref + "Li", (emax + 1) * F, F32)
            S.op(EN, lambda e: e.memset(Lr.v([[1, F]]), 1.0), writes=[Lr])
            S.op(EN, lambda e: e.memset(Li.v([[1, F]]), 0.0), writes=[Li])
            tt(Lr.v([[1, F]], off=F), mag.v(), c.v(), ALU.mult, [mag, c], [Lr])
            tt(Li.v([[1, F]], off=F), mag.v(), s_.v(), ALU.mult, [mag, s_], [Li])

            def cmul(outr, outi, ar_, ai_, br_, bi_, rd, wr):
                tt(t1.v(), ar_, br_, ALU.mult, rd, [t1])
                tt(t2.v(), ai_, bi_, ALU.mult, rd, [t2])
                tt(outr, t1.v(), t2.v(), ALU.subtract, [t1, t2], wr)
                tt(t1.v(), ar_, bi_, ALU.mult, rd, [t1])
                tt(t2.v(), ai_, br_, ALU.mult, rd, [t2])
                tt(outi, t1.v(), t2.v(), ALU.add, [t1, t2], wr)

            for e_ in range(2, emax + 1):
                cmul(Lr.v([[1, F]], off=e_ * F), Li.v([[1, F]], off=e_ * F),
                     Lr.v([[1, F]], off=(e_ - 1) * F), Li.v([[1, F]], off=(e_ - 1) * F),
                     Lr.v([[1, F]], off=F), Li.v([[1, F]], off=F), [Lr, Li], [Lr, Li])
            cr = S.carve(pref + "cr", F, F32)
            ci_ = S.carve(pref + "ci", F, F32)
            den = dt
            lm1 = ai
            S.op(EN, lambda e: e.tensor_scalar(out=lm1.v(), in0=Lr.v([[1, F]], off=F), scalar1=-1.0,
                                               scalar2=None, op0=ALU.add), reads=[Lr], writes=[lm1])
            tt(t1.v(), are.v(), are.v(), ALU.mult, [are], [t1])
            tt(t2.v(), aim.v(), aim.v(), ALU.mult, [aim], [t2])
            tt(den.v(), t1.v(), t2.v(), ALU.add, [t1, t2], [den])
            S.op(V, lambda e: e.reciprocal(out=den.v(), in_=den.v()), reads=[den], writes=[den])
            L1i = Li.v([[1, F]], off=F)
            tt(t1.v(), lm1.v(), are.v(), ALU.mult, [lm1, are], [t1])
            tt(t2.v(), L1i, aim.v(), ALU.mult, [Li, aim], [t2])
            tt(cr.v(), t1.v(), t2.v(), ALU.add, [t1, t2], [cr])
            tt(cr.v(), cr.v(), den.v(), ALU.mult, [cr, den], [cr])
            tt(t1.v(), L1i, are.v(), ALU.mult, [Li, are], [t1])
            tt(t2.v(), lm1.v(), aim.v(), ALU.mult, [lm1, aim], [t2])
            tt(ci_.v(), t1.v(), t2.v(), ALU.subtract, [t1, t2], [ci_])
            tt(ci_.v(), ci_.v(), den.v(), ALU.mult, [ci_, den], [ci_])
            return dict(Lr=Lr, Li=Li, ar=ar, cr=cr, ci=ci_, ur=c, ui=s_, t1=t1, t2=t2, cmul=cmul,
                        square=square)

        def part_c():
            F = 384
            cm = lam_tables("cm_", cm_d, F, 1, V)
            bre = S.carve("cm_bre", F, F32)
            bim = S.carve("cm_bim", F, F32)
            load(bre, cm_d["bre"])
            load(bim, cm_d["bim"])
            cur = [S.carve(f"cm_cur{i}", 2 * F, F32) for i in range(2)]
            cm["cmul"](cur[0].v([[1, F]]), cur[0].v([[1, F]], off=F), cm["cr"].v(), cm["ci"].v(), bre.v(), bim.v(),
                       [cm["cr"], cm["ci"], bre, bim], [cur[0]])
            for j in range(7, -1, -1):
                k_ = (7 - j) % 2
                cb = cur[k_]
                for part in range(2):
                    S.op(V, lambda e, j=j, part=part, cb=cb: e.tensor_copy(
                        out=W1.v([[8 * 2 * 128, 3], [1, 128]], off=(j * 2 + part) * 128),
                        in_=cb.v([[128, 3], [1, 128]], off=part * 384)), reads=[cb], writes=[W1])
                if j > 0:
                    nb_ = cur[1 - k_]
                    cm["cmul"](nb_.v([[1, F]]), nb_.v([[1, F]], off=F), cb.v([[1, F]]), cb.v([[1, F]], off=F),
                               cm["Lr"].v([[1, F]], off=F), cm["Li"].v([[1, F]], off=F),
                               [cb, cm["Lr"], cm["Li"]], [nb_])


        def part_s():
            sm = lam_tables("sm_", sm_d, 12, 8, V)
            F = 12
            sb = {}
            for k in ("bre", "bim", "cre", "cim"):
                sb[k] = S.carve("sm_" + k, 192, F32)
                load(sb[k], sm_d[k])
            t1w = S.carve("sm_t1w", 192, F32)
            t2w = S.carve("sm_t2w", 192, F32)

            def bc16(buf, off=0):
                return buf.v([[1, 12], [0, 16]], off=off)

            def w16(buf):
                return buf.v([[16, 12], [1, 16]])

            sbbr = S.carve("sm_bbr", 192, F32)
            sbbi = S.carve("sm_bbi", 192, F32)
            tt(w16(t1w), bc16(sm["cr"]), w16(sb["bre"]), ALU.mult, [sm["cr"], sb["bre"]], [t1w])
            tt(w16(t2w), bc16(sm["ci"]), w16(sb["bim"]), ALU.mult, [sm["ci"], sb["bim"]], [t2w])
            tt(sbbr.v(), t1w.v(), t2w.v(), ALU.subtract, [t1w, t2w], [sbbr])
            tt(w16(t1w), bc16(sm["cr"]), w16(sb["bim"]), ALU.mult, [sm["cr"], sb["bim"]], [t1w])
            tt(w16(t2w), bc16(sm["ci"]), w16(sb["bre"]), ALU.mult, [sm["ci"], sb["bre"]], [t2w])
            tt(sbbi.v(), t1w.v(), t2w.v(), ALU.add, [t1w, t2w], [sbbi])
            Bpr = S.carve("Bpr", 12 * 128, F32)
            Bpi = S.carve("Bpi", 12 * 128, F32)
            for bp, srcb in ((Bpr, sbbr), (Bpi, sbbi)):
                S.op(A, lambda e, bp=bp: e.memzero(bp.v()), writes=[bp])
                for pp in range(4):
                    for h in range(2):
                        S.op(V, lambda e, bp=bp, srcb=srcb, pp=pp, h=h: e.tensor_copy(
                            out=bp.v([[4 * 128, 3], [1, 16]], off=pp * 128 + 32 * pp + 16 * h, p0=64 * h, np_=64),
                            in_=srcb.v([[4 * 16, 3], [1, 16]], off=pp * 16, p0=64 * h, np_=64)),
                            reads=[srcb], writes=[bp])
            CLr = S.carve("CLr", 12 * 9 * 32, F32)
            CLi = S.carve("CLi", 12 * 9 * 32, F32)
            S.op(A, lambda e: e.memzero(CLr.v()), writes=[CLr])
            S.op(A, lambda e: e.memzero(CLi.v()), writes=[CLi])
            ta = S.carve("sm_ta", 192, F32)
            tb = S.carve("sm_tb", 192, F32)
            for e_ in range(9):
                lr = bc16(sm["Lr"], off=e_ * 12)
                li = bc16(sm["Li"], off=e_ * 12)
                tt(w16(t1w), lr, w16(sb["cre"]), ALU.mult, [sm["Lr"], sb["cre"]], [t1w])
                tt(w16(t2w), li, w16(sb["cim"]), ALU.mult, [sm["Li"], sb["cim"]], [t2w])
                tt(w16(ta), li, w16(sb["cre"]), ALU.mult, [sm["Li"], sb["cre"]], [ta])
                tt(w16(tb), lr, w16(sb["cim"]), ALU.mult, [sm["Lr"], sb["cim"]], [tb])
                for h in range(2):
                    dst = dict(dims=[[9 * 32, 12], [1, 16]], off=e_ * 32 + 16 * h, p0=64 * h, np_=64)
                    srcv = dict(dims=[[16, 12], [1, 16]], p0=64 * h, np_=64)
                    S.op(V, lambda e, dst=dst, srcv=srcv: e.tensor_tensor(
                        out=CLr.v(**dst), in0=t1w.v(**srcv), in1=t2w.v(**srcv), op=ALU.subtract),
                        reads=[t1w, t2w], writes=[CLr])
                    S.op(V, lambda e, dst=dst, srcv=srcv: e.scalar_tensor_tensor(
                        out=CLi.v(**dst), in0=ta.v(**srcv), scalar=-1.0, in1=tb.v(**srcv),
                        op0=ALU.mult, op1=ALU.subtract), reads=[ta, tb], writes=[CLi])
            S.op(A, lambda e: e.activation(out=W3r.v(), in_=CLr.v(), func=AF.Copy), reads=[CLr], writes=[W3r])
            S.op(A, lambda e: e.activation(out=W3i.v(), in_=CLi.v(), func=AF.Copy), reads=[CLi], writes=[W3i])
            for p in range(12):
                ct, pp = divmod(p, 4)
                pb = bank("m")
                S.op(P, lambda e, p=p, pb=pb: e.matmul(
                    pb.v([[1, 256]]), lhsT=Bpr.v([[1, 128]], off=p * 128),
                    rhs=CLr.v([[1, 256]], off=p * 9 * 32), start=True, stop=False),
                    reads=[Bpr, CLr], writes=[pb])
                S.op(P, lambda e, p=p, pb=pb: e.matmul(
                    pb.v([[1, 256]]), lhsT=Bpi.v([[1, 128]], off=p * 128),
                    rhs=CLi.v([[1, 256]], off=p * 9 * 32), start=False, stop=True),
                    reads=[Bpi, CLi], writes=[pb])
                S.op(V, lambda e, ct=ct, pp=pp, pb=pb: e.tensor_copy(
                    out=Kc.v([[128, 8], [1, 32]], off=ct * 1024 + 32 * pp), in_=pb.v([[32, 8], [1, 32]])),
                    reads=[pb], writes=[Kc])
            for ct in range(3):
                S.op(V, lambda e, ct=ct: e.scalar_tensor_tensor(
                    out=Kc.v([[1, 128]], off=ct * 1024), in0=identf.v(), scalar=small.v([[1, 1]], off=20 + ct),
                    in1=Kc.v([[1, 128]], off=ct * 1024), op0=ALU.mult, op1=ALU.add),
                    reads=[identf, small, Kc], writes=[Kc])
            S.op(A, lambda e: e.activation(out=rQ.v(), in_=sm["ar"].v(), func=AF.Exp, scale=float(Q)),
                 reads=[sm["ar"]], writes=[rQ])
            for _ in range(3):
                sm["square"](sm["ur"], sm["ui"])
            pwr = S.carve("pwr", 12, F32)
            pwi = S.carve("pwi", 12, F32)
            S.op(V, lambda e: e.tensor_copy(out=pwr.v(), in_=sm["ur"].v()), reads=[sm["ur"]], writes=[pwr])
            S.op(V, lambda e: e.tensor_copy(out=pwi.v(), in_=sm["ui"].v()), reads=[sm["ui"]], writes=[pwi])
            S.op(V, lambda e: e.tensor_copy(out=cosT.v([[64, 12], [1, 1]]), in_=pwr.v([[1, 12], [1, 1]])),
                 reads=[pwr], writes=[cosT])
            S.op(V, lambda e: e.tensor_copy(out=sinT.v([[64, 12], [1, 1]]), in_=pwi.v([[1, 12], [1, 1]])),
                 reads=[pwi], writes=[sinT])
            tmr = S.carve("tmr", 12 * 32, F32)
            tmi = S.carve("tmi", 12 * 32, F32)
            n = 1
            while n < 64:
                def tv(buf, off, cnt, n=n):
                    return buf.v([[64, 12], [1, cnt]], off=off)

                def pv(buf, cnt):
                    return buf.v([[1, 12], [0, cnt]])

                def tm(buf, cnt):
                    return buf.v([[32, 12], [1, cnt]])
                tt(tm(tmr, n), tv(cosT, 0, n), pv(pwr, n), ALU.mult, [cosT, pwr], [tmr])
                tt(tm(tmi, n), tv(sinT, 0, n), pv(pwi, n), ALU.mult, [sinT, pwi], [tmi])
                tt(tv(cosT, n, n), tm(tmr, n), tm(tmi, n), ALU.subtract, [tmr, tmi], [cosT])
                tt(tm(tmr, n), tv(cosT, 0, n), pv(pwi, n), ALU.mult, [cosT, pwi], [tmr])
                tt(tm(tmi, n), tv(sinT, 0, n), pv(pwr, n), ALU.mult, [sinT, pwr], [tmi])
                tt(tv(sinT, n, n), tm(tmr, n), tm(tmi, n), ALU.add, [tmr, tmi], [sinT])
                sm["square"](pwr, pwi)
                n *= 2
            S.op(V, lambda e: e.tensor_copy(out=decT.v([[64, 12], [1, 64]]),
                                            in_=rQ.v([[1, 12], [0, 64]])), reads=[rQ], writes=[decT])
            S.op(V, lambda e: e.memset(decT.v([[64, 12], [1, 1]]), 0.0), writes=[decT])


        ops_c = S.capture(part_c)
        ops_s = S.capture(part_s)

        def split_head(ops_):
            n_act = 0
            for k_, o_ in enumerate(ops_):
                if o_[0] == A:
                    n_act += 1
                    if n_act == 4:
                        return ops_[:k_ + 1], ops_[k_ + 1:]
            return ops_, []

        hc_, tc_ = split_head(ops_c)
        hs_, ts_ = split_head(ops_s)
        hc_ = hc_ + [o_ for o_ in tc_ if o_[4] is not None and o_[0] == SY]
        tc_ = [o_ for o_ in tc_ if not (o_[4] is not None and o_[0] == SY)]
        hs_ = hs_ + [o_ for o_ in ts_ if o_[4] is not None and o_[0] == SY]
        ts_ = [o_ for o_ in ts_ if not (o_[4] is not None and o_[0] == SY)]
        hs_ = hs_ + [o_ for o_ in ts_ if o_[0] == A and o_[4] is None and len(o_[2]) == 0]
        ts_ = [o_ for o_ in ts_ if not (o_[0] == A and o_[4] is None and len(o_[2]) == 0)]
        S.replay_interleaved(hc_, hs_)
        wps = S.carve("wps", 384, F32)
        load(wps, w_pool_d, np_=96)
        for gi, w in enumerate(WINDOWS):
            S.op(V, lambda e, gi=gi, w=w: e.tensor_scalar(
                out=wp1.v([[1, 96]], off=gi * 96, np_=96), in0=wps.v([[1, 96]], off=gi * 96, np_=96),
                scalar1=1.0 / w, scalar2=None, op0=ALU.mult), reads=[wps], writes=[wp1])
            S.op(V, lambda e, gi=gi: e.tensor_scalar(
                out=wp2.v([[1, 96]], off=gi * 96, np_=96), in0=wps.v([[1, 96]], off=gi * 96, np_=96),
                scalar1=-1.0, scalar2=None, op0=ALU.mult), reads=[wps], writes=[wp2])

        stg = [S.carve(f"stg{i}", 1024, F32) for i in range(2)]
        for dc in range(8):
            for hf in range(2):
                k_ = (dc * 2 + hf) % 2
                st = stg[k_]
                load(st, w_in_d[dc * 128:(dc + 1) * 128, hf * 1024:(hf + 1) * 1024], q=dq[k_])
                S.op(A, lambda e, st=st, dc=dc, hf=hf: e.activation(
                    out=wi.v([[1, 1024]], off=dc * 2048 + hf * 1024), in_=st.v(), func=AF.Copy,
                    scale=small.v([[1, 1]], off=dc)), reads=[st, small], writes=[wi])
        rows = [96] * 4 + [128] * 5
        r0 = 0
        for ci, nr in enumerate(rows):
            S.op(G_, lambda e, ci=ci, nr=nr, r0=r0: e.dma_start(
                out=wo.v([[1, 1024]], off=ci * 1024, np_=nr), in_=w_out_d[r0:r0 + nr, :]),
                writes=[wo], dma="wo")
            r0 += nr
        for ct in range(3):
            S.op(G_, lambda e, ct=ct: e.dma_start(
                out=wg.v([[1, 768]], off=ct * 768), in_=w_glu_d[ct * 128:(ct + 1) * 128, :]),
                writes=[wg], dma="wg")
        for dc in range(8):
            S.op(G_, lambda e, dc=dc: e.dma_start(
                out=wkv.v([[1, 512]], off=dc * 512), in_=w_kv_d[dc * 128:(dc + 1) * 128, :]),
                writes=[wkv], dma="wkv")
        S.replay_interleaved(tc_, ts_)
        S.mark('t2')
        S.barrier(exclude=('wo', 'wg', 'wkv'))
        S.arena_reset()
        env = dict(locals())
        build_main(nc, S, env)
        S.emit()
    return nc


def build_main(nc, S, g):
    V, A, P, G_, SY = "vector", "scalar", "tensor", "gpsimd", "sync"
    NB, L, NT = g["NB"], g["L"], g["NT"]
    x_d, mem_d, out_d, w_kv_d, dbg_d = g["x_d"], g["mem_d"], g["out_d"], g["w_kv_d"], g["dbg_d"]
    wi, wo, wg, wp1, wp2 = g["wi"], g["wo"], g["wg"], g["wp1"], g["wp2"]
    W1, W3r, W3i, Kc = g["W1"], g["W3r"], g["W3i"], g["Kc"]
    cosT, sinT, decT, rQ = g["cosT"], g["sinT"], g["decT"], g["rQ"]
    gpost, identf, identb, ones, small, ctab = (g["gpost"], g["identf"], g["identb"], g["ones"],
                                                g["small"], g["ctab"])
    ps_po, ps_T, bank = g["ps_po"], g["ps_T"], g["bank"]

    def tt(out, a, b, op, reads, writes, eng=V):
        S.op(eng, lambda e: e.tensor_tensor(out=out, in0=a, in1=b, op=op), reads=reads, writes=writes)

    class BankRef:
        def __init__(self, main, bufs=None):
            self.main = main
            self.bufs = bufs if bufs is not None else [main]

        def v(self, *a, **k):
            return self.main.v(*a, **k)

        def consumed(self):
            return all(b_.last_w is None or len(b_.readers) > 0 for b_ in self.bufs)

    class Pool_:
        def __init__(self, refs):
            self.refs = list(refs)
            self.i = 0

        def take(self):
            r = self.refs[self.i % len(self.refs)]
            self.i += 1
            assert r.consumed(), f"PSUM bank {r.main.name} reused before consumed"
            return r

        def add(self, ref):
            self.refs.insert(self.i % len(self.refs) if self.refs else 0, ref)

    ps_w, ps_m, ps_Tf = g["ps_w"], g["ps_m"], g["ps_Tf"]
    R_w = [BankRef(ps_w[0]), BankRef(ps_w[1])]
    R_m = [BankRef(ps_m[0]), BankRef(ps_m[1]), BankRef(ps_m[2])]
    R_T = BankRef(ps_Tf, [ps_Tf, g["ps_T"]])
    R_po = [BankRef(g["ps_po"][0]), BankRef(g["ps_po"][1])]

    def mm(out, lhsT, rhs, start, stop, reads, writes, tp=None):
        if tp is None:
            S.op(P, lambda e: e.matmul(out, lhsT=lhsT, rhs=rhs, start=start, stop=stop),
                 reads=reads, writes=writes)
        else:
            S.op(P, lambda e: e.matmul(out, lhsT=lhsT, rhs=rhs, start=start, stop=stop, tile_position=tp),
                 reads=reads, writes=writes)

    xb = [S.carve(f"xb{i}", 1024, F32) for i in range(2)]
    xr = xb
    hb2 = [S.carve(f"hb{i}", 1024, BF16) for i in range(2)]
    hT = S.carve("hT", 8 * 512, BF16)
    _Up = S.carve("Up", 4 * 528, BF16)
    Up2 = [_Up, _Up]
    SA = S.carve("SA", 528, F32)
    SB = S.carve("SB", 528, F32)
    Sb = S.carve("Sb", 4 * 512, BF16)
    Uz2 = [S.carve(f"Uz{k}", 3 * 1024, BF16) for k in range(2)]
    qT2 = [S.carve(f"qT{k}", 2 * 512, BF16) for k in range(2)]
    Gt = S.carve("Gt", 9 * 512, BF16)
    PT = S.carve("PT", 4 * 512, BF16)
    bt = S.carve("bt", 2 * 768, F32)
    tmpS = S.carve("tmpS", 2 * 768, F32)
    Hs = S.carve("Hs", 2 * 12 * 65, BF16)
    Hc = S.carve("Hc", 24, F32)
    hct = S.carve("hct", 48, F32)
    Yc = S.carve("Yc", 1024, F32)
    Ycb = [Buf("Yc_lo", Yc.base), Buf("Yc_hi", Yc.base)]
    scr = [S.carve(f"scr{i}", 512, F32) for i in range(3)]
    Yg = S.carve("Yg", 3 * 512, BF16)
    t1f = S.carve("t1f", 1024, F32)
    junk = t1f
    kT = S.carve("kT", 2 * 256, BF16)
    Vv = S.carve("Vv", 2 * 256, BF16)
    stb = [S.carve(f"st{k}", 4, F32) for k in range(4)]
    scr_i = [0]

    def scratch():
        b = scr[scr_i[0] % 3]
        scr_i[0] += 1
        assert b.last_w is None or len(b.readers) > 0, f"scratch {b.name} reused before consumed"
        return b

    for Uz in Uz2:
        S.op(G_, lambda e, Uz=Uz: e.memset(Uz.v(), 0.0), writes=[Uz])
    tix = {(b_, i_): n_ for n_, (b_, i_) in enumerate((b_, i_) for b_ in range(NB) for i_ in range(NT))}

    cmm_done = set()

    def par(b, i):
        return tix[(b, i)] % 2

    def rms_stats_a(src_ap, src_bufs, k, jb=None):
        sb_ = stb[k]
        jb = junk if jb is None else jb
        S.op(A, lambda e: e.activation(out=jb.v([[1, 1024]]), in_=src_ap, func=AF.Square,
                                       accum_out=sb_.v([[1, 1]], off=0)),
             reads=src_bufs, writes=[jb, sb_])
        return sb_.v([[1, 1]], off=2)

    def rms_stats_b(k):
        sb_ = stb[k]
        S.op(G_, lambda e: e.tensor_scalar(out=sb_.v([[1, 1]], off=1), in0=sb_.v([[1, 1]], off=0),
                                           scalar1=1.0 / D, scalar2=EPS, op0=ALU.mult, op1=ALU.add),
             reads=[sb_], writes=[sb_])
        S.op(G_, lambda e: e.tensor_tensor(out=sb_.v([[1, 1]], off=2), in0=sb_.v([[1, 1]], off=1),
                                           in1=small.v([[1, 1]], off=23), op=ALU.pow),
             reads=[sb_, small], writes=[sb_])

    def nt_front(srcs):
        rs = [rms_stats_a(src.v(), [src], k) for k, src in enumerate(srcs)]
        for k, src in enumerate(srcs):
            rms_stats_b(k)
        for k, src in enumerate(srcs):
            S.op(A, lambda e, k=k, src=src: e.activation(out=hb2[k].v(), in_=src.v(), func=AF.Copy, scale=rs[k]),
                 reads=[src, stb[k]], writes=[hb2[k]])

    def nt_back(dst_offs, ncols_dst, scale_cols=None):
        for k in range(2):
            hb = hb2[k]
            for dc in range(8):
                S.op(P, lambda e, dc=dc, hb=hb: e.transpose(ps_T.v([[1, 128]], off=dc * 128),
                                                            hb.v([[1, 128]], off=dc * 128), identb.v()),
                     reads=[hb, identb], writes=[ps_T, ps_Tf])
            dst_off = dst_offs[k]
            if scale_cols is None:
                S.op(A, lambda e, dst_off=dst_off: e.activation(
                    out=hT.v([[ncols_dst, 8], [1, 128]], off=dst_off), in_=ps_T.v([[128, 8], [1, 128]]),
                    func=AF.Copy), reads=[ps_T, ps_Tf], writes=[hT])
            else:
                S.op(V, lambda e, dst_off=dst_off: e.tensor_tensor(
                    out=hT.v([[ncols_dst, 8], [1, 128]], off=dst_off), in0=ps_T.v([[128, 8], [1, 128]]),
                    in1=small.v([[1, 8], [0, 128]], off=scale_cols), op=ALU.mult),
                    reads=[ps_T, ps_Tf, small], writes=[hT])

    def norm_transpose_pair(srcs, dst_offs, ncols_dst, scale_cols=None):
        nt_front(srcs)
        nt_back(dst_offs, ncols_dst, scale_cols)

    def dbg_store(key, row0, buf, ap, nrows):
        if not dbg_d:
            return
        S.op(SY, lambda e: e.dma_start(out=dbg_d[key][row0:row0 + nrows, :], in_=ap), reads=[buf],
             dma="dbg_" + buf.name)

    import os as _os
    dbg_tile = (0, int(_os.environ.get('KDBGT', 1 if NT > 1 else 0)))

    wkv = g["wkv"]
    CHUNKS_B1P = [("pool", gi, 96 * gi, 96) for gi in range(4)]
    CHUNKS_B1S = ([("ssm", ct, 384 + 128 * ct, 128) for ct in range(3)]
                  + [("q", a, 768 + 128 * a, 128) for a in range(2)])
    CHUNKS_B1 = CHUNKS_B1P + CHUNKS_B1S
    CHUNKS_B2 = ([("gate", gi, 1024 + 96 * gi, 96) for gi in range(4)]
                 + [("gate", 4 + k, 1024 + 384 + 128 * k, 128) for k in range(5)])

    def gen_kv(b, pool):
        for mb in range(2):
            xs = xb[mb]
            S.op(SY, lambda e, xs=xs, mb=mb, b=b: e.dma_start(out=xs.v(), in_=mem_d[b, mb * 128:(mb + 1) * 128, :]),
                 writes=[xs], dma=xs.name)
        nt_front(xb)
        yield
        yield
        nt_back([0, 128], 512, scale_cols=8)
        yield
        for a in range(2):
            pr = pool.take()
            for dc in range(8):
                mm(pr.v([[1, 256]]), wkv.v([[1, 128]], off=dc * 512 + a * 128), hT.v([[1, 256]], off=dc * 512),
                   dc == 0, dc == 7, [wkv, hT], pr.bufs)
            S.op(A, lambda e, a=a, pr=pr: e.activation(out=kT.v([[1, 256]], off=a * 256), in_=pr.v([[1, 256]]),
                                                       func=AF.Copy, scale=0.125), reads=pr.bufs, writes=[kT])
            yield
        for mc in range(2):
            pr = pool.take()
            for dc in range(8):
                mm(pr.v([[1, 256]]), hT.v([[1, 128]], off=dc * 512 + mc * 128), wkv.v([[1, 256]], off=dc * 512 + 256),
                   dc == 0, dc == 7, [wkv, hT], pr.bufs)
            S.op(A, lambda e, mc=mc, pr=pr: e.activation(out=Vv.v([[1, 256]], off=mc * 256), in_=pr.v([[1, 256]]),
                                                         func=AF.Copy), reads=pr.bufs, writes=[Vv])
            yield

    def st_kv(b):
        for _ in gen_kv(b, Pool_([R_m[2], R_w[1]])):
            pass

    def gen_boundary(b, i, nxt):
        yield from gen_E(b, i, [R_m[2], R_w[1]])
        yield from gen_kv(nxt[0], Pool_([R_m[2], R_w[1]]))
        yield from gen_A(*nxt)
        st_reset_halo(nxt[0])
        yield from gen_B(*nxt, CHUNKS_B1S + CHUNKS_B1P, Pool_([R_w[0]]))

    def st_reset_state():
        S.op(G_, lambda e: e.memset(Hc.v(), 0.0), writes=[Hc])

    def st_reset_halo(b):
        Up = Up2[par(b, 0)]
        S.op(G_, lambda e: e.memset(Up.v([[528, 4], [1, 16]], np_=96), 0.0), writes=[Up])

    def a_front(b, i, pr):
        t0 = i * T
        for k in range(2):
            xs = xb[k]
            blk = pr * 2 + k
            S.op(SY, lambda e, xs=xs, b=b, r0=t0 + blk * 128: e.dma_start(out=xs.v(), in_=x_d[b, r0:r0 + 128, :]),
                 writes=[xs], dma=xs.name)
        nt_front(xb)

    def gen_A(b, i, skip_front0=False):
        if not skip_front0:
            a_front(b, i, 0)
            yield
        nt_back([0, 128], 512)
        a_front(b, i, 1)
        yield
        if skip_front0:
            yield
            yield
        nt_back([256, 384], 512)
        yield

    def st_A(b, i):
        for _ in gen_A(b, i):
            pass

    def gen_AB(b, i, chunks, pool, skip_front0=False, stride=1):
        yield from gen_A(b, i, skip_front0)
        yield from gen_B(b, i, chunks, pool, stride)

    def gen_B(b, i, chunks, pool, stride=1):
        dbg_on = bool(dbg_d) and (b, i) == dbg_tile
        Up, Uz, qT = Up2[par(b, i)], Uz2[par(b, i)], qT2[par(b, i)]
        for kind, idx, c0, M in chunks:
            pr = pool.take()
            pb = pr
            for dc in range(8):
                mm(pb.v([[1, 512]], np_=M), wi.v([[1, M]], off=dc * 2048 + c0), hT.v([[1, 512]], off=dc * 512),
                   dc == 0, dc == 7, [wi, hT], pr.bufs)
            if dbg_on:
                sd = scratch()
                S.op(V, lambda e, pb=pb, sd=sd, M=M: e.tensor_copy(out=sd.v(np_=M), in_=pb.v(np_=M)),
                     reads=pr.bufs, writes=[sd])
                dbg_store("proj", c0, sd, sd.v(np_=M), M)
            if kind == "pool":
                assert tix[(b, i)] == 0 or (tix[(b, i)] - 1) in cmm_done, "pool chunk before previous Cmm"
                S.op(A, lambda e, pb=pb, idx=idx: e.activation(out=Up.v([[1, 512]], off=idx * 528 + 16, np_=96),
                                                               in_=pb.v(np_=96), func=AF.Copy),
                     reads=pr.bufs, writes=[Up])
            elif kind == "ssm":
                S.op(A, lambda e, pb=pb, idx=idx: e.activation(
                    out=Uz.v([[16, 64], [1, 8]], off=idx * 1024 + 8), in_=pb.v([[8, 64], [1, 8]]), func=AF.Copy),
                    reads=pr.bufs, writes=[Uz])
            elif kind == "q":
                S.op(A, lambda e, pb=pb, idx=idx: e.activation(out=qT.v([[1, 512]], off=idx * 512), in_=pb.v(),
                                                               func=AF.Copy), reads=pr.bufs, writes=[qT])
            else:
                S.op(A, lambda e, pb=pb, idx=idx, M=M: e.activation(
                    out=Gt.v([[1, 512]], off=idx * 512, np_=M), in_=pb.v(np_=M), func=AF.Silu),
                    reads=pr.bufs, writes=[Gt])
            for _ in range(stride):
                yield

    def st_B(b, i, chunks, pool=None):
        for _ in gen_B(b, i, chunks, pool if pool is not None else Pool_(R_w)):
            pass

    def st_Csum(b, i):
        Up = Up2[par(b, i)]
        for gi, w in enumerate(WINDOWS):
            u0 = gi * 528

            def U(c_lo, n, u0=u0):
                return Up.v([[1, n]], off=u0 + c_lo, np_=96)

            def sa(buf, c_lo, n):
                return buf.v([[1, n]], off=c_lo, np_=96)

            outS = Sb.v([[1, 512]], off=gi * 512, np_=96)
            steps = int(math.log2(w))
            prev_buf = None
            for s_i in range(steps):
                sh = 1 << s_i
                last = s_i == steps - 1
                lo = 16 if last else (2 * sh - 1)
                n = 528 - lo
                if s_i == 0:
                    in0, in1, rd = U(lo, n), U(lo - sh, n), [Up]
                else:
                    in0, in1, rd = sa(prev_buf, lo, n), sa(prev_buf, lo - sh, n), [prev_buf]
                if last:
                    tt(outS, in0, in1, ALU.add, rd, [Sb], eng=G_)
                    if i == 0 and w > 1:
                        dst16 = SB if prev_buf is SA else SA
                        if s_i == 0:
                            a0, a1 = U(16, 16), U(16 - sh, 16)
                        else:
                            a0, a1 = sa(prev_buf, 16, 16), sa(prev_buf, 16 - sh, 16)
                        tt(sa(dst16, 0, 16), a0, a1, ALU.add, rd, [dst16], eng=G_)
                        tt(Sb.v([[1, 16]], off=gi * 512, np_=96), sa(dst16, 0, 16),
                           ctab.v([[1, 16]], off=gi * 16, np_=96), ALU.mult, [dst16, ctab], [Sb], eng=G_)
                else:
                    dst = SA if prev_buf is not SA else SB
                    tt(sa(dst, lo, n), in0, in1, ALU.add, rd, [dst], eng=G_)
                    prev_buf = dst

    def gen_Cmm(b, i, cmm_pool):
        dbg_on = bool(dbg_d) and (b, i) == dbg_tile
        Up = Up2[par(b, i)]
        UpN = Up2[1 - par(b, i)]
        for gi, w in enumerate(WINDOWS):
            outS = Sb.v([[1, 512]], off=gi * 512, np_=96)
            pr = cmm_pool.take()
            pb = pr
            mm(pb.v([[1, 512]], np_=96), wp1.v([[1, 96]], off=gi * 96, np_=96), outS, True, False, [wp1, Sb], pr.bufs)
            mm(pb.v([[1, 512]], np_=96), wp2.v([[1, 96]], off=gi * 96, np_=96),
               Up.v([[1, 512]], off=gi * 528 + 16, np_=96), False, True, [wp2, Up], pr.bufs)
            if dbg_on:
                sd = scratch()
                S.op(V, lambda e, pb=pb, sd=sd, gi=gi: e.tensor_scalar(
                    out=sd.v(np_=96), in0=pb.v(np_=96), scalar1=small.v([[1, 1]], off=16 + gi, np_=96),
                    scalar2=None, op0=ALU.mult), reads=pr.bufs + [small], writes=[sd])
                dbg_store("ypool", 96 * gi, sd, sd.v(np_=96), 96)
            gv = Gt.v([[1, 512]], off=gi * 512, np_=96)
            S.op(V, lambda e, pb=pb, gi=gi, gv=gv: e.scalar_tensor_tensor(
                out=gv, in0=pb.v(np_=96), scalar=small.v([[1, 1]], off=16 + gi, np_=96), in1=gv,
                op0=ALU.mult, op1=ALU.mult), reads=pr.bufs + [small, Gt], writes=[Gt])
            if gi == 3:
                if i + 1 < NT:
                    S.op(G_, lambda e: e.tensor_copy(out=UpN.v([[528, 4], [1, 15]], off=1, np_=96),
                                                     in_=Up.v([[528, 4], [1, 15]], off=513, np_=96)),
                         reads=[Up], writes=[UpN] if UpN is not Up else [Up])
                cmm_done.add(tix[(b, i)])
            yield

    xb4 = []

    def st_D1a(b, i):
        Uz = Uz2[par(b, i)]
        S.op(V, lambda e: e.tensor_copy(out=Hs.v([[65, 24], [1, 1]]), in_=Hc.v([[1, 24], [1, 1]])),
             reads=[Hc], writes=[Hs])
        xbanks = [bank("w", force=0), bank("w"), bank("m", force=0), bank("m")]
        xb4[:] = xbanks
        for ct in range(3):
            for part in range(2):
                for j in range(8):
                    for pp in range(4):
                        mm(xbanks[pp].v([[1, 64]], off=(ct * 2 + part) * 64),
                           W1.v([[1, 128]], off=ct * 2048 + (j * 2 + part) * 128, p0=32 * pp, np_=32),
                           Uz.v([[16, 64]], off=ct * 1024 + 8 + j, p0=32 * pp, np_=32),
                           j == 0, j == 7, [W1, Uz], [xbanks[pp]], tp=((96, 0) if pp == 3 else None))

    def st_D1b(b, i):
        xbanks = list(xb4)
        w0 = xbanks[0]
        xre = w0.v([[512, 4], [128, 3], [1, 64]], off=0)
        xim = w0.v([[512, 4], [128, 3], [1, 64]], off=64)
        tabv = lambda buf: buf.v([[64, 4], [256, 3], [1, 64]])
        t3 = lambda buf: buf.v([[192, 4], [64, 3], [1, 64]])
        btv = lambda off: bt.v([[64, 4], [256, 3], [1, 64]], off=off)
        T0 = lambda dims=None: tmpS.v(dims if dims is not None else [[1, 768]], off=0)
        T1 = lambda dims=None: tmpS.v(dims if dims is not None else [[1, 768]], off=768)
        d3 = [[192, 4], [64, 3], [1, 64]]
        tt(T0(d3), xre, tabv(cosT), ALU.mult, xbanks + [cosT], [tmpS])
        tt(T1(d3), xim, tabv(sinT), ALU.mult, xbanks + [sinT], [tmpS])
        tt(btv(0), T0(d3), T1(d3), ALU.add, [tmpS], [bt])
        tt(T0(d3), xim, tabv(cosT), ALU.mult, xbanks + [cosT], [tmpS])
        tt(T1(d3), xre, tabv(sinT), ALU.mult, xbanks + [sinT], [tmpS])
        tt(btv(768), T0(d3), T1(d3), ALU.subtract, [tmpS], [bt])
        tt(hct.v([[12, 2], [1, 12]]), Hc.v([[12, 2], [1, 12]]), rQ.v([[0, 2], [1, 12]]), ALU.mult, [Hc, rQ], [hct])
        tt(bt.v([[64, 24], [1, 1]]), bt.v([[64, 24], [1, 1]]), hct.v([[1, 24], [1, 1]]), ALU.add, [bt, hct], [bt])
        for part in range(2):
            S.op(V, lambda e, part=part: e.tensor_tensor_scan(
                out=bt.v([[1, 768]], off=part * 768), data0=decT.v(), data1=bt.v([[1, 768]], off=part * 768),
                initial=0.0, op0=ALU.mult, op1=ALU.add), reads=[decT, bt], writes=[bt])
        Gs = bt
        gre = Gs.v([[1, 768]], off=0)
        gim = Gs.v([[1, 768]], off=768)
        hv = lambda off: Hs.v([[65, 12], [1, 64]], off=off + 1)
        d12 = [[64, 12], [1, 64]]
        tt(T0(), gre, cosT.v(), ALU.mult, [Gs, cosT], [tmpS])
        tt(T1(), gim, sinT.v(), ALU.mult, [Gs, sinT], [tmpS])
        tt(hv(0), T0(d12), T1(d12), ALU.subtract, [tmpS], [Hs])
        tt(T0(), gre, sinT.v(), ALU.mult, [Gs, sinT], [tmpS])
        tt(T1(), gim, cosT.v(), ALU.mult, [Gs, cosT], [tmpS])
        tt(hv(780), T0(d12), T1(d12), ALU.add, [tmpS], [Hs])

    def st_D1c(b, i):
        Gs = bt
        lastv = lambda buf, off: buf.v([[64, 12], [1, 1]], off=off + 63)
        h4 = [hct.v([[1, 12], [1, 1]], off=12 * k) for k in range(4)]
        tt(h4[0], lastv(Gs, 0), lastv(cosT, 0), ALU.mult, [Gs, cosT], [hct], eng=G_)
        tt(h4[1], lastv(Gs, 768), lastv(sinT, 0), ALU.mult, [Gs, sinT], [hct], eng=G_)
        tt(h4[2], lastv(Gs, 0), lastv(sinT, 0), ALU.mult, [Gs, sinT], [hct], eng=G_)
        tt(h4[3], lastv(Gs, 768), lastv(cosT, 0), ALU.mult, [Gs, cosT], [hct], eng=G_)
        tt(Hc.v([[1, 12], [1, 1]]), h4[0], h4[1], ALU.subtract, [hct], [Hc], eng=G_)
        tt(Hc.v([[1, 12], [1, 1]], off=12), h4[2], h4[3], ALU.add, [hct], [Hc], eng=G_)

    def gen_D2(b, i, d2_idle=2):
        dbg_on = bool(dbg_d) and (b, i) == dbg_tile
        Uz = Uz2[par(b, i)]
        pool = Pool_([R_m[0], R_m[1]])

        def conv(ct, pr):
            for m in range(8):
                mm(pr.v([[8, 64], [1, 8]]), Kc.v([[1, 128]], off=ct * 1024 + m * 128),
                   Uz.v([[16, 64], [1, 8]], off=ct * 1024 + 8 - m), m == 0, m == 7, [Kc, Uz], pr.bufs)

        assert R_po[0].consumed() and R_po[1].consumed()
        conv(0, R_po[0])
        yield
        conv(1, R_po[1])
        yield
        for _ in range(d2_idle):
            yield
        yield from gen_Cmm(b, i, pool)
        for ct in range(3):
            hp_ = 64 * (ct % 2)
            for half in range(2):
                pr = pool.take()
                for q2 in range(2):
                    pp = half * 2 + q2
                    p = ct * 4 + pp
                    mm(pr.v([[1, 256]], off=q2 * 256, p0=hp_, np_=64), Hs.v([[1, 64]], off=p * 65),
                       W3r.v([[1, 256]], off=p * 288 + 32), True, False, [Hs, W3r], pr.bufs)
                    mm(pr.v([[1, 256]], off=q2 * 256, p0=hp_, np_=64), Hs.v([[1, 64]], off=780 + p * 65),
                       W3i.v([[1, 256]], off=p * 288 + 32), False, True, [Hs, W3i], pr.bufs)
                S.op(V, lambda e, pr=pr, half=half, hp_=hp_: e.tensor_copy(
                    out=Yc.v([[32, 2], [128, 8], [1, 32]], off=half * 64, p0=hp_, np_=64),
                    in_=pr.v([[256, 2], [32, 8], [1, 32]], p0=hp_, np_=64)),
                    reads=pr.bufs, writes=[Ycb[ct % 2]])
            yield
            ptr = pool.take()
            for j in range(8):
                S.op(P, lambda e, j=j, ptr=ptr, hp_=hp_: e.transpose(
                    ptr.v([[1, 64]], off=j * 64), Yc.v([[1, 128]], off=j * 128, p0=hp_, np_=64),
                    identf.v([[1, 64]], off=hp_, p0=hp_, np_=64)), reads=[Ycb[ct % 2], identf], writes=ptr.bufs)
            isb = tmpS
            S.op(V, lambda e, ptr=ptr, isb=isb: e.tensor_copy(
                out=isb.v([[1, 8], [8, 64]]), in_=ptr.v([[64, 8], [1, 64]])),
                reads=ptr.bufs, writes=[isb])
            yield
            if ct < 2:
                pcv = R_po[ct]
            else:
                pcv = pool.take()
                conv(2, pcv)
            yl = scratch()
            tt(yl.v(), pcv.v(), isb.v([[1, 512]]), ALU.add, pcv.bufs + [isb], [yl])
            if ct < 2:
                pool.add(R_po[ct])
            if dbg_on:
                dbg_store("ylin", 128 * ct, yl, yl.v(), 128)
            S.op(A, lambda e, ct=ct, yl=yl: e.activation(out=Yg.v([[1, 512]], off=ct * 512), in_=yl.v(),
                                                         func=AF.Gelu_apprx_tanh), reads=[yl], writes=[Yg])
            yield
        for c3 in range(3):
            pz = []
            for oc in (c3, 3 + c3):
                pr = pool.take()
                for ct in range(3):
                    mm(pr.v(), wg.v([[1, 128]], off=ct * 768 + oc * 128), Yg.v([[1, 512]], off=ct * 512),
                       ct == 0, ct == 2, [wg, Yg], pr.bufs)
                pz.append(pr)
            sg = scratch()
            S.op(A, lambda e, sg=sg, pr=pz[1]: e.activation(out=sg.v(), in_=pr.v(), func=AF.Sigmoid),
                 reads=pz[1].bufs, writes=[sg])
            tmp = scratch()
            tt(tmp.v(), pz[0].v(), sg.v(), ALU.mult, pz[0].bufs + [sg], [tmp])
            if dbg_on:
                dbg_store("yssm", 128 * c3, tmp, tmp.v(), 128)
            gv = Gt.v([[1, 512]], off=(4 + c3) * 512)
            tt(gv, tmp.v(), gv, ALU.mult, [tmp, Gt], [Gt], eng=G_)
            yield

    def gen_E(b, i, banks):
        dbg_on = bool(dbg_d) and (b, i) == dbg_tile
        qT = qT2[par(b, i)]
        pool = Pool_(banks)
        for a in range(2):
            for hh in range(2):
                for mc in range(2):
                    pr = pool.take()
                    mm(pr.v(), kT.v([[1, 128]], off=a * 256 + mc * 128, p0=64 * hh, np_=64),
                       qT.v([[1, 512]], off=a * 512, p0=64 * hh, np_=64), True, True, [kT, qT], pr.bufs)
                    S.op(A, lambda e, pr=pr, hh=hh, mc=mc: e.activation(
                        out=PT.v([[1, 512]], off=(hh * 2 + mc) * 512), in_=pr.v(), func=AF.Exp),
                        reads=pr.bufs, writes=[PT])
                    yield
            po_ = pool.take()
            pd_ = pool.take()
            for hh in range(2):
                for mc in range(2):
                    mm(po_.v([[1, 512]], p0=64 * hh, np_=64),
                       Vv.v([[1, 64]], off=mc * 256 + (2 * a + hh) * 64),
                       PT.v([[1, 512]], off=(hh * 2 + mc) * 512), mc == 0, mc == 1, [Vv, PT], po_.bufs)
            for hh in range(2):
                for mc in range(2):
                    mm(pd_.v([[1, 512]], p0=64 * hh, np_=64), ones.v([[1, 64]]),
                       PT.v([[1, 512]], off=(hh * 2 + mc) * 512), mc == 0, mc == 1, [ones, PT], pd_.bufs)
            ld_ = scratch()
            S.op(A, lambda e, ld_=ld_, pd_=pd_: e.activation(out=ld_.v(), in_=pd_.v(), func=AF.Ln),
                 reads=pd_.bufs, writes=[ld_])
            rd_ = ld_
            S.op(A, lambda e, rd_=rd_, ld_=ld_: e.activation(out=rd_.v(), in_=ld_.v(), func=AF.Exp, scale=-1.0),
                 reads=[ld_], writes=[rd_])
            ya = scratch()
            tt(ya.v(), po_.v(), rd_.v(), ALU.mult, po_.bufs + [rd_], [ya])
            if dbg_on:
                dbg_store("yatt", 128 * a, ya, ya.v(), 128)
            gv = Gt.v([[1, 512]], off=(7 + a) * 512)
            tt(gv, ya.v(), gv, ALU.mult, [ya, Gt], [Gt], eng=G_)
            yield

    def merge(*gens):
        gens = list(gens)
        while gens:
            for g_ in list(gens):
                try:
                    next(g_)
                except StopIteration:
                    gens.remove(g_)

    def st_F(b, i):
        t0 = i * T
        rows = [96] * 4 + [128] * 5
        ps_w = g["ps_w"]
        def reload(blk_):
            xs_ = xr[blk_ % 2]
            r0_ = t0 + blk_ * 128
            S.op(SY, lambda e, xs_=xs_, b=b, r0_=r0_: e.dma_start(out=xs_.v(), in_=x_d[b, r0_:r0_ + 128, :]),
                 writes=[xs_], dma=xs_.name)

        reload(0)
        for blk in range(4):
            xs = xr[blk % 2]
            r0 = t0 + blk * 128
            if blk + 1 < 4:
                reload(blk + 1)
            pbufs = ps_po if blk % 2 == 1 else [ps_w[0], ps_w[1]]
            po_ap = pbufs[0].v([[1, 1024]])
            halves = [pbufs[0].v(), pbufs[1].v()]
            hb_ = pbufs
            for hf in range(2):
                for ci, nr in enumerate(rows):
                    mm(halves[hf], Gt.v([[1, 128]], off=ci * 512 + blk * 128, np_=nr),
                       wo.v([[1, 512]], off=ci * 1024 + hf * 512, np_=nr), ci == 0, ci == 8, [Gt, wo], [hb_[hf]])
            k = 2 + (blk % 2)
            r = rms_stats_a(po_ap, pbufs, k, jb=PT)
            rms_stats_b(k)
            S.op(V, lambda e, r=r, po_ap=po_ap: e.scalar_tensor_tensor(
                out=t1f.v(), in0=po_ap, scalar=r, in1=gpost.v(), op0=ALU.mult, op1=ALU.mult),
                reads=list(pbufs) + [stb[k], gpost], writes=[t1f])
            tt(xs.v(), xs.v(), t1f.v(), ALU.add, [xs, t1f], [xs], eng=G_)
            S.op(SY, lambda e, xs=xs, b=b, r0=r0: e.dma_start(out=out_d[b, r0:r0 + 128, :], in_=xs.v()),
                 reads=[xs], dma=xs.name)

    tiles = [(b, i) for b in range(NB) for i in range(NT)]
    st_kv(0)
    st_reset_state()
    st_reset_halo(0)
    st_A(*tiles[0])
    st_B(*tiles[0], CHUNKS_B1)
    for n_, (b, i) in enumerate(tiles):
        nxt = tiles[n_ + 1] if n_ + 1 < len(tiles) else None
        same = nxt is not None and nxt[0] == b
        st_Csum(b, i)
        st_B(b, i, CHUNKS_B2)
        if same:
            a_front(*nxt, 0)
        st_D1a(b, i)
        st_D1b(b, i)
        if same:
            merge(gen_D2(b, i), gen_E(b, i, [R_m[2], R_w[1]]),
                  gen_AB(*nxt, CHUNKS_B1S + CHUNKS_B1P, Pool_([R_w[0]]), skip_front0=True, stride=2))
            st_D1c(b, i)
        elif nxt is not None:
            merge(gen_D2(b, i), gen_boundary(b, i, nxt))
            st_D1c(b, i)
            st_reset_state()
        else:
            merge(gen_D2(b, i), gen_E(b, i, [R_m[2], R_w[1]]))
            st_D1c(b, i)
        st_F(b, i)


def host_layout(inp):
    f = lambda a: np.ascontiguousarray(np.asarray(a, dtype=np.float32))
    m = {}
    m["w_in"] = f(inp["w_in"][0])
    m["w_out"] = f(inp["w_out"][0])
    m["w_glu"] = f(inp["w_glu"][0])
    m["w_kv"] = f(inp["w_kv"][0])
    m["w_pool_t"] = f(np.transpose(inp["w_pool"][0], (1, 0, 2)).reshape(96, 384))
    m["gpre_t"] = f(inp["g_pre"][0].reshape(8, 128).T)
    m["gmem_t"] = f(inp["g_mem"][0].reshape(8, 128).T)
    m["g_post"] = f(inp["g_post"][0].reshape(1, D))
    m["pscale_t"] = f(inp["pool_scale"][0].reshape(4, 96).T)
    m["dskip_t"] = f(inp["d_skip"][0].reshape(3, 128).T)
    ct = np.ones((4, 16), np.float32)
    for gi, w in enumerate(WINDOWS):
        for t in range(16):
            ct[gi, t] = float(w) / float(min(t + 1, w))
    m["pool_ctab"] = f(np.broadcast_to(ct.reshape(1, 64), (96, 64)))
    m["ident"] = f(np.eye(128, dtype=np.float32))
    a_re, a_im, ldt = inp["a_re"][0], inp["a_im"][0], inp["log_dt"][0]
    b_re, b_im, c_re, c_im = inp["b_re"][0], inp["b_im"][0], inp["c_re"][0], inp["c_im"][0]
    sm = lambda a: f(a.reshape(12, 128).T)
    m["sm_are"], m["sm_aim"] = sm(a_re), sm(a_im)
    m["sm_ldt"] = f(np.repeat(ldt.reshape(12, 2), 64, axis=1).T)
    smb = lambda a: f(a.reshape(12, 2, 64, 16).transpose(1, 2, 0, 3).reshape(128, 192))
    m["sm_bre"], m["sm_bim"] = smb(b_re), smb(b_im)
    smc = lambda a: f(a.reshape(12, 2, 16, 64).transpose(1, 3, 0, 2).reshape(128, 192))
    m["sm_cre"], m["sm_cim"] = smc(c_re), smc(c_im)
    def cm_rep(a):
        v = a.reshape(3, 4, 2, 64)
        v = v.transpose(1, 0, 2, 3).reshape(4, 1, 3 * 128)
        return f(np.broadcast_to(v, (4, 32, 384)).reshape(128, 384))
    m["cm_are"], m["cm_aim"] = cm_rep(a_re), cm_rep(a_im)
    m["cm_ldt"] = cm_rep(np.repeat(ldt.reshape(24, 1), 64, axis=1))
    def cm_b(a):
        v = a.reshape(3, 4, 2, 64, 16)
        o = np.zeros((4, 2, 16, 3, 2, 64), np.float32)
        for g2 in range(2):
            o[:, g2, :, :, g2, :] = v[:, :, g2].transpose(1, 3, 0, 2)
        return f(o.reshape(128, 384))
    m["cm_bre"], m["cm_bim"] = cm_b(b_re), cm_b(b_im)
    return m


_NC_CACHE = {}


def kernel(**inputs):
    x = np.asarray(inputs["x"], dtype=np.float32)
    mem = np.asarray(inputs["mem"], dtype=np.float32)
    B, L, _ = x.shape
    NB = B // N_CORES
    key = (NB, L)
    if key not in _NC_CACHE:
        _NC_CACHE[key] = build_nc(NB, L)
    nc = _NC_CACHE[key]
    shared = host_layout(inputs)
    in_maps = []
    for c in range(N_CORES):
        mp = dict(shared)
        mp["x"] = np.ascontiguousarray(x[c * NB:(c + 1) * NB])
        mp["mem"] = np.ascontiguousarray(mem[c * NB:(c + 1) * NB])
        in_maps.append(mp)
    res = run_bass_kernel_spmd(nc, in_maps, core_ids=list(range(N_CORES)))
    return np.concatenate([np.asarray(r["out"], dtype=np.float32) for r in res.results], axis=0)
```

```python
import contextlib
import math

import numpy as np

import concourse.bass as bass
import concourse.mybir as mybir
from concourse.bass_utils import run_bass_kernel_spmd

F32 = mybir.dt.float32
BF16 = mybir.dt.bfloat16
AF = mybir.ActivationFunctionType
ALU = mybir.AluOpType

N_CORES = 8
D = 1024
POOL_W = 384
SSM_W = 384
ATT_W = 256
POOL_GW = 96
WINDOWS = (2, 4, 8, 16)
N_MEM = 256
HD = 64
EPS = 1e-6
Q = 8
T = 512
NCH = T // Q


class Buf:
    def __init__(self, name, base):
        self.name = name
        self.base = base
        self.tensor = base.tensor
        self.off0 = base.offset
        self.pstep = base.ap[0][0]
        self.n = base.ap[-1][1]
        self.last_w = None
        self.readers = []

    def v(self, dims=None, off=0, p0=0, np_=128):
        if dims is None:
            dims = [[1, self.n]]
        return bass.AP(self.tensor, self.off0 + p0 * self.pstep + off,
                       [[self.pstep, np_]] + [list(d) for d in dims])


class Sched:
    SEM_LIMIT = 30000

    def __init__(self, nc, es):
        self.nc = nc
        self.es = es
        self.ops = []
        self.sem_cache = {}

    def buf(self, name, n, dtype, psum=False):
        if psum:
            t = self.es.enter_context(self.nc.psum_tensor(name, [128, n], dtype))
        else:
            t = self.es.enter_context(self.nc.sbuf_tensor(name, [128, n], dtype))
        return Buf(name, t[:])

    def arena(self, n_f32):
        self.arena_t = self.es.enter_context(self.nc.sbuf_tensor("arena", [128, n_f32], F32))
        self.arena_n = n_f32
        self.arena_pos = 0
        self.arena_max = 0

    def arena_reset(self):
        self.phase_max = getattr(self, "phase_max", []) + [self.arena_pos]
        self.arena_pos = 0

    def carve(self, name, n, dtype):
        w = n if dtype == F32 else (n + 1) // 2
        w = (w + 1) // 2 * 2
        c0 = self.arena_pos
        assert c0 + w <= self.arena_n, f"arena overflow at {name}: {c0 + w} > {self.arena_n}"
        self.arena_pos += w
        self.arena_max = max(self.arena_max, self.arena_pos)
        base = self.arena_t[:, c0:c0 + w]
        if dtype != F32:
            base = base.bitcast(dtype)
        return Buf(name, base)

    def barrier(self, exclude=()):
        last = {}
        for i, o in enumerate(self.ops):
            key = ("dma", o["dma"]) if o["dma"] is not None else ("eng", o["eng"])
            if o["dma"] is not None and o["dma"] in exclude:
                continue
            last[key] = i
        deps = set(last.values())
        for eng in ("tensor", "vector", "scalar", "gpsimd", "sync"):
            idx = len(self.ops)
            self.ops.append(dict(eng=eng, fn=(lambda e: e.nop()), deps=set(deps), dma=None))

    def _deps(self, eng, reads, writes):
        deps = set()
        for b in reads:
            if b.last_w is not None:
                deps.add(b.last_w)
        for b in writes:
            if b.last_w is not None:
                deps.add(b.last_w)
            for r in b.readers:
                deps.add(r)
        return deps

    def op(self, eng, fn, reads=(), writes=(), dma=None):
        cap = getattr(self, "_cap", None)
        if cap is not None:
            cap.append((eng, fn, tuple(reads), tuple(writes), dma))
            return -1
        idx = len(self.ops)
        deps = self._deps(eng, reads, writes)
        self.ops.append(dict(eng=eng, fn=fn, deps=deps, dma=dma))
        for b in reads:
            b.readers.append(idx)
        for b in writes:
            b.last_w = idx
            b.readers = []
        return idx

    def capture(self, f):
        self._cap = []
        try:
            f()
            return self._cap
        finally:
            self._cap = None

    def replay_interleaved(self, *lists):
        its = [iter(l) for l in lists]
        while its:
            for it in list(its):
                try:
                    self.op(*next(it))
                except StopIteration:
                    its.remove(it)

    def mark(self, name):
        self.marks = getattr(self, "marks", {})
        if name not in self.marks:
            self.marks[name] = len(self.ops)

    def _sem(self, key):
        if key not in self.sem_cache:
            name = "s_" + "_".join(str(k) for k in key)
            self.sem_cache[key] = self.es.enter_context(self.nc.semaphore(name))
        return self.sem_cache[key]

    def lower(self):
        ops = self.ops
        needed = set()
        for o in ops:
            needed |= o["deps"]
        prog = {e: [] for e in ("tensor", "vector", "scalar", "gpsimd", "sync")}
        cnt = {}
        waited = {}
        for i, o in enumerate(ops):
            eng = o["eng"]
            if o["dma"] is not None:
                stream = ("dma", o["dma"])
                inc = 16
                sig = True
            else:
                stream = ("eng", eng)
                inc = 1
                sig = i in needed
            wl = {}
            for d in o["deps"]:
                od = ops[d]
                ds = od["stream"]
                if ds == stream and eng == "tensor" and od["dma"] is None:
                    continue
                v = od["sig"]
                if ds not in wl or wl[ds] < v:
                    wl[ds] = v
            for ds, v in wl.items():
                w = waited.get((eng, ds))
                if w is not None and w >= v:
                    continue
                waited[(eng, ds)] = v
                sem = self._sem(ds + (v[0],))
                prog[eng].append(("wait", sem, v[1]))
            if sig:
                ep, c = cnt.get(stream, (0, 0))
                if c + inc > self.SEM_LIMIT:
                    ep, c = ep + 1, 0
                c += inc
                cnt[stream] = (ep, c)
                o["sig"] = (ep, c)
                sem = self._sem(stream + (ep,))
                prog[eng].append(("op", o["fn"], sem, inc))
            else:
                o["sig"] = None
                prog[eng].append(("op", o["fn"], None, 0))
            o["stream"] = stream
        self.prog = prog
        return prog

    def emit(self):
        import os
        stop = os.environ.get("KSTOP")
        if stop and stop in getattr(self, "marks", {}):
            self.ops = self.ops[:self.marks[stop]]
        last = {}
        for i, o in enumerate(self.ops):
            key = ("dma", o["dma"]) if o["dma"] is not None else ("eng", o["eng"])
            last[key] = i
        self.ops.append(dict(eng="sync", fn=(lambda e: e.nop()), deps=set(last.values()), dma=None))
        prog = self.lower()
        nc = self.nc

        def run(e, name):
            for it in prog[name]:
                if it[0] == "wait":
                    e.wait_ge(it[1], it[2])
                else:
                    ins = it[1](e)
                    if it[2] is not None:
                        ins.then_inc(it[2], it[3])

        with nc.Block() as block:
            @block.sync
            def _(e):
                run(e, "sync")

            @block.scalar
            def _(e):
                run(e, "scalar")

            @block.vector
            def _(e):
                run(e, "vector")

            @block.gpsimd
            def _(e):
                run(e, "gpsimd")

            @block.tensor
            def _(e):
                run(e, "tensor")


def build_nc(NB, L, dbg=False):
    nc = bass.Bass("TRN2", target_bir_lowering=False)
    NT = L // T

    def din(name, shape):
        return nc.dram_tensor(name, list(shape), F32, kind="ExternalInput").ap()

    x_d = din("x", [NB, L, D])
    mem_d = din("mem", [NB, N_MEM, D])
    w_in_d = din("w_in", [D, 2 * D])
    w_out_d = din("w_out", [D, D])
    w_glu_d = din("w_glu", [SSM_W, 2 * SSM_W])
    w_kv_d = din("w_kv", [D, 2 * ATT_W])
    w_pool_d = din("w_pool_t", [96, 4 * 96])
    gpre_d = din("gpre_t", [128, 8])
    gmem_d = din("gmem_t", [128, 8])
    gpost_d = din("g_post", [1, D])
    pscale_d = din("pscale_t", [96, 4])
    dskip_d = din("dskip_t", [128, 3])
    ctab_d = din("pool_ctab", [96, 64])
    ident_d = din("ident", [128, 128])
    sm_d = {k: din("sm_" + k, [128, 12]) for k in ("are", "aim", "ldt")}
    for k in ("bre", "bim", "cre", "cim"):
        sm_d[k] = din("sm_" + k, [128, 192])
    cm_d = {k: din("cm_" + k, [128, 384]) for k in ("are", "aim", "ldt", "bre", "bim")}
    out_d = nc.dram_tensor("out", [NB, L, D], F32, kind="ExternalOutput").ap()
    dbg_d = {}
    if dbg:
        for k, n in (("proj", 2048), ("ypool", 384), ("yssm", 384), ("yatt", 256), ("ylin", 384)):
            dbg_d[k] = nc.dram_tensor("dbg_" + k, [n, T], F32, kind="ExternalOutput").ap()

    with contextlib.ExitStack() as es:
        S = Sched(nc, es)
        V, A, P, G_, SY = "vector", "scalar", "tensor", "gpsimd", "sync"

        wi = S.buf("wi", 8 * 2048, BF16)
        wo = S.buf("wo", 9 * 1024, BF16)
        wg = S.buf("wg", 3 * 768, BF16)
        wkv = S.buf("wkv", 8 * 512, BF16)
        wp1 = S.buf("wp1", 4 * 96, BF16)
        wp2 = S.buf("wp2", 4 * 96, BF16)
        W1 = S.buf("W1", 3 * 8 * 2 * 128, BF16)
        W3r = S.buf("W3r", 12 * 9 * 32, BF16)
        W3i = S.buf("W3i", 12 * 9 * 32, BF16)
        Kc = S.buf("Kc", 3 * 8 * 128, BF16)
        cosT = S.buf("cosT", 12 * 64, F32)
        sinT = S.buf("sinT", 12 * 64, F32)
        decT = S.buf("decT", 12 * 64, F32)
        rQ = S.buf("rQ", 12, F32)
        gpost = S.buf("gpost", 1024, F32)
        identf = S.buf("identf", 128, F32)
        identb = S.buf("identb", 128, BF16)
        ones = S.buf("ones", 64, BF16)
        small = S.buf("small", 64, F32)
        ctab = S.buf("ctab", 64, F32)
        _ps = es.enter_context(nc.psum_tensor("ps_all", [128, 4096], F32))
        ps_po = [Buf("ps_po0", _ps[:, 0:512]), Buf("ps_po1", _ps[:, 512:1024])]
        ps_T = Buf("ps_T", _ps[:, 1024:1536].bitcast(BF16))
        ps_w = [Buf(f"ps_w{i}", _ps[:, 1536 + 512 * i:2048 + 512 * i]) for i in range(2)]
        ps_m = [Buf(f"ps_m{i}", _ps[:, 2560 + 512 * i:3072 + 512 * i]) for i in range(3)]
        ps_Tf = Buf("ps_Tf", _ps[:, 1024:1536])
        rr = {"w": 0, "m": 0}

        def bank(kind, force=None):
            lst = ps_w if kind == "w" else ps_m
            if force is not None:
                rr[kind] = force
            b = lst[rr[kind] % len(lst)]
            rr[kind] += 1
            assert b.last_w is None or len(b.readers) > 0, f"PSUM bank {b.name} reused before consumed"
            return b

        S.arena(int(nc.sbuf_bytes_remaining) // 4 - 256)

        dq = [SY, A]

        def load(buf, dram_ap, np_=128, dims=None, q=SY, off=0):
            S.op(q, lambda e: e.dma_start(out=buf.v(dims, off=off, np_=np_), in_=dram_ap),
                 writes=[buf], dma=buf.name)

        load(small, gpre_d, dims=[[1, 8]], off=0)
        load(small, gmem_d, dims=[[1, 8]], off=8)
        load(small, pscale_d, np_=96, dims=[[1, 4]], off=16)
        load(small, dskip_d, dims=[[1, 3]], off=20)
        S.op(V, lambda e: e.memset(small.v([[1, 1]], off=23), -0.5), writes=[small])
        load(ctab, ctab_d, np_=96)
        load(identf, ident_d)
        load(gpost, gpost_d.partition_broadcast(128))
        S.op(V, lambda e: e.tensor_copy(out=identb.v(), in_=identf.v()), reads=[identf], writes=[identb])
        S.op(V, lambda e: e.memset(ones.v(), 1.0), writes=[ones])


        def tt(out, a, b, op, reads, writes, eng=V):
            S.op(eng, lambda e: e.tensor_tensor(out=out, in0=a, in1=b, op=op), reads=reads, writes=writes)

        def lam_tables(pref, src, F, emax, EN):
            are = S.carve(pref + "are", F, F32)
            aim = S.carve(pref + "aim", F, F32)
            ldt = S.carve(pref + "ldt", F, F32)
            load(are, src["are"])
            load(aim, src["aim"])
            load(ldt, src["ldt"])
            def tt(out, a_, b_, op, reads, writes):
                S.op(EN, lambda e: e.tensor_tensor(out=out, in0=a_, in1=b_, op=op), reads=reads, writes=writes)

            dt = S.carve(pref + "dt", F, F32)
            ar = S.carve(pref + "ar", F, F32)
            ai = S.carve(pref + "ai", F, F32)
            c = S.carve(pref + "c", F, F32)
            s_ = S.carve(pref + "s", F, F32)
            t1 = S.carve(pref + "t1", F, F32)
            t2 = S.carve(pref + "t2", F, F32)
            mag = S.carve(pref + "mag", F, F32)
            hp = S.carve(pref + "hp", 2, F32)
            S.op(EN, lambda e: e.memset(hp.v([[1, 1]]), math.pi / 2), writes=[hp])
            S.op(A, lambda e: e.activation(out=dt.v(), in_=ldt.v(), func=AF.Exp), reads=[ldt], writes=[dt])
            tt(ar.v(), are.v(), dt.v(), ALU.mult, [are, dt], [ar])
            tt(ai.v(), aim.v(), dt.v(), ALU.mult, [aim, dt], [ai])
            S.op(A, lambda e: e.activation(out=mag.v(), in_=ar.v(), func=AF.Exp), reads=[ar], writes=[mag])
            S.op(A, lambda e: e.activation(out=s_.v(), in_=ai.v(), func=AF.Sin, scale=1.0 / 8),
                 reads=[ai], writes=[s_])
            S.op(A, lambda e: e.activation(out=t1.v(), in_=ai.v(), func=AF.Sin, scale=1.0 / 16),
                 reads=[ai], writes=[t1])
            tt(t2.v(), t1.v(), t1.v(), ALU.mult, [t1], [t2])
            S.op(EN, lambda e: e.tensor_scalar(out=c.v(), in0=t2.v(), scalar1=-2.0, scalar2=1.0,
                                               op0=ALU.mult, op1=ALU.add), reads=[t2], writes=[c])

            def square(cr, ci_):
                tt(t1.v(), cr.v(), cr.v(), ALU.mult, [cr], [t1])
                tt(t2.v(), ci_.v(), ci_.v(), ALU.mult, [ci_], [t2])
                tt(ci_.v(), cr.v(), ci_.v(), ALU.mult, [cr, ci_], [ci_])
                S.op(EN, lambda e: e.tensor_scalar(out=ci_.v(), in0=ci_.v(), scalar1=2.0, scalar2=None,
                                                   op0=ALU.mult), reads=[ci_], writes=[ci_])
                tt(cr.v(), t1.v(), t2.v(), ALU.subtract, [t1, t2], [cr])

            for _ in range(3):
                square(c, s_)
            Lr = S.carve(pref + "Lr", (emax + 1) * F, F32)
            Li = S.carve(pref + "Li", (emax + 1) * F, F32)
            S.op(EN, lambda e: e.memset(Lr.v([[1, F]]), 1.0), writes=[Lr])
            S.op(EN, lambda e: e.memset(Li.v([[1, F]]), 0.0), writes=[Li])
            tt(Lr.v([[1, F]], off=F), mag.v(), c.v(), ALU.mult, [mag, c], [Lr])
            tt(Li.v([[1, F]], off=F), mag.v(), s_.v(), ALU.mult, [mag, s_], [Li])

            def cmul(outr, outi, ar_, ai_, br_, bi_, rd, wr):
                tt(t1.v(), ar_, br_, ALU.mult, rd, [t1])
                tt(t2.v(), ai_, bi_, ALU.mult, rd, [t2])
                tt(outr, t1.v(), t2.v(), ALU.subtract, [t1, t2], wr)
                tt(t1.v(), ar_, bi_, ALU.mult, rd, [t1])
                tt(t2.v(), ai_, br_, ALU.mult, rd, [t2])
                tt(outi, t1.v(), t2.v(), ALU.add, [t1, t2], wr)

            for e_ in range(2, emax + 1):
                cmul(Lr.v([[1, F]], off=e_ * F), Li.v([[1, F]], off=e_ * F),
                     Lr.v([[1, F]], off=(e_ - 1) * F), Li.v([[1, F]], off=(e_ - 1) * F),
                     Lr.v([[1, F]], off=F), Li.v([[1, F]], off=F), [Lr, Li], [Lr, Li])
            cr = S.carve(pref + "cr", F, F32)
            ci_ = S.carve(pref + "ci", F, F32)
            den = dt
            lm1 = ai
            S.op(EN, lambda e: e.tensor_scalar(out=lm1.v(), in0=Lr.v([[1, F]], off=F), scalar1=-1.0,
                                               scalar2=None, op0=ALU.add), reads=[Lr], writes=[lm1])
            tt(t1.v(), are.v(), are.v(), ALU.mult, [are], [t1])
            tt(t2.v(), aim.v(), aim.v(), ALU.mult, [aim], [t2])
            tt(den.v(), t1.v(), t2.v(), ALU.add, [t1, t2], [den])
            S.op(V, lambda e: e.reciprocal(out=den.v(), in_=den.v()), reads=[den], writes=[den])
            L1i = Li.v([[1, F]], off=F)
            tt(t1.v(), lm1.v(), are.v(), ALU.mult, [lm1, are], [t1])
            tt(t2.v(), L1i, aim.v(), ALU.mult, [Li, aim], [t2])
            tt(cr.v(), t1.v(), t2.v(), ALU.add, [t1, t2], [cr])
            tt(cr.v(), cr.v(), den.v(), ALU.mult, [cr, den], [cr])
            tt(t1.v(), L1i, are.v(), ALU.mult, [Li, are], [t1])
            tt(t2.v(), lm1.v(), aim.v(), ALU.mult, [lm1, aim], [t2])
            tt(ci_.v(), t1.v(), t2.v(), ALU.subtract, [t1, t2], [ci_])
            tt(ci_.v(), ci_.v(), den.v(), ALU.mult, [ci_, den], [ci_])
            return dict(Lr=Lr, Li=Li, ar=ar, cr=cr, ci=ci_, ur=c, ui=s_, t1=t1, t2=t2, cmul=cmul,
                        square=square)

        def part_c():
            F = 384
            cm = lam_tables("cm_", cm_d, F, 1, V)
            bre = S.carve("cm_bre", F, F32)
            bim = S.carve("cm_bim", F, F32)
            load(bre, cm_d["bre"])
            load(bim, cm_d["bim"])
            cur = [S.carve(f"cm_cur{i}", 2 * F, F32) for i in range(2)]
            cm["cmul"](cur[0].v([[1, F]]), cur[0].v([[1, F]], off=F), cm["cr"].v(), cm["ci"].v(), bre.v(), bim.v(),
                       [cm["cr"], cm["ci"], bre, bim], [cur[0]])
            for j in range(7, -1, -1):
                k_ = (7 - j) % 2
                cb = cur[k_]
                for part in range(2):
                    S.op(V, lambda e, j=j, part=part, cb=cb: e.tensor_copy(
                        out=W1.v([[8 * 2 * 128, 3], [1, 128]], off=(j * 2 + part) * 128),
                        in_=cb.v([[128, 3], [1, 128]], off=part * 384)), reads=[cb], writes=[W1])
                if j > 0:
                    nb_ = cur[1 - k_]
                    cm["cmul"](nb_.v([[1, F]]), nb_.v([[1, F]], off=F), cb.v([[1, F]]), cb.v([[1, F]], off=F),
                               cm["Lr"].v([[1, F]], off=F), cm["Li"].v([[1, F]], off=F),
                               [cb, cm["Lr"], cm["Li"]], [nb_])


        def part_s():
            sm = lam_tables("sm_", sm_d, 12, 8, V)
            F = 12
            sb = {}
            for k in ("bre", "bim", "cre", "cim"):
                sb[k] = S.carve("sm_" + k, 192, F32)
                load(sb[k], sm_d[k])
            t1w = S.carve("sm_t1w", 192, F32)
            t2w = S.carve("sm_t2w", 192, F32)

            def bc16(buf, off=0):
                return buf.v([[1, 12], [0, 16]], off=off)

            def w16(buf):
                return buf.v([[16, 12], [1, 16]])

            sbbr = S.carve("sm_bbr", 192, F32)
            sbbi = S.carve("sm_bbi", 192, F32)
            tt(w16(t1w), bc16(sm["cr"]), w16(sb["bre"]), ALU.mult, [sm["cr"], sb["bre"]], [t1w])
            tt(w16(t2w), bc16(sm["ci"]), w16(sb["bim"]), ALU.mult, [sm["ci"], sb["bim"]], [t2w])
            tt(sbbr.v(), t1w.v(), t2w.v(), ALU.subtract, [t1w, t2w], [sbbr])
            tt(w16(t1w), bc16(sm["cr"]), w16(sb["bim"]), ALU.mult, [sm["cr"], sb["bim"]], [t1w])
            tt(w16(t2w), bc16(sm["ci"]), w16(sb["bre"]), ALU.mult, [sm["ci"], sb["bre"]], [t2w])
            tt(sbbi.v(), t1w.v(), t2w.v(), ALU.add, [t1w, t2w], [sbbi])
            Bpr = S.carve("Bpr", 12 * 128, F32)
            Bpi = S.carve("Bpi", 12 * 128, F32)
            for bp, srcb in ((Bpr, sbbr), (Bpi, sbbi)):
                S.op(A, lambda e, bp=bp: e.memzero(bp.v()), writes=[bp])
                for pp in range(4):
                    for h in range(2):
                        S.op(V, lambda e, bp=bp, srcb=srcb, pp=pp, h=h: e.tensor_copy(
                            out=bp.v([[4 * 128, 3], [1, 16]], off=pp * 128 + 32 * pp + 16 * h, p0=64 * h, np_=64),
                            in_=srcb.v([[4 * 16, 3], [1, 16]], off=pp * 16, p0=64 * h, np_=64)),
                            reads=[srcb], writes=[bp])
            CLr = S.carve("CLr", 12 * 9 * 32, F32)
            CLi = S.carve("CLi", 12 * 9 * 32, F32)
            S.op(A, lambda e: e.memzero(CLr.v()), writes=[CLr])
            S.op(A, lambda e: e.memzero(CLi.v()), writes=[CLi])
            ta = S.carve("sm_ta", 192, F32)
            tb = S.carve("sm_tb", 192, F32)
            for e_ in range(9):
                lr = bc16(sm["Lr"], off=e_ * 12)
                li = bc16(sm["Li"], off=e_ * 12)
                tt(w16(t1w), lr, w16(sb["cre"]), ALU.mult, [sm["Lr"], sb["cre"]], [t1w])
                tt(w16(t2w), li, w16(sb["cim"]), ALU.mult, [sm["Li"], sb["cim"]], [t2w])
                tt(w16(ta), li, w16(sb["cre"]), ALU.mult, [sm["Li"], sb["cre"]], [ta])
                tt(w16(tb), lr, w16(sb["cim"]), ALU.mult, [sm["Lr"], sb["cim"]], [tb])
                for h in range(2):
                    dst = dict(dims=[[9 * 32, 12], [1, 16]], off=e_ * 32 + 16 * h, p0=64 * h, np_=64)
                    srcv = dict(dims=[[16, 12], [1, 16]], p0=64 * h, np_=64)
                    S.op(V, lambda e, dst=dst, srcv=srcv: e.tensor_tensor(
                        out=CLr.v(**dst), in0=t1w.v(**srcv), in1=t2w.v(**srcv), op=ALU.subtract),
                        reads=[t1w, t2w], writes=[CLr])
                    S.op(V, lambda e, dst=dst, srcv=srcv: e.scalar_tensor_tensor(
                        out=CLi.v(**dst), in0=ta.v(**srcv), scalar=-1.0, in1=tb.v(**srcv),
                        op0=ALU.mult, op1=ALU.subtract), reads=[ta, tb], writes=[CLi])
            S.op(A, lambda e: e.activation(out=W3r.v(), in_=CLr.v(), func=AF.Copy), reads=[CLr], writes=[W3r])
            S.op(A, lambda e: e.activation(out=W3i.v(), in_=CLi.v(), func=AF.Copy), reads=[CLi], writes=[W3i])
            for p in range(12):
                ct, pp = divmod(p, 4)
                pb = bank("m")
                S.op(P, lambda e, p=p, pb=pb: e.matmul(
                    pb.v([[1, 256]]), lhsT=Bpr.v([[1, 128]], off=p * 128),
                    rhs=CLr.v([[1, 256]], off=p * 9 * 32), start=True, stop=False),
                    reads=[Bpr, CLr], writes=[pb])
                S.op(P, lambda e, p=p, pb=pb: e.matmul(
                    pb.v([[1, 256]]), lhsT=Bpi.v([[1, 128]], off=p * 128),
                    rhs=CLi.v([[1, 256]], off=p * 9 * 32), start=False, stop=True),
                    reads=[Bpi, CLi], writes=[pb])
                S.op(V, lambda e, ct=ct, pp=pp, pb=pb: e.tensor_copy(
                    out=Kc.v([[128, 8], [1, 32]], off=ct * 1024 + 32 * pp), in_=pb.v([[32, 8], [1, 32]])),
                    reads=[pb], writes=[Kc])
            for ct in range(3):
                S.op(V, lambda e, ct=ct: e.scalar_tensor_tensor(
                    out=Kc.v([[1, 128]], off=ct * 1024), in0=identf.v(), scalar=small.v([[1, 1]], off=20 + ct),
                    in1=Kc.v([[1, 128]], off=ct * 1024), op0=ALU.mult, op1=ALU.add),
                    reads=[identf, small, Kc], writes=[Kc])
            S.op(A, lambda e: e.activation(out=rQ.v(), in_=sm["ar"].v(), func=AF.Exp, scale=float(Q)),
                 reads=[sm["ar"]], writes=[rQ])
            for _ in range(3):
                sm["square"](sm["ur"], sm["ui"])
            pwr = S.carve("pwr", 12, F32)
            pwi = S.carve("pwi", 12, F32)
            S.op(V, lambda e: e.tensor_copy(out=pwr.v(), in_=sm["ur"].v()), reads=[sm["ur"]], writes=[pwr])
            S.op(V, lambda e: e.tensor_copy(out=pwi.v(), in_=sm["ui"].v()), reads=[sm["ui"]], writes=[pwi])
            S.op(V, lambda e: e.tensor_copy(out=cosT.v([[64, 12], [1, 1]]), in_=pwr.v([[1, 12], [1, 1]])),
                 reads=[pwr], writes=[cosT])
            S.op(V, lambda e: e.tensor_copy(out=sinT.v([[64, 12], [1, 1]]), in_=pwi.v([[1, 12], [1, 1]])),
                 reads=[pwi], writes=[sinT])
            tmr = S.carve("tmr", 12 * 32, F32)
            tmi = S.carve("tmi", 12 * 32, F32)
            n = 1
            while n < 64:
                def tv(buf, off, cnt, n=n):
                    return buf.v([[64, 12], [1, cnt]], off=off)

                def pv(buf, cnt):
                    return buf.v([[1, 12], [0, cnt]])

                def tm(buf, cnt):
                    return buf.v([[32, 12], [1, cnt]])
                tt(tm(tmr, n), tv(cosT, 0, n), pv(pwr, n), ALU.mult, [cosT, pwr], [tmr])
                tt(tm(tmi, n), tv(sinT, 0, n), pv(pwi, n), ALU.mult, [sinT, pwi], [tmi])
                tt(tv(cosT, n, n), tm(tmr, n), tm(tmi, n), ALU.subtract, [tmr, tmi], [cosT])
                tt(tm(tmr, n), tv(cosT, 0, n), pv(pwi, n), ALU.mult, [cosT, pwi], [tmr])
                tt(tm(tmi, n), tv(sinT, 0, n), pv(pwr, n), ALU.mult, [sinT, pwr], [tmi])
                tt(tv(sinT, n, n), tm(tmr, n), tm(tmi, n), ALU.add, [tmr, tmi], [sinT])
                sm["square"](pwr, pwi)
                n *= 2
            S.op(V, lambda e: e.tensor_copy(out=decT.v([[64, 12], [1, 64]]),
                                            in_=rQ.v([[1, 12], [0, 64]])), reads=[rQ], writes=[decT])
            S.op(V, lambda e: e.memset(decT.v([[64, 12], [1, 1]]), 0.0), writes=[decT])


        ops_c = S.capture(part_c)
        ops_s = S.capture(part_s)

        def split_head(ops_):
            n_act = 0
            for k_, o_ in enumerate(ops_):
                if o_[0] == A:
                    n_act += 1
                    if n_act == 4:
                        return ops_[:k_ + 1], ops_[k_ + 1:]
            return ops_, []

        hc_, tc_ = split_head(ops_c)
        hs_, ts_ = split_head(ops_s)
        hc_ = hc_ + [o_ for o_ in tc_ if o_[4] is not None and o_[0] == SY]
        tc_ = [o_ for o_ in tc_ if not (o_[4] is not None and o_[0] == SY)]
        hs_ = hs_ + [o_ for o_ in ts_ if o_[4] is not None and o_[0] == SY]
        ts_ = [o_ for o_ in ts_ if not (o_[4] is not None and o_[0] == SY)]
        hs_ = hs_ + [o_ for o_ in ts_ if o_[0] == A and o_[4] is None and len(o_[2]) == 0]
        ts_ = [o_ for o_ in ts_ if not (o_[0] == A and o_[4] is None and len(o_[2]) == 0)]
        S.replay_interleaved(hc_, hs_)
        wps = S.carve("wps", 384, F32)
        load(wps, w_pool_d, np_=96)
        for gi, w in enumerate(WINDOWS):
            S.op(V, lambda e, gi=gi, w=w: e.tensor_scalar(
                out=wp1.v([[1, 96]], off=gi * 96, np_=96), in0=wps.v([[1, 96]], off=gi * 96, np_=96),
                scalar1=1.0 / w, scalar2=None, op0=ALU.mult), reads=[wps], writes=[wp1])
            S.op(V, lambda e, gi=gi: e.tensor_scalar(
                out=wp2.v([[1, 96]], off=gi * 96, np_=96), in0=wps.v([[1, 96]], off=gi * 96, np_=96),
                scalar1=-1.0, scalar2=None, op0=ALU.mult), reads=[wps], writes=[wp2])

        stg = [S.carve(f"stg{i}", 1024, F32) for i in range(2)]
        for dc in range(8):
            for hf in range(2):
                k_ = (dc * 2 + hf) % 2
                st = stg[k_]
                load(st, w_in_d[dc * 128:(dc + 1) * 128, hf * 1024:(hf + 1) * 1024], q=dq[k_])
                S.op(A, lambda e, st=st, dc=dc, hf=hf: e.activation(
                    out=wi.v([[1, 1024]], off=dc * 2048 + hf * 1024), in_=st.v(), func=AF.Copy,
                    scale=small.v([[1, 1]], off=dc)), reads=[st, small], writes=[wi])
        rows = [96] * 4 + [128] * 5
        r0 = 0
        for ci, nr in enumerate(rows):
            S.op(G_, lambda e, ci=ci, nr=nr, r0=r0: e.dma_start(
                out=wo.v([[1, 1024]], off=ci * 1024, np_=nr), in_=w_out_d[r0:r0 + nr, :]),
                writes=[wo], dma="wo")
            r0 += nr
        for ct in range(3):
            S.op(G_, lambda e, ct=ct: e.dma_start(
                out=wg.v([[1, 768]], off=ct * 768), in_=w_glu_d[ct * 128:(ct + 1) * 128, :]),
                writes=[wg], dma="wg")
        for dc in range(8):
            S.op(G_, lambda e, dc=dc: e.dma_start(
                out=wkv.v([[1, 512]], off=dc * 512), in_=w_kv_d[dc * 128:(dc + 1) * 128, :]),
                writes=[wkv], dma="wkv")
        S.replay_interleaved(tc_, ts_)
        S.mark('t2')
        S.barrier(exclude=('wo', 'wg', 'wkv'))
        S.arena_reset()
        env = dict(locals())
        build_main(nc, S, env)
        S.emit()
    return nc


def build_main(nc, S, g):
    V, A, P, G_, SY = "vector", "scalar", "tensor", "gpsimd", "sync"
    NB, L, NT = g["NB"], g["L"], g["NT"]
    x_d, mem_d, out_d, w_kv_d, dbg_d = g["x_d"], g["mem_d"], g["out_d"], g["w_kv_d"], g["dbg_d"]
    wi, wo, wg, wp1, wp2 = g["wi"], g["wo"], g["wg"], g["wp1"], g["wp2"]
    W1, W3r, W3i, Kc = g["W1"], g["W3r"], g["W3i"], g["Kc"]
    cosT, sinT, decT, rQ = g["cosT"], g["sinT"], g["decT"], g["rQ"]
    gpost, identf, identb, ones, small, ctab = (g["gpost"], g["identf"], g["identb"], g["ones"],
                                                g["small"], g["ctab"])
    ps_po, ps_T, bank = g["ps_po"], g["ps_T"], g["bank"]

    def tt(out, a, b, op, reads, writes, eng=V):
        S.op(eng, lambda e: e.tensor_tensor(out=out, in0=a, in1=b, op=op), reads=reads, writes=writes)

    class BankRef:
        def __init__(self, main, bufs=None):
            self.main = main
            self.bufs = bufs if bufs is not None else [main]

        def v(self, *a, **k):
            return self.main.v(*a, **k)

        def consumed(self):
            return all(b_.last_w is None or len(b_.readers) > 0 for b_ in self.bufs)

    class Pool_:
        def __init__(self, refs):
            self.refs = list(refs)
            self.i = 0

        def take(self):
            r = self.refs[self.i % len(self.refs)]
            self.i += 1
            assert r.consumed(), f"PSUM bank {r.main.name} reused before consumed"
            return r

        def add(self, ref):
            self.refs.insert(self.i % len(self.refs) if self.refs else 0, ref)

    ps_w, ps_m, ps_Tf = g["ps_w"], g["ps_m"], g["ps_Tf"]
    R_w = [BankRef(ps_w[0]), BankRef(ps_w[1])]
    R_m = [BankRef(ps_m[0]), BankRef(ps_m[1]), BankRef(ps_m[2])]
    R_T = BankRef(ps_Tf, [ps_Tf, g["ps_T"]])
    R_po = [BankRef(g["ps_po"][0]), BankRef(g["ps_po"][1])]

    def mm(out, lhsT, rhs, start, stop, reads, writes, tp=None):
        if tp is None:
            S.op(P, lambda e: e.matmul(out, lhsT=lhsT, rhs=rhs, start=start, stop=stop),
                 reads=reads, writes=writes)
        else:
            S.op(P, lambda e: e.matmul(out, lhsT=lhsT, rhs=rhs, start=start, stop=stop, tile_position=tp),
                 reads=reads, writes=writes)

    xb = [S.carve(f"xb{i}", 1024, F32) for i in range(2)]
    xr = xb
    hb2 = [S.carve(f"hb{i}", 1024, BF16) for i in range(2)]
    hT = S.carve("hT", 8 * 512, BF16)
    _Up = S.carve("Up", 4 * 528, BF16)
    Up2 = [_Up, _Up]
    SA = S.carve("SA", 528, F32)
    SB = S.carve("SB", 528, F32)
    Sb = S.carve("Sb", 4 * 512, BF16)
    Uz2 = [S.carve(f"Uz{k}", 3 * 1024, BF16) for k in range(2)]
    qT2 = [S.carve(f"qT{k}", 2 * 512, BF16) for k in range(2)]
    Gt = S.carve("Gt", 9 * 512, BF16)
    PT = S.carve("PT", 4 * 512, BF16)
    bt = S.carve("bt", 2 * 768, F32)
    tmpS = S.carve("tmpS", 2 * 768, F32)
    Hs = S.carve("Hs", 2 * 12 * 65, BF16)
    Hc = S.carve("Hc", 24, F32)
    hct = S.carve("hct", 48, F32)
    Yc = S.carve("Yc", 1024, F32)
    Ycb = [Buf("Yc_lo", Yc.base), Buf("Yc_hi", Yc.base)]
    scr = [S.carve(f"scr{i}", 512, F32) for i in range(3)]
    Yg = S.carve("Yg", 3 * 512, BF16)
    t1f = S.carve("t1f", 1024, F32)
    junk = t1f
    kT = S.carve("kT", 2 * 256, BF16)
    Vv = S.carve("Vv", 2 * 256, BF16)
    stb = [S.carve(f"st{k}", 4, F32) for k in range(4)]
    scr_i = [0]

    def scratch():
        b = scr[scr_i[0] % 3]
        scr_i[0] += 1
        assert b.last_w is None or len(b.readers) > 0, f"scratch {b.name} reused before consumed"
        return b

    for Uz in Uz2:
        S.op(G_, lambda e, Uz=Uz: e.memset(Uz.v(), 0.0), writes=[Uz])
    tix = {(b_, i_): n_ for n_, (b_, i_) in enumerate((b_, i_) for b_ in range(NB) for i_ in range(NT))}

    cmm_done = set()

    def par(b, i):
        return tix[(b, i)] % 2

    def rms_stats_a(src_ap, src_bufs, k, jb=None):
        sb_ = stb[k]
        jb = junk if jb is None else jb
        S.op(A, lambda e: e.activation(out=jb.v([[1, 1024]]), in_=src_ap, func=AF.Square,
                                       accum_out=sb_.v([[1, 1]], off=0)),
             reads=src_bufs, writes=[jb, sb_])
        return sb_.v([[1, 1]], off=2)

    def rms_stats_b(k):
        sb_ = stb[k]
        S.op(G_, lambda e: e.tensor_scalar(out=sb_.v([[1, 1]], off=1), in0=sb_.v([[1, 1]], off=0),
                                           scalar1=1.0 / D, scalar2=EPS, op0=ALU.mult, op1=ALU.add),
             reads=[sb_], writes=[sb_])
        S.op(G_, lambda e: e.tensor_tensor(out=sb_.v([[1, 1]], off=2), in0=sb_.v([[1, 1]], off=1),
                                           in1=small.v([[1, 1]], off=23), op=ALU.pow),
             reads=[sb_, small], writes=[sb_])

    def nt_front(srcs):
        rs = [rms_stats_a(src.v(), [src], k) for k, src in enumerate(srcs)]
        for k, src in enumerate(srcs):
            rms_stats_b(k)
        for k, src in enumerate(srcs):
            S.op(A, lambda e, k=k, src=src: e.activation(out=hb2[k].v(), in_=src.v(), func=AF.Copy, scale=rs[k]),
                 reads=[src, stb[k]], writes=[hb2[k]])

    def nt_back(dst_offs, ncols_dst, scale_cols=None):
        for k in range(2):
            hb = hb2[k]
            for dc in range(8):
                S.op(P, lambda e, dc=dc, hb=hb: e.transpose(ps_T.v([[1, 128]], off=dc * 128),
                                                            hb.v([[1, 128]], off=dc * 128), identb.v()),
                     reads=[hb, identb], writes=[ps_T, ps_Tf])
            dst_off = dst_offs[k]
            if scale_cols is None:
                S.op(A, lambda e, dst_off=dst_off: e.activation(
                    out=hT.v([[ncols_dst, 8], [1, 128]], off=dst_off), in_=ps_T.v([[128, 8], [1, 128]]),
                    func=AF.Copy), reads=[ps_T, ps_Tf], writes=[hT])
            else:
                S.op(V, lambda e, dst_off=dst_off: e.tensor_tensor(
                    out=hT.v([[ncols_dst, 8], [1, 128]], off=dst_off), in0=ps_T.v([[128, 8], [1, 128]]),
                    in1=small.v([[1, 8], [0, 128]], off=scale_cols), op=ALU.mult),
                    reads=[ps_T, ps_Tf, small], writes=[hT])

    def norm_transpose_pair(srcs, dst_offs, ncols_dst, scale_cols=None):
        nt_front(srcs)
        nt_back(dst_offs, ncols_dst, scale_cols)

    def dbg_store(key, row0, buf, ap, nrows):
        if not dbg_d:
            return
        S.op(SY, lambda e: e.dma_start(out=dbg_d[key][row0:row0 + nrows, :], in_=ap), reads=[buf],
             dma="dbg_" + buf.name)

    import os as _os
    dbg_tile = (0, int(_os.environ.get('KDBGT', 1 if NT > 1 else 0)))

    wkv = g["wkv"]
    CHUNKS_B1P = [("pool", gi, 96 * gi, 96) for gi in range(4)]
    CHUNKS_B1S = ([("ssm", ct, 384 + 128 * ct, 128) for ct in range(3)]
                  + [("q", a, 768 + 128 * a, 128) for a in range(2)])
    CHUNKS_B1 = CHUNKS_B1P + CHUNKS_B1S
    CHUNKS_B2 = ([("gate", gi, 1024 + 96 * gi, 96) for gi in range(4)]
                 + [("gate", 4 + k, 1024 + 384 + 128 * k, 128) for k in range(5)])

    def gen_kv(b, pool):
        for mb in range(2):
            xs = xb[mb]
            S.op(SY, lambda e, xs=xs, mb=mb, b=b: e.dma_start(out=xs.v(), in_=mem_d[b, mb * 128:(mb + 1) * 128, :]),
                 writes=[xs], dma=xs.name)
        nt_front(xb)
        yield
        yield
        nt_back([0, 128], 512, scale_cols=8)
        yield
        for a in range(2):
            pr = pool.take()
            for dc in range(8):
                mm(pr.v([[1, 256]]), wkv.v([[1, 128]], off=dc * 512 + a * 128), hT.v([[1, 256]], off=dc * 512),
                   dc == 0, dc == 7, [wkv, hT], pr.bufs)
            S.op(A, lambda e, a=a, pr=pr: e.activation(out=kT.v([[1, 256]], off=a * 256), in_=pr.v([[1, 256]]),
                                                       func=AF.Copy, scale=0.125), reads=pr.bufs, writes=[kT])
            yield
        for mc in range(2):
            pr = pool.take()
            for dc in range(8):
                mm(pr.v([[1, 256]]), hT.v([[1, 128]], off=dc * 512 + mc * 128), wkv.v([[1, 256]], off=dc * 512 + 256),
                   dc == 0, dc == 7, [wkv, hT], pr.bufs)
            S.op(A, lambda e, mc=mc, pr=pr: e.activation(out=Vv.v([[1, 256]], off=mc * 256), in_=pr.v([[1, 256]]),
                                                         func=AF.Copy), reads=pr.bufs, writes=[Vv])
            yield

    def st_kv(b):
        for _ in gen_kv(b, Pool_([R_m[2], R_w[1]])):
            pass

    def gen_boundary(b, i, nxt):
        yield from gen_E(b, i, [R_m[2], R_w[1]])
        yield from gen_kv(nxt[0], Pool_([R_m[2], R_w[1]]))
        yield from gen_A(*nxt)
        st_reset_halo(nxt[0])
        yield from gen_B(*nxt, CHUNKS_B1S + CHUNKS_B1P, Pool_([R_w[0]]))

    def st_reset_state():
        S.op(G_, lambda e: e.memset(Hc.v(), 0.0), writes=[Hc])

    def st_reset_halo(b):
        Up = Up2[par(b, 0)]
        S.op(G_, lambda e: e.memset(Up.v([[528, 4], [1, 16]], np_=96), 0.0), writes=[Up])

    def a_front(b, i, pr):
        t0 = i * T
        for k in range(2):
            xs = xb[k]
            blk = pr * 2 + k
            S.op(SY, lambda e, xs=xs, b=b, r0=t0 + blk * 128: e.dma_start(out=xs.v(), in_=x_d[b, r0:r0 + 128, :]),
                 writes=[xs], dma=xs.name)
        nt_front(xb)

    def gen_A(b, i, skip_front0=False):
        if not skip_front0:
            a_front(b, i, 0)
            yield
        nt_back([0, 128], 512)
        a_front(b, i, 1)
        yield
        if skip_front0:
            yield
            yield
        nt_back([256, 384], 512)
        yield

    def st_A(b, i):
        for _ in gen_A(b, i):
            pass

    def gen_AB(b, i, chunks, pool, skip_front0=False, stride=1):
        yield from gen_A(b, i, skip_front0)
        yield from gen_B(b, i, chunks, pool, stride)

    def gen_B(b, i, chunks, pool, stride=1):
        dbg_on = bool(dbg_d) and (b, i) == dbg_tile
        Up, Uz, qT = Up2[par(b, i)], Uz2[par(b, i)], qT2[par(b, i)]
        for kind, idx, c0, M in chunks:
            pr = pool.take()
            pb = pr
            for dc in range(8):
                mm(pb.v([[1, 512]], np_=M), wi.v([[1, M]], off=dc * 2048 + c0), hT.v([[1, 512]], off=dc * 512),
                   dc == 0, dc == 7, [wi, hT], pr.bufs)
            if dbg_on:
                sd = scratch()
                S.op(V, lambda e, pb=pb, sd=sd, M=M: e.tensor_copy(out=sd.v(np_=M), in_=pb.v(np_=M)),
                     reads=pr.bufs, writes=[sd])
                dbg_store("proj", c0, sd, sd.v(np_=M), M)
            if kind == "pool":
                assert tix[(b, i)] == 0 or (tix[(b, i)] - 1) in cmm_done, "pool chunk before previous Cmm"
                S.op(A, lambda e, pb=pb, idx=idx: e.activation(out=Up.v([[1, 512]], off=idx * 528 + 16, np_=96),
                                                               in_=pb.v(np_=96), func=AF.Copy),
                     reads=pr.bufs, writes=[Up])
            elif kind == "ssm":
                S.op(A, lambda e, pb=pb, idx=idx: e.activation(
                    out=Uz.v([[16, 64], [1, 8]], off=idx * 1024 + 8), in_=pb.v([[8, 64], [1, 8]]), func=AF.Copy),
                    reads=pr.bufs, writes=[Uz])
            elif kind == "q":
                S.op(A, lambda e, pb=pb, idx=idx: e.activation(out=qT.v([[1, 512]], off=idx * 512), in_=pb.v(),
                                                               func=AF.Copy), reads=pr.bufs, writes=[qT])
            else:
                S.op(A, lambda e, pb=pb, idx=idx, M=M: e.activation(
                    out=Gt.v([[1, 512]], off=idx * 512, np_=M), in_=pb.v(np_=M), func=AF.Silu),
                    reads=pr.bufs, writes=[Gt])
            for _ in range(stride):
                yield

    def st_B(b, i, chunks, pool=None):
        for _ in gen_B(b, i, chunks, pool if pool is not None else Pool_(R_w)):
            pass

    def st_Csum(b, i):
        Up = Up2[par(b, i)]
        for gi, w in enumerate(WINDOWS):
            u0 = gi * 528

            def U(c_lo, n, u0=u0):
                return Up.v([[1, n]], off=u0 + c_lo, np_=96)

            def sa(buf, c_lo, n):
                return buf.v([[1, n]], off=c_lo, np_=96)

            outS = Sb.v([[1, 512]], off=gi * 512, np_=96)
            steps = int(math.log2(w))
            prev_buf = None
            for s_i in range(steps):
                sh = 1 << s_i
                last = s_i == steps - 1
                lo = 16 if last else (2 * sh - 1)
                n = 528 - lo
                if s_i == 0:
                    in0, in1, rd = U(lo, n), U(lo - sh, n), [Up]
                else:
                    in0, in1, rd = sa(prev_buf, lo, n), sa(prev_buf, lo - sh, n), [prev_buf]
                if last:
                    tt(outS, in0, in1, ALU.add, rd, [Sb], eng=G_)
                    if i == 0 and w > 1:
                        dst16 = SB if prev_buf is SA else SA
                        if s_i == 0:
                            a0, a1 = U(16, 16), U(16 - sh, 16)
                        else:
                            a0, a1 = sa(prev_buf, 16, 16), sa(prev_buf, 16 - sh, 16)
                        tt(sa(dst16, 0, 16), a0, a1, ALU.add, rd, [dst16], eng=G_)
                        tt(Sb.v([[1, 16]], off=gi * 512, np_=96), sa(dst16, 0, 16),
                           ctab.v([[1, 16]], off=gi * 16, np_=96), ALU.mult, [dst16, ctab], [Sb], eng=G_)
                else:
                    dst = SA if prev_buf is not SA else SB
                    tt(sa(dst, lo, n), in0, in1, ALU.add, rd, [dst], eng=G_)
                    prev_buf = dst

    def gen_Cmm(b, i, cmm_pool):
        dbg_on = bool(dbg_d) and (b, i) == dbg_tile
        Up = Up2[par(b, i)]
        UpN = Up2[1 - par(b, i)]
        for gi, w in enumerate(WINDOWS):
            outS = Sb.v([[1, 512]], off=gi * 512, np_=96)
            pr = cmm_pool.take()
            pb = pr
            mm(pb.v([[1, 512]], np_=96), wp1.v([[1, 96]], off=gi * 96, np_=96), outS, True, False, [wp1, Sb], pr.bufs)
            mm(pb.v([[1, 512]], np_=96), wp2.v([[1, 96]], off=gi * 96, np_=96),
               Up.v([[1, 512]], off=gi * 528 + 16, np_=96), False, True, [wp2, Up], pr.bufs)
            if dbg_on:
                sd = scratch()
                S.op(V, lambda e, pb=pb, sd=sd, gi=gi: e.tensor_scalar(
                    out=sd.v(np_=96), in0=pb.v(np_=96), scalar1=small.v([[1, 1]], off=16 + gi, np_=96),
                    scalar2=None, op0=ALU.mult), reads=pr.bufs + [small], writes=[sd])
                dbg_store("ypool", 96 * gi, sd, sd.v(np_=96), 96)
            gv = Gt.v([[1, 512]], off=gi * 512, np_=96)
            S.op(V, lambda e, pb=pb, gi=gi, gv=gv: e.scalar_tensor_tensor(
                out=gv, in0=pb.v(np_=96), scalar=small.v([[1, 1]], off=16 + gi, np_=96), in1=gv,
                op0=ALU.mult, op1=ALU.mult), reads=pr.bufs + [small, Gt], writes=[Gt])
            if gi == 3:
                if i + 1 < NT:
                    S.op(G_, lambda e: e.tensor_copy(out=UpN.v([[528, 4], [1, 15]], off=1, np_=96),
                                                     in_=Up.v([[528, 4], [1, 15]], off=513, np_=96)),
                         reads=[Up], writes=[UpN] if UpN is not Up else [Up])
                cmm_done.add(tix[(b, i)])
            yield

    xb4 = []

    def st_D1a(b, i):
        Uz = Uz2[par(b, i)]
        S.op(V, lambda e: e.tensor_copy(out=Hs.v([[65, 24], [1, 1]]), in_=Hc.v([[1, 24], [1, 1]])),
             reads=[Hc], writes=[Hs])
        xbanks = [bank("w", force=0), bank("w"), bank("m", force=0), bank("m")]
        xb4[:] = xbanks
        for ct in range(3):
            for part in range(2):
                for j in range(8):
                    for pp in range(4):
                        mm(xbanks[pp].v([[1, 64]], off=(ct * 2 + part) * 64),
                           W1.v([[1, 128]], off=ct * 2048 + (j * 2 + part) * 128, p0=32 * pp, np_=32),
                           Uz.v([[16, 64]], off=ct * 1024 + 8 + j, p0=32 * pp, np_=32),
                           j == 0, j == 7, [W1, Uz], [xbanks[pp]], tp=((96, 0) if pp == 3 else None))

    def st_D1b(b, i):
        xbanks = list(xb4)
        w0 = xbanks[0]
        xre = w0.v([[512, 4], [128, 3], [1, 64]], off=0)
        xim = w0.v([[512, 4], [128, 3], [1, 64]], off=64)
        tabv = lambda buf: buf.v([[64, 4], [256, 3], [1, 64]])
        t3 = lambda buf: buf.v([[192, 4], [64, 3], [1, 64]])
        btv = lambda off: bt.v([[64, 4], [256, 3], [1, 64]], off=off)
        T0 = lambda dims=None: tmpS.v(dims if dims is not None else [[1, 768]], off=0)
        T1 = lambda dims=None: tmpS.v(dims if dims is not None else [[1, 768]], off=768)
        d3 = [[192, 4], [64, 3], [1, 64]]
        tt(T0(d3), xre, tabv(cosT), ALU.mult, xbanks + [cosT], [tmpS])
        tt(T1(d3), xim, tabv(sinT), ALU.mult, xbanks + [sinT], [tmpS])
        tt(btv(0), T0(d3), T1(d3), ALU.add, [tmpS], [bt])
        tt(T0(d3), xim, tabv(cosT), ALU.mult, xbanks + [cosT], [tmpS])
        tt(T1(d3), xre, tabv(sinT), ALU.mult, xbanks + [sinT], [tmpS])
        tt(btv(768), T0(d3), T1(d3), ALU.subtract, [tmpS], [bt])
        tt(hct.v([[12, 2], [1, 12]]), Hc.v([[12, 2], [1, 12]]), rQ.v([[0, 2], [1, 12]]), ALU.mult, [Hc, rQ], [hct])
        tt(bt.v([[64, 24], [1, 1]]), bt.v([[64, 24], [1, 1]]), hct.v([[1, 24], [1, 1]]), ALU.add, [bt, hct], [bt])
        for part in range(2):
            S.op(V, lambda e, part=part: e.tensor_tensor_scan(
                out=bt.v([[1, 768]], off=part * 768), data0=decT.v(), data1=bt.v([[1, 768]], off=part * 768),
                initial=0.0, op0=ALU.mult, op1=ALU.add), reads=[decT, bt], writes=[bt])
        Gs = bt
        gre = Gs.v([[1, 768]], off=0)
        gim = Gs.v([[1, 768]], off=768)
        hv = lambda off: Hs.v([[65, 12], [1, 64]], off=off + 1)
        d12 = [[64, 12], [1, 64]]
        tt(T0(), gre, cosT.v(), ALU.mult, [Gs, cosT], [tmpS])
        tt(T1(), gim, sinT.v(), ALU.mult, [Gs, sinT], [tmpS])
        tt(hv(0), T0(d12), T1(d12), ALU.subtract, [tmpS], [Hs])
        tt(T0(), gre, sinT.v(), ALU.mult, [Gs, sinT], [tmpS])
        tt(T1(), gim, cosT.v(), ALU.mult, [Gs, cosT], [tmpS])
        tt(hv(780), T0(d12), T1(d12), ALU.add, [tmpS], [Hs])

    def st_D1c(b, i):
        Gs = bt
        lastv = lambda buf, off: buf.v([[64, 12], [1, 1]], off=off + 63)
        h4 = [hct.v([[1, 12], [1, 1]], off=12 * k) for k in range(4)]
        tt(h4[0], lastv(Gs, 0), lastv(cosT, 0), ALU.mult, [Gs, cosT], [hct], eng=G_)
        tt(h4[1], lastv(Gs, 768), lastv(sinT, 0), ALU.mult, [Gs, sinT], [hct], eng=G_)
        tt(h4[2], lastv(Gs, 0), lastv(sinT, 0), ALU.mult, [Gs, sinT], [hct], eng=G_)
        tt(h4[3], lastv(Gs, 768), lastv(cosT, 0), ALU.mult, [Gs, cosT], [hct], eng=G_)
        tt(Hc.v([[1, 12], [1, 1]]), h4[0], h4[1], ALU.subtract, [hct], [Hc], eng=G_)
        tt(Hc.v([[1, 12], [1, 1]], off=12), h4[2], h4[3], ALU.add, [hct], [Hc], eng=G_)

    def gen_D2(b, i, d2_idle=2):
        dbg_on = bool(dbg_d) and (b, i) == dbg_tile
        Uz = Uz2[par(b, i)]
        pool = Pool_([R_m[0], R_m[1]])

        def conv(ct, pr):
            for m in range(8):
                mm(pr.v([[8, 64], [1, 8]]), Kc.v([[1, 128]], off=ct * 1024 + m * 128),
                   Uz.v([[16, 64], [1, 8]], off=ct * 1024 + 8 - m), m == 0, m == 7, [Kc, Uz], pr.bufs)

        assert R_po[0].consumed() and R_po[1].consumed()
        conv(0, R_po[0])
        yield
        conv(1, R_po[1])
        yield
        for _ in range(d2_idle):
            yield
        yield from gen_Cmm(b, i, pool)
        for ct in range(3):
            hp_ = 64 * (ct % 2)
            for half in range(2):
                pr = pool.take()
                for q2 in range(2):
                    pp = half * 2 + q2
                    p = ct * 4 + pp
                    mm(pr.v([[1, 256]], off=q2 * 256, p0=hp_, np_=64), Hs.v([[1, 64]], off=p * 65),
                       W3r.v([[1, 256]], off=p * 288 + 32), True, False, [Hs, W3r], pr.bufs)
                    mm(pr.v([[1, 256]], off=q2 * 256, p0=hp_, np_=64), Hs.v([[1, 64]], off=780 + p * 65),
                       W3i.v([[1, 256]], off=p * 288 + 32), False, True, [Hs, W3i], pr.bufs)
                S.op(V, lambda e, pr=pr, half=half, hp_=hp_: e.tensor_copy(
                    out=Yc.v([[32, 2], [128, 8], [1, 32]], off=half * 64, p0=hp_, np_=64),
                    in_=pr.v([[256, 2], [32, 8], [1, 32]], p0=hp_, np_=64)),
                    reads=pr.bufs, writes=[Ycb[ct % 2]])
            yield
            ptr = pool.take()
            for j in range(8):
                S.op(P, lambda e, j=j, ptr=ptr, hp_=hp_: e.transpose(
                    ptr.v([[1, 64]], off=j * 64), Yc.v([[1, 128]], off=j * 128, p0=hp_, np_=64),
                    identf.v([[1, 64]], off=hp_, p0=hp_, np_=64)), reads=[Ycb[ct % 2], identf], writes=ptr.bufs)
            isb = tmpS
            S.op(V, lambda e, ptr=ptr, isb=isb: e.tensor_copy(
                out=isb.v([[1, 8], [8, 64]]), in_=ptr.v([[64, 8], [1, 64]])),
                reads=ptr.bufs, writes=[isb])
            yield
            if ct < 2:
                pcv = R_po[ct]
            else:
                pcv = pool.take()
                conv(2, pcv)
            yl = scratch()
            tt(yl.v(), pcv.v(), isb.v([[1, 512]]), ALU.add, pcv.bufs + [isb], [yl])
            if ct < 2:
                pool.add(R_po[ct])
            if dbg_on:
                dbg_store("ylin", 128 * ct, yl, yl.v(), 128)
            S.op(A, lambda e, ct=ct, yl=yl: e.activation(out=Yg.v([[1, 512]], off=ct * 512), in_=yl.v(),
                                                         func=AF.Gelu_apprx_tanh), reads=[yl], writes=[Yg])
            yield
        for c3 in range(3):
            pz = []
            for oc in (c3, 3 + c3):
                pr = pool.take()
                for ct in range(3):
                    mm(pr.v(), wg.v([[1, 128]], off=ct * 768 + oc * 128), Yg.v([[1, 512]], off=ct * 512),
                       ct == 0, ct == 2, [wg, Yg], pr.bufs)
                pz.append(pr)
            sg = scratch()
            S.op(A, lambda e, sg=sg, pr=pz[1]: e.activation(out=sg.v(), in_=pr.v(), func=AF.Sigmoid),
                 reads=pz[1].bufs, writes=[sg])
            tmp = scratch()
            tt(tmp.v(), pz[0].v(), sg.v(), ALU.mult, pz[0].bufs + [sg], [tmp])
            if dbg_on:
                dbg_store("yssm", 128 * c3, tmp, tmp.v(), 128)
            gv = Gt.v([[1, 512]], off=(4 + c3) * 512)
            tt(gv, tmp.v(), gv, ALU.mult, [tmp, Gt], [Gt], eng=G_)
            yield

    def gen_E(b, i, banks):
        dbg_on = bool(dbg_d) and (b, i) == dbg_tile
        qT = qT2[par(b, i)]
        pool = Pool_(banks)
        for a in range(2):
            for hh in range(2):
                for mc in range(2):
                    pr = pool.take()
                    mm(pr.v(), kT.v([[1, 128]], off=a * 256 + mc * 128, p0=64 * hh, np_=64),
                       qT.v([[1, 512]], off=a * 512, p0=64 * hh, np_=64), True, True, [kT, qT], pr.bufs)
                    S.op(A, lambda e, pr=pr, hh=hh, mc=mc: e.activation(
                        out=PT.v([[1, 512]], off=(hh * 2 + mc) * 512), in_=pr.v(), func=AF.Exp),
                        reads=pr.bufs, writes=[PT])
                    yield
            po_ = pool.take()
            pd_ = pool.take()
            for hh in range(2):
                for mc in range(2):
                    mm(po_.v([[1, 512]], p0=64 * hh, np_=64),
                       Vv.v([[1, 64]], off=mc * 256 + (2 * a + hh) * 64),
                       PT.v([[1, 512]], off=(hh * 2 + mc) * 512), mc == 0, mc == 1, [Vv, PT], po_.bufs)
            for hh in range(2):
                for mc in range(2):
                    mm(pd_.v([[1, 512]], p0=64 * hh, np_=64), ones.v([[1, 64]]),
                       PT.v([[1, 512]], off=(hh * 2 + mc) * 512), mc == 0, mc == 1, [ones, PT], pd_.bufs)
            ld_ = scratch()
            S.op(A, lambda e, ld_=ld_, pd_=pd_: e.activation(out=ld_.v(), in_=pd_.v(), func=AF.Ln),
                 reads=pd_.bufs, writes=[ld_])
            rd_ = ld_
            S.op(A, lambda e, rd_=rd_, ld_=ld_: e.activation(out=rd_.v(), in_=ld_.v(), func=AF.Exp, scale=-1.0),
                 reads=[ld_], writes=[rd_])
            ya = scratch()
            tt(ya.v(), po_.v(), rd_.v(), ALU.mult, po_.bufs + [rd_], [ya])
            if dbg_on:
                dbg_store("yatt", 128 * a, ya, ya.v(), 128)
            gv = Gt.v([[1, 512]], off=(7 + a) * 512)
            tt(gv, ya.v(), gv, ALU.mult, [ya, Gt], [Gt], eng=G_)
            yield

    def merge(*gens):
        gens = list(gens)
        while gens:
            for g_ in list(gens):
                try:
                    next(g_)
                except StopIteration:
                    gens.remove(g_)

    def st_F(b, i):
        t0 = i * T
        rows = [96] * 4 + [128] * 5
        ps_w = g["ps_w"]
        def reload(blk_):
            xs_ = xr[blk_ % 2]
            r0_ = t0 + blk_ * 128
            S.op(SY, lambda e, xs_=xs_, b=b, r0_=r0_: e.dma_start(out=xs_.v(), in_=x_d[b, r0_:r0_ + 128, :]),
                 writes=[xs_], dma=xs_.name)

        reload(0)
        for blk in range(4):
            xs = xr[blk % 2]
            r0 = t0 + blk * 128
            if blk + 1 < 4:
                reload(blk + 1)
            pbufs = ps_po if blk % 2 == 1 else [ps_w[0], ps_w[1]]
            po_ap = pbufs[0].v([[1, 1024]])
            halves = [pbufs[0].v(), pbufs[1].v()]
            hb_ = pbufs
            for hf in range(2):
                for ci, nr in enumerate(rows):
                    mm(halves[hf], Gt.v([[1, 128]], off=ci * 512 + blk * 128, np_=nr),
                       wo.v([[1, 512]], off=ci * 1024 + hf * 512, np_=nr), ci == 0, ci == 8, [Gt, wo], [hb_[hf]])
            k = 2 + (blk % 2)
            r = rms_stats_a(po_ap, pbufs, k, jb=PT)
            rms_stats_b(k)
            S.op(V, lambda e, r=r, po_ap=po_ap: e.scalar_tensor_tensor(
                out=t1f.v(), in0=po_ap, scalar=r, in1=gpost.v(), op0=ALU.mult, op1=ALU.mult),
                reads=list(pbufs) + [stb[k], gpost], writes=[t1f])
            tt(xs.v(), xs.v(), t1f.v(), ALU.add, [xs, t1f], [xs])
            S.op(SY, lambda e, xs=xs, b=b, r0=r0: e.dma_start(out=out_d[b, r0:r0 + 128, :], in_=xs.v()),
                 reads=[xs], dma=xs.name)

    tiles = [(b, i) for b in range(NB) for i in range(NT)]
    st_kv(0)
    st_reset_state()
    st_reset_halo(0)
    st_A(*tiles[0])
    st_B(*tiles[0], CHUNKS_B1)
    for n_, (b, i) in enumerate(tiles):
        nxt = tiles[n_ + 1] if n_ + 1 < len(tiles) else None
        same = nxt is not None and nxt[0] == b
        st_Csum(b, i)
        st_B(b, i, CHUNKS_B2)
        if same:
            a_front(*nxt, 0)
        st_D1a(b, i)
        st_D1b(b, i)
        if same:
            merge(gen_D2(b, i), gen_E(b, i, [R_m[2], R_w[1]]),
                  gen_AB(*nxt, CHUNKS_B1S + CHUNKS_B1P, Pool_([R_w[0]]), skip_front0=True, stride=2))
            st_D1c(b, i)
        elif nxt is not None:
            merge(gen_D2(b, i), gen_boundary(b, i, nxt))
            st_D1c(b, i)
            st_reset_state()
        else:
            merge(gen_D2(b, i), gen_E(b, i, [R_m[2], R_w[1]]))
            st_D1c(b, i)
        st_F(b, i)


def host_layout(inp):
    f = lambda a: np.ascontiguousarray(np.asarray(a, dtype=np.float32))
    m = {}
    m["w_in"] = f(inp["w_in"][0])
    m["w_out"] = f(inp["w_out"][0])
    m["w_glu"] = f(inp["w_glu"][0])
    m["w_kv"] = f(inp["w_kv"][0])
    m["w_pool_t"] = f(np.transpose(inp["w_pool"][0], (1, 0, 2)).reshape(96, 384))
    m["gpre_t"] = f(inp["g_pre"][0].reshape(8, 128).T)
    m["gmem_t"] = f(inp["g_mem"][0].reshape(8, 128).T)
    m["g_post"] = f(inp["g_post"][0].reshape(1, D))
    m["pscale_t"] = f(inp["pool_scale"][0].reshape(4, 96).T)
    m["dskip_t"] = f(inp["d_skip"][0].reshape(3, 128).T)
    ct = np.ones((4, 16), np.float32)
    for gi, w in enumerate(WINDOWS):
        for t in range(16):
            ct[gi, t] = float(w) / float(min(t + 1, w))
    m["pool_ctab"] = f(np.broadcast_to(ct.reshape(1, 64), (96, 64)))
    m["ident"] = f(np.eye(128, dtype=np.float32))
    a_re, a_im, ldt = inp["a_re"][0], inp["a_im"][0], inp["log_dt"][0]
    b_re, b_im, c_re, c_im = inp["b_re"][0], inp["b_im"][0], inp["c_re"][0], inp["c_im"][0]
    sm = lambda a: f(a.reshape(12, 128).T)
    m["sm_are"], m["sm_aim"] = sm(a_re), sm(a_im)
    m["sm_ldt"] = f(np.repeat(ldt.reshape(12, 2), 64, axis=1).T)
    smb = lambda a: f(a.reshape(12, 2, 64, 16).transpose(1, 2, 0, 3).reshape(128, 192))
    m["sm_bre"], m["sm_bim"] = smb(b_re), smb(b_im)
    smc = lambda a: f(a.reshape(12, 2, 16, 64).transpose(1, 3, 0, 2).reshape(128, 192))
    m["sm_cre"], m["sm_cim"] = smc(c_re), smc(c_im)
    def cm_rep(a):
        v = a.reshape(3, 4, 2, 64)
        v = v.transpose(1, 0, 2, 3).reshape(4, 1, 3 * 128)
        return f(np.broadcast_to(v, (4, 32, 384)).reshape(128, 384))
    m["cm_are"], m["cm_aim"] = cm_rep(a_re), cm_rep(a_im)
    m["cm_ldt"] = cm_rep(np.repeat(ldt.reshape(24, 1), 64, axis=1))
    def cm_b(a):
        v = a.reshape(3, 4, 2, 64, 16)
        o = np.zeros((4, 2, 16, 3, 2, 64), np.float32)
        for g2 in range(2):
            o[:, g2, :, :, g2, :] = v[:, :, g2].transpose(1, 3, 0, 2)
        return f(o.reshape(128, 384))
    m["cm_bre"], m["cm_bim"] = cm_b(b_re), cm_b(b_im)
    return m


_NC_CACHE = {}


def kernel(**inputs):
    x = np.asarray(inputs["x"], dtype=np.float32)
    mem = np.asarray(inputs["mem"], dtype=np.float32)
    B, L, _ = x.shape
    NB = B // N_CORES
    key = (NB, L)
    if key not in _NC_CACHE:
        _NC_CACHE[key] = build_nc(NB, L)
    nc = _NC_CACHE[key]
    shared = host_layout(inputs)
    in_maps = []
    for c in range(N_CORES):
        mp = dict(shared)
        mp["x"] = np.ascontiguousarray(x[c * NB:(c + 1) * NB])
        mp["mem"] = np.ascontiguousarray(mem[c * NB:(c + 1) * NB])
        in_maps.append(mp)
    res = run_bass_kernel_spmd(nc, in_maps, core_ids=list(range(N_CORES)))
    return np.concatenate([np.asarray(r["out"], dtype=np.float32) for r in res.results], axis=0)
```
